# Optimizing a Trainium2 kernel written in Bass

```python
import jax, jax.numpy as jnp
from jax import lax
import numpy as np

D_MODEL = 1024
BATCH = 8
SEQ = 4096
DEPTH = 1

PLE_DIM = 256
ATTN_HEADS = 8
ATTN_KV_HEADS = 2
ATTN_HEAD_DIM = 64
ATTN_GROUPS = ATTN_HEADS // ATTN_KV_HEADS
WINDOW = 128
ATTN_BLOCK = 128
ROPE_THETA = 10000.0
DN_HEADS = 4
DN_HEAD_DIM = 128
DN_CONV = 4
DN_CHUNK = 64
D_FF = 4 * D_MODEL
EPS = 1e-6

ATTN_Q = ATTN_HEADS * ATTN_HEAD_DIM
ATTN_KV = ATTN_KV_HEADS * ATTN_HEAD_DIM
DN_W = DN_HEADS * DN_HEAD_DIM
MIX_WIDTH = ATTN_Q + DN_W
SPLIT_SIZES = (ATTN_Q, ATTN_KV, ATTN_KV, DN_W, DN_W, DN_W, DN_W, DN_HEADS, DN_HEADS)
D_IN = sum(SPLIT_SIZES)
CONV_CH = 3 * DN_W

kernel_name = "hybrid_swa_sink_gated_deltanet_block"


def rmsnorm(x, g):
    xf = x.astype(jnp.float32)
    y = xf * lax.rsqrt(jnp.mean(xf * xf, axis=-1, keepdims=True) + EPS) * g.astype(jnp.float32)
    return y.astype(x.dtype)


def l2norm(x):
    return x * lax.rsqrt(jnp.sum(x * x, axis=-1, keepdims=True) + EPS)


def rope(t, positions):
    dh = t.shape[-1]
    half = dh // 2
    inv = 1.0 / (ROPE_THETA ** (jnp.arange(half, dtype=jnp.float32) * (2.0 / dh)))
    ang = positions.astype(jnp.float32)[:, None] * inv[None, :]
    cos = jnp.cos(ang)[None, :, None, :]
    sin = jnp.sin(ang)[None, :, None, :]
    tf = t.astype(jnp.float32)
    t1, t2 = tf[..., :half], tf[..., half:]
    out = jnp.concatenate([t1 * cos - t2 * sin, t2 * cos + t1 * sin], axis=-1)
    return out.astype(t.dtype)


def sliding_window_sink_attention(q, k, v, sinks):
    B, S = q.shape[0], q.shape[1]
    nb = S // ATTN_BLOCK
    qb = q.reshape(B, nb, ATTN_BLOCK, ATTN_KV_HEADS, ATTN_GROUPS, ATTN_HEAD_DIM)
    kb = k.reshape(B, nb, ATTN_BLOCK, ATTN_KV_HEADS, ATTN_HEAD_DIM)
    vb = v.reshape(B, nb, ATTN_BLOCK, ATTN_KV_HEADS, ATTN_HEAD_DIM)

    def with_prev(t):
        prev = jnp.concatenate([jnp.zeros_like(t[:, :1]), t[:, :-1]], axis=1)
        return jnp.concatenate([prev, t], axis=2)

    kw, vw = with_prev(kb), with_prev(vb)
    scale = ATTN_HEAD_DIM ** -0.5
    s = jnp.einsum('bnqhgd,bnkhd->bnhgqk', qb, kw).astype(jnp.float32) * scale
    blk = jnp.arange(nb)[:, None, None]
    qpos = blk * ATTN_BLOCK + jnp.arange(ATTN_BLOCK)[None, :, None]
    kpos = blk * ATTN_BLOCK - ATTN_BLOCK + jnp.arange(2 * ATTN_BLOCK)[None, None, :]
    valid = (kpos <= qpos) & (qpos - kpos < WINDOW) & (kpos >= 0)
    s = jnp.where(valid[:, None, None, :, :], s, -jnp.inf)
    sink = jnp.broadcast_to(
        sinks.astype(jnp.float32).reshape(1, 1, ATTN_KV_HEADS, ATTN_GROUPS, 1, 1),
        s.shape[:-1] + (1,))
    probs = jax.nn.softmax(jnp.concatenate([s, sink], axis=-1), axis=-1)[..., :-1]
    o = jnp.einsum('bnhgqk,bnkhd->bnqhgd', probs.astype(v.dtype), vw)
    return o.reshape(B, S, ATTN_Q)


def causal_conv(x, w):
    c = x.shape[-1]
    return lax.conv_general_dilated(
        x, w[:, None, :].astype(x.dtype), window_strides=(1,),
        padding=((DN_CONV - 1, 0),), dimension_numbers=('NWC', 'WIO', 'NWC'),
        feature_group_count=c)


def chunk_gated_delta_rule(q, k, v, g, beta):
    B, S, H, DK = q.shape
    DV = v.shape[-1]
    C = DN_CHUNK
    nc = S // C

    def to_chunks(t):
        t = jnp.moveaxis(t, 2, 1)
        return t.reshape(t.shape[:2] + (nc, C) + t.shape[3:])

    q, k, v, g, beta = (to_chunks(t) for t in (q, k, v, g, beta))
    g = jnp.cumsum(g, axis=-1)
    tril = jnp.tril(jnp.ones((C, C), dtype=bool))
    strict = jnp.tril(jnp.ones((C, C), dtype=bool), -1)
    decay = jnp.exp(jnp.where(tril, g[..., :, None] - g[..., None, :], -jnp.inf))
    k_beta = k * beta[..., None]
    v_beta = v * beta[..., None]
    L = jnp.where(strict, jnp.einsum('bhncd,bhnsd->bhncs', k_beta, k) * decay, 0.0)
    eye = jnp.eye(C, dtype=q.dtype)
    T = lax.linalg.triangular_solve(eye + L, jnp.broadcast_to(eye, L.shape),
                                    left_side=True, lower=True)
    u = jnp.einsum('bhncs,bhnsd->bhncd', T, v_beta)
    w = jnp.einsum('bhncs,bhnsd->bhncd', T, k_beta * jnp.exp(g)[..., None])
    a_qk = jnp.einsum('bhncd,bhnsd->bhncs', q, k) * decay
    q_g = q * jnp.exp(g)[..., None]
    g_last = g[..., -1]
    k_d = k * jnp.exp(g_last[..., None] - g)[..., None]
    d_last = jnp.exp(g_last)

    xs = tuple(jnp.moveaxis(t, 2, 0) for t in (q_g, k_d, u, w, a_qk, d_last))

    def step(state, inp):
        q_i, k_i, u_i, w_i, a_i, d_i = inp
        v_new = u_i - jnp.einsum('bhck,bhkv->bhcv', w_i, state)
        o = jnp.einsum('bhck,bhkv->bhcv', q_i, state) + jnp.einsum('bhcs,bhsv->bhcv', a_i, v_new)
        state = state * d_i[..., None, None] + jnp.einsum('bhck,bhcv->bhkv', k_i, v_new)
        return state, o

    s0 = jnp.zeros((B, H, DK, DV), dtype=q.dtype)
    _, o = lax.scan(step, s0, xs)
    o = jnp.moveaxis(o, 0, 2).reshape(B, H, S, DV)
    return jnp.moveaxis(o, 1, 2)


def gated_deltanet(q, k, v, z, b, a, conv_w, a_log, dt_bias, norm_w):
    B, S = q.shape[0], q.shape[1]
    qkv = jax.nn.silu(causal_conv(jnp.concatenate([q, k, v], axis=-1), conv_w))
    qc, kc, vc = jnp.split(qkv.astype(jnp.float32), 3, axis=-1)
    shp = (B, S, DN_HEADS, DN_HEAD_DIM)
    qc = l2norm(qc.reshape(shp)) * (DN_HEAD_DIM ** -0.5)
    kc = l2norm(kc.reshape(shp))
    vc = vc.reshape(shp)
    beta = jax.nn.sigmoid(b.astype(jnp.float32))
    g = -jnp.exp(a_log.astype(jnp.float32)) * jax.nn.softplus(
        a.astype(jnp.float32) + dt_bias.astype(jnp.float32))
    o = chunk_gated_delta_rule(qc, kc, vc, g, beta)
    o = o * lax.rsqrt(jnp.mean(o * o, axis=-1, keepdims=True) + EPS) * norm_w.astype(jnp.float32)
    o = o * jax.nn.silu(z.astype(jnp.float32).reshape(shp))
    return o.reshape(B, S, DN_W).astype(q.dtype)


def setup_inputs(seed: int = 0) -> dict:
    key = jax.random.key(seed)
    ks = jax.random.split(key, 20)
    f32 = jnp.float32
    nrm = lambda k, shape, s: jax.random.normal(k, shape, f32) * s
    gain = lambda k, shape: 1.0 + 0.02 * jax.random.normal(k, shape, f32)
    dt = jnp.exp(jax.random.uniform(ks[6], (DEPTH, DN_HEADS), f32, np.log(1e-3), np.log(1e-1)))
    return {
        "x": nrm(ks[0], (BATCH, SEQ, D_MODEL), 1.0),
        "p": nrm(ks[1], (DEPTH, BATCH, SEQ, PLE_DIM), 1.0),
        "norm_mix": gain(ks[2], (DEPTH, D_MODEL)),
        "w_in": nrm(ks[3], (DEPTH, D_MODEL, D_IN), D_MODEL ** -0.5),
        "conv_w": nrm(ks[4], (DEPTH, DN_CONV, CONV_CH), DN_CONV ** -0.5),
        "a_log": jnp.log(jax.random.uniform(ks[5], (DEPTH, DN_HEADS), f32, 1.0, 16.0)),
        "dt_bias": dt + jnp.log(-jnp.expm1(-dt)),
        "dn_norm": gain(ks[7], (DEPTH, DN_HEAD_DIM)),
        "sinks": nrm(ks[8], (DEPTH, ATTN_HEADS), 0.5),
        "w_o": nrm(ks[9], (DEPTH, MIX_WIDTH, D_MODEL), MIX_WIDTH ** -0.5),
        "norm_mlp": gain(ks[10], (DEPTH, D_MODEL)),
        "w_up": nrm(ks[11], (DEPTH, D_MODEL, D_FF), D_MODEL ** -0.5),
        "w_down": nrm(ks[12], (DEPTH, D_FF, D_MODEL), D_FF ** -0.5),
        "norm_ple": gain(ks[13], (DEPTH, D_MODEL)),
        "w_ple_gate": nrm(ks[14], (DEPTH, D_MODEL, D_MODEL), D_MODEL ** -0.5),
        "w_ple_proj": nrm(ks[15], (DEPTH, PLE_DIM, D_MODEL), PLE_DIM ** -0.5),
        "norm_final": gain(ks[16], (D_MODEL,)),
    }


def reference(x, p, norm_mix, w_in, conv_w, a_log, dt_bias, dn_norm, sinks, w_o,
              norm_mlp, w_up, w_down, norm_ple, w_ple_gate, w_ple_proj, norm_final):
    B, S, _ = x.shape
    positions = jnp.arange(S)
    split_idx = np.cumsum(SPLIT_SIZES)[:-1].tolist()
    h = x
    for i in range(DEPTH):
        u = rmsnorm(h, norm_mix[i])
        proj = u @ w_in[i]
        aq, ak, av, dq, dk, dv, dz, db, da = jnp.split(proj, split_idx, axis=-1)
        aq = rope(aq.reshape(B, S, ATTN_HEADS, ATTN_HEAD_DIM), positions)
        ak = rope(ak.reshape(B, S, ATTN_KV_HEADS, ATTN_HEAD_DIM), positions)
        av = av.reshape(B, S, ATTN_KV_HEADS, ATTN_HEAD_DIM)
        attn_out = sliding_window_sink_attention(aq, ak, av, sinks[i])
        dn_out = gated_deltanet(dq, dk, dv, dz, db, da, conv_w[i], a_log[i],
                                dt_bias[i], dn_norm[i])
        h = h + jnp.concatenate([attn_out, dn_out], axis=-1) @ w_o[i]
        m = rmsnorm(h, norm_mlp[i])
        h = h + jnp.square(jax.nn.relu(m @ w_up[i])) @ w_down[i]
        gate = jax.nn.sigmoid(rmsnorm(h, norm_ple[i]) @ w_ple_gate[i])
        h = h + gate * (p[i] @ w_ple_proj[i])
    return rmsnorm(h, norm_final)
```

```python
import os
import numpy as np
from contextlib import ExitStack
import concourse.bass as bass
import concourse.mybir as mybir
from concourse.bass_utils import run_bass_kernel_spmd

F32 = mybir.dt.float32
BF16 = mybir.dt.bfloat16
F32R = mybir.dt.float32r
AF = mybir.ActivationFunctionType
ALU = mybir.AluOpType

D = 1024
EPS = 1e-6
TT = 512
NSLOT = 29
NW = 3
NEG = -30000.0

C_ID, C_ONE, C_TRI, C_BON, C_ML, C_MU, C_AP, C_AC = [i * 128 for i in range(8)]
C_G = 1024
C_CW = C_G + 32
C_DNN = C_CW + 48
C_ALOG = C_DNN + 1
C_DTB = C_ALOG + 4
C_SNK = C_DTB + 4
C_M0 = C_SNK + 4
C_M1 = C_M0 + 1
C_END = C_M1 + 1
NCP = C_END + 128


class Buf:
    __slots__ = ("w", "r", "excl")

    def __init__(self, excl=False):
        self.w = None
        self.r = {}
        self.excl = excl


class EngRec:
    def __init__(self, name):
        self.name = name
        self.ops = []
        self.sem = None
        self.cnt = 0
        self.waited = {}

    def wait(self, tok):
        sem, val = tok
        if self.waited.get(id(sem), 0) >= val:
            return
        self.waited[id(sem)] = val
        self.ops.append(I("wait_ge", sem, val))


def _deps(reads, writes, mysem=None):
    deps = []
    for b in reads:
        if b.w is not None:
            deps.append(b.w)
        if b.excl:
            deps.extend(t for t in b.r.values() if t[0] is not mysem)
    for b in writes:
        if b.w is not None:
            deps.append(b.w)
        deps.extend(b.r.values())
    return deps


def _update(tok, reads, writes):
    for b in writes:
        b.w = tok
        b.r = {}
    for b in reads:
        sem, val = tok
        b.r[id(sem)] = tok


class Ins:
    __slots__ = ("name", "a", "k")

    def __init__(self, name, a, k):
        self.name, self.a, self.k = name, a, k

    def __call__(self, e):
        return getattr(e, self.name)(*self.a, **self.k)

    def cost(self, engname):
        try:
            if self.name == "matmul":
                rhs = self.k["rhs"]
                n = 1
                for d in rhs.shape[1:]:
                    n *= d
                dt = self.k["lhsT"].dtype
                if dt == BF16:
                    return max(0.06, n / 2100.0)
                if dt == F32R:
                    return max(0.213, n / 600.0)
                return max(0.45, n / 300.0)
            if self.name == "transpose":
                return 0.12
            out = self.k.get("out")
            n = 1
            for d in out.shape[1:]:
                n *= d
            if engname == "act":
                return 0.2 + n / 1000.0
            return 0.15 + n / 950.0
        except Exception:
            return 0.3


def I(name, *a, **k):
    return Ins(name, a, k)


class SchedState:
    cur = None
    fin = {}
    LAT = 0.3


def _est(eng, deps, cost):
    start = getattr(eng, "t", 0.0)
    for tok in deps:
        start = max(start, SchedState.fin.get((id(tok[0]), tok[1]), 0.0) + SchedState.LAT)
    fin = start + cost
    eng.t = fin
    if SchedState.cur is not None:
        SchedState.cur.clock = max(SchedState.cur.clock, fin)
    return fin


def op(eng, fn, R=(), W=()):
    deps = _deps(R, W, eng.sem)
    for tok in deps:
        if eng.name == "pe" and tok[0] is eng.sem:
            continue
        eng.wait(tok)
    eng.cnt += 1
    tok = (eng.sem, eng.cnt)
    SchedState.fin[(id(eng.sem), eng.cnt)] = _est(eng, deps, fn.cost(eng.name) if isinstance(fn, Ins) else 0.3)
    sem = eng.sem
    eng.ops.append(lambda e, fn=fn, sem=sem: fn(e).then_inc(sem, 1))
    _update(tok, R, W)
    return tok


def pe_group(eng, fns, R=(), W=()):
    deps = _deps(R, W, eng.sem)
    for tok in deps:
        if tok[0] is eng.sem:
            continue
        eng.wait(tok)
    SchedState.fin[(id(eng.sem), eng.cnt + 1)] = _est(
        eng, deps, sum(fn.cost("pe") if isinstance(fn, Ins) else 0.2 for fn in fns))
    for fn in fns[:-1]:
        eng.ops.append(lambda e, fn=fn: fn(e))
    eng.cnt += 1
    tok = (eng.sem, eng.cnt)
    sem = eng.sem
    last = fns[-1]
    eng.ops.append(lambda e, fn=last, sem=sem: fn(e).then_inc(sem, 1))
    _update(tok, R, W)
    return tok


class DmaQ:
    def __init__(self, eng, sems):
        self.eng = eng
        self.sems = sems
        self.vals = [0] * len(sems)
        self.i = 0

    def dma(self, fn, R=(), W=()):
        k = self.i % len(self.sems)
        self.i += 1
        if self.vals[k] > 0:
            self.eng.wait((self.sems[k], self.vals[k]))
        for tok in _deps(R, W):
            self.eng.wait(tok)
        self.vals[k] += 16
        sem = self.sems[k]
        tok = (sem, self.vals[k])
        self.eng.ops.append(lambda e, fn=fn, sem=sem: fn(e).then_inc(sem, 16))
        _update(tok, R, W)
        return tok


def build(S, dbg=None):
    NT = S // TT
    SchedState.fin = {}
    SchedState.cur = None
    PH = int(os.environ.get('KPH', '99'))
    SUB = int(os.environ.get('KSUB', '99'))
    SKEW = int(os.environ.get('KSKEW', '10'))
    nc = bass.Bass("TRN2", target_bir_lowering=False)
    xT_d = nc.dram_tensor("xT", [128, 8, S], F32, kind="ExternalInput").ap()
    pT_d = nc.dram_tensor("pT", [128, 2, S], F32, kind="ExternalInput").ap()
    wp_d = nc.dram_tensor("wpack", [NSLOT, 128, 4096], F32, kind="ExternalInput").ap()
    cp_d = nc.dram_tensor("cpack", [128, NCP], F32, kind="ExternalInput").ap()
    rope_d = nc.dram_tensor("rope", [128, 4, S], F32, kind="ExternalInput").ap()
    out_d = nc.dram_tensor("outT", [128, 8, S], F32, kind="ExternalOutput").ap()
    dbg_d = {}
    if dbg:
        for nm, shp in dbg.items():
            dbg_d[nm] = nc.dram_tensor("dbg_" + nm, list(shp), F32, kind="ExternalOutput").ap()

    es = ExitStack()
    with es:
        def sb(name, shape, dt):
            return es.enter_context(nc.sbuf_tensor(name, list(shape), dt))

        def sem(name):
            return es.enter_context(nc.semaphore(name))

        pe, act, dve, pool, sp = (EngRec(n) for n in ("pe", "act", "dve", "pool", "sp"))
        for e in (pe, act, dve, pool, sp):
            e.sem = sem("s_" + e.name)
        spq = DmaQ(sp, [sem("spd%d" % i) for i in range(12)])
        plq = DmaQ(pool, [sem("pld%d" % i) for i in range(NW)])

        cpk = sb("cpk", [128, NCP], F32)
        onesb = sb("onesb", [128, 128], BF16)
        am = sb("am", [128, 2, 4, 128], BF16)
        mL4 = sb("mL4", [128, 4, 128], F32)
        mU4 = sb("mU4", [128, 4, 128], F32)
        small = sb("small", [128, 64], F32)
        hT = sb("hT", [128, 8, TT], F32)
        uT = sb("uT", [128, 8, TT], BF16)
        rstd = sb("rstd", [128, TT], F32)
        ropeT = sb("ropeT", [128, 4, TT], F32)
        wb = [sb("wb%d" % i, [128, 4096], BF16) for i in range(NW)]
        qT = sb("qT", [128, 4, TT], BF16)
        kT = sb("kT", [128, 2, TT + 128], BF16)
        vtok = sb("vtok", [128, 5, 128], BF16)
        Eb = [sb("Eb%d" % i, [128, 4, 128], BF16) for i in range(2)]
        PT = [sb("PT%d" % i, [128, 4, 128], BF16) for i in range(4)]
        aoT = sb("aoT", [128, 4, TT], BF16)
        rt1 = sb("rt1", [128, TT], F32)
        rt2 = sb("rt2", [128, TT], F32)
        scrA = sb("scrA", [128, 12 * (TT + 3)], F32R)
        scrB = sb("scrB", [128, 16 * TT], F32)
        zs = sb("zs", [128, 4, TT], BF16)
        dnT = sb("dnT", [128, 4, TT], BF16)
        hist = sb("hist", [128, 12, 3], F32)
        ba_sb = sb("ba_sb", [128, 4, 8], F32)
        gtok = sb("gtok", [128, 64], F32)
        gpre = sb("gpre", [128, 4, 12], F32)
        gb4 = sb("gb4", [128, 4, 128], F32)
        tmpM = sb("tmpM", [128, 4, 128], F32)
        Dst = sb("Dst", [128, 4, 128], F32)
        DTi = sb("DTi", [128, 4, 128], F32)
        egbc = sb("egbc", [128, 4, 128], F32)
        dl = sb("dl", [128, 4, 2], F32)
        Am = [sb("Am%d" % i, [128, 4, 128], F32R) for i in range(2)]
        Bm = [sb("Bm%d" % i, [128, 4, 128], F32R) for i in range(2)]
        Pm = [sb("Pm%d" % i, [128, 4, 128], F32R) for i in range(2)]
        u_sb = sb("u_sb", [128, 4, 128], F32)
        wT = sb("wT", [128, 4, 128], F32R)
        aT = sb("aT", [128, 4, 128], F32R)
        qgT = sb("qgT", [128, 4, 128], F32R)
        kd_t = [sb("kd_t%d" % i, [128, 4, 128], F32R) for i in range(2)]
        kbg_t = sb("kbg_t", [128, 4, 128], F32R)
        vb_t = sb("vb_t", [128, 4, 128], F32R)
        vnew = sb("vnew", [128, 4, 128], F32R)
        Sst = [sb("Sst%d" % i, [128, 4, 128], F32R) for i in range(2)]
        pTs = sb("pTs", [128, 2, TT], F32)
        pTb = sb("pTb", [128, 2, TT], BF16)
        sg = rt1

        ps = [es.enter_context(nc.psum_tensor("ps%d" % i, [128, 512], F32)) for i in range(8)]
        ps_b = [Buf(excl=True) for _ in range(8)]
        rot = [0]

        pinned = set()

        def bank(pin=False):
            for _ in range(12):
                i = rot[0] % 6
                rot[0] += 1
                if i not in pinned:
                    break
            else:
                raise RuntimeError("no free PSUM bank")
            if pin:
                pinned.add(i)
            return ps[i], ps_b[i]

        def unpin(pbb):
            pinned.discard(ps_b.index(pbb))

        pcw = scrA[:, :].rearrange("p (c t) -> p c t", c=12)
        pc = scrA[:, :].bitcast(F32).rearrange("p (c t) -> p c t", c=12)
        qkn = scrA[:, 0:8 * TT].rearrange("p (c t) -> p c t", c=8)
        outt = scrB[:, 0:8 * TT].rearrange("p (c t) -> p c t", c=8)
        y = scrB[:, 0:12 * TT].rearrange("p (c t) -> p c t", c=12)
        oTs = scrB[:, 12 * TT:16 * TT].rearrange("p (c t) -> p c t", c=4)
        hid = scrB[:, :].bitcast(BF16).rearrange("p (c t) -> p c t", c=32)

        def c_(off, n=128):
            return cpk[:, off:off + n]

        ident = c_(C_ID)
        onesf = c_(C_ONE)
        tri = c_(C_TRI)
        bones = c_(C_BON)
        endsel = c_(C_END)
        esink = small[:, 0:4]
        negA = small[:, 4:8]
        epsc = small[:, 8:9]
        eps128 = small[:, 9:10]
        onec = small[:, 10:11]
        dtb = c_(C_DTB, 4)
        dtb4 = small[:, 16:32].rearrange("p (b h) -> p b h", b=4)
        negA4 = small[:, 32:48].rearrange("p (b h) -> p b h", b=4)

        B = {}
        for nm in ("gpre", "cpk", "const", "hT", "uT", "rstd", "rope", "qT", "kT", "vtok", "Eb", "aoT", "rt1", "rt2",
                   "scrA", "scrB", "zs", "dnT", "hist", "ba", "gtok", "gb4", "tmpM", "Dst", "DTi", "egbc", "dl",
                   "u_sb", "wT", "aT", "qgT", "kd_t", "kbg_t", "vb_t", "vnew", "pTs", "pTb", "sg", "oTs", "outt"):
            B[nm] = Buf()
        BS = [{nm: Buf() for nm in ("gb4", "tmpM", "gtok", "egbc", "dl", "Dst", "DTi", "qgT", "Bm0", "Bm1", "Am0", "Am1",
                                    "Pm0", "Pm1", "aT", "kd_t", "kbg_t", "vb_t", "u_sb", "wT", "vnew", "S0", "S1", "oTs")}
              for _ in range(2)]
        hT_b = [Buf() for _ in range(8)]
        hTc = [hT[:, c, :] for c in range(8)]
        _xpn = ("gb4", "tmpM", "Dst", "DTi", "egbc", "u_sb")
        XP = [t[:, :, :].rearrange("p h t -> p (h t)") for t in (gb4, tmpM, Dst, DTi, egbc, u_sb)]
        XP += [pTs[:, 0, :], pTs[:, 1, :]]
        XP_b = [(BS[0][nm], BS[1][nm]) for nm in _xpn] + [(B["pTs"],), (B["pTs"],)]
        pc_b = [Buf() for _ in range(12)]
        y_b = [Buf() for _ in range(12)]
        qkn_b = [Buf() for _ in range(8)]
        uT_b = [Buf() for _ in range(8)]
        PT_b = [Buf() for _ in range(4)]
        Eb_b = [Buf(), Buf()]
        Am_b = [Buf(), Buf()]
        Bm_b = [Buf(), Buf()]
        Pm_b = [Buf(), Buf()]
        S_b = [Buf(), Buf()]
        wb_b = [Buf() for _ in range(NW)]

        wstate = {"next": 0}
        total_loads = NT * NSLOT

        def issue_load(L):
            slot = L % NSLOT
            i = L % NW
            if slot == 7:
                src = wp_d[slot].rearrange("p (k n) -> p k n", k=8)[:, :, 0:136]
                dst = wb[i][:, :].rearrange("p (k n) -> p k n", k=8)[:, :, 0:136]
            elif slot == 28:
                src = wp_d[slot][:, 0:2048]
                dst = wb[i][:, 0:2048]
            else:
                src = wp_d[slot]
                dst = wb[i][:, :]
            plq.dma(I("dma_start", out=dst, in_=src), R=(), W=(wb_b[i],))

        def wslot(L):
            while wstate["next"] <= min(L, total_loads - 1):
                issue_load(wstate["next"])
                wstate["next"] += 1
            return wb[L % NW], wb_b[L % NW]

        def wdone(L):
            while wstate["next"] < total_loads and wstate["next"] <= L + NW:
                issue_load(wstate["next"])
                wstate["next"] += 1

        spq.dma(I("dma_start", out=cpk[:, :], in_=cp_d), W=(B["cpk"],))
        CK = (B["cpk"],)
        op(dve, I("tensor_copy", out=onesb[:, :], in_=onesf), R=CK, W=(B["const"],))
        op(dve, I("memset", am[:, :, :, :], 0.0), W=(B["const"],))
        for a_i in range(2):
            for j in range(4):
                if j == 0 and a_i == 0:
                    continue
                src = c_(C_AP) if j % 2 == 0 else c_(C_AC)
                op(dve, I("tensor_copy", out=am[:, a_i, j, :], in_=src),
                   R=CK, W=(B["const"],))
        for j in range(4):
            op(dve, I("tensor_copy", out=mL4[:, j, :], in_=c_(C_ML)), R=CK, W=(B["const"],))
            op(dve, I("tensor_copy", out=mU4[:, j, :], in_=c_(C_MU)), R=CK, W=(B["const"],))
        op(dve, I("memset", small[:, 8:9], EPS), W=(B["const"],))
        op(dve, I("memset", small[:, 9:10], 128.0 * EPS), W=(B["const"],))
        op(dve, I("memset", small[:, 10:11], 1.0), W=(B["const"],))
        op(act, I("activation", out=small[:, 0:4], in_=c_(C_SNK, 4), func=AF.Exp), R=CK, W=(B["const"],))
        op(act, I("activation", out=small[:, 4:8], in_=c_(C_ALOG, 4), func=AF.Exp), R=CK, W=(B["const"],))
        op(dve, I("tensor_scalar", out=small[:, 4:8], in0=small[:, 4:8], scalar1=-1.0, scalar2=None,
                                          op0=ALU.mult), R=CK, W=(B["const"],))
        op(dve, I("tensor_scalar", out=Sst[0][:, :, :], in0=mL4[:, :, :], scalar1=0.0, scalar2=None, op0=ALU.mult),
           R=(B["const"],), W=(BS[0]["S0"], BS[1]["S0"]))
        for bb in range(4):
            op(dve, I("tensor_copy", out=small[:, 16 + 4 * bb:20 + 4 * bb], in_=c_(C_DTB, 4)), R=CK, W=(B["const"],))
            op(dve, I("tensor_copy", out=small[:, 32 + 4 * bb:36 + 4 * bb], in_=small[:, 4:8]), R=(B["const"],), W=(B["const"],))
        op(dve, I("tensor_scalar", out=vnew[:, :, :], in0=mL4[:, :, :], scalar1=0.0, scalar2=None, op0=ALU.mult),
           R=(B["const"],), W=(BS[0]["vnew"], BS[1]["vnew"]))
        op(dve, I("memset", hist[:, :, :], 0.0), W=(B["hist"],))
        op(dve, I("memset", kT[:, :, :], 0.0), W=(B["kT"],))
        op(dve, I("memset", vtok[:, :, :], 0.0), W=(B["vtok"],))
        CC = (B["cpk"], B["const"])

        def _tb(x):
            return x if isinstance(x, tuple) else (x,)

        def rmsnorm(srcs, src_b, gcol, dst, dst_b, dst_all_b=None):
            for c in range(8):
                op(act, I("activation", out=uT[:, c, :], in_=srcs[c], func=AF.Square), R=_tb(src_b[c]), W=(uT_b[c],))
            pb, pbb = bank()
            for c in range(8):
                pe_group(pe, [I("matmul", pb[:, :], lhsT=onesb[:, :], rhs=uT[:, c, :], start=(c == 0), stop=(c == 7))],
                         R=(uT_b[c],) + CC, W=(pbb,))
            op(act, I("activation", out=rstd[:, :], in_=pb[:, :], func=AF.Ln, bias=epsc, scale=1.0 / D),
               R=(pbb,) + CC, W=(B["rstd"],))
            op(act, I("activation", out=rstd[:, :], in_=rstd[:, :], func=AF.Exp, scale=-0.5),
               R=(B["rstd"],), W=(B["rstd"],))
            for c in range(8):
                wbufs = (dst_b[c],) if dst_all_b is None else (dst_all_b,)
                op(dve, I("scalar_tensor_tensor",
                    out=dst[:, c, :], in0=srcs[c], scalar=cpk[:, gcol + c:gcol + c + 1], in1=rstd[:, :],
                    op0=ALU.mult, op1=ALU.mult), R=_tb(src_b[c]) + (B["rstd"],) + CC, W=wbufs)

        def mm8(pb, wv, j, src=None, n0=0, n1=TT):
            return [I("matmul", pb[:, 0:n1 - n0], lhsT=wv[:, kc, j * 128:(j + 1) * 128],
                                              rhs=uT[:, kc, n0:n1], start=(kc == 0), stop=(kc == 7))
                    for kc in range(8)]

        def mm8_split(pb, pbb, wv, j, wtb):
            fns = mm8(pb, wv, j)
            for kc in range(8):
                pe_group(pe, [fns[kc]], R=(uT_b[kc], wtb), W=(pbb,))

        def dbg_dump(name, ap_src, bufs, t0=None):
            if name not in dbg_d:
                return
            d = dbg_d[name]
            plq.dma(I("dma_start", out=d, in_=ap_src), R=bufs, W=())

        out_toks = []

        def tile_body(ti):
            t0 = ti * TT
            L0 = ti * NSLOT
            NPF = 8 if ti > 0 else 0
            for c in range(NPF, 8):
                spq.dma(I("dma_start", out=hT[:, c, :], in_=xT_d[:, c, t0:t0 + TT]), W=(hT_b[c],))
            spq.dma(I("dma_start", out=ropeT[:, :, :], in_=rope_d[:, :, t0:t0 + TT]), W=(B["rope"],))
            if ti > 0:
                op(dve, I("tensor_copy", out=kT[:, :, 0:128], in_=kT[:, :, TT:TT + 128]),
                   R=(B["kT"],), W=(B["kT"],))
                op(dve, I("tensor_copy", out=vtok[:, 0, :], in_=vtok[:, 4, :]), R=(B["vtok"],), W=(B["vtok"],))
            rmsnorm([XP[c] if c < NPF else hTc[c] for c in range(8)],
                    [XP_b[c] if c < NPF else hT_b[c] for c in range(8)], C_G + 0, uT, uT_b)
            for c in range(NPF):
                op(act, I("activation", out=hT[:, c, :], in_=XP[c], func=AF.Copy), R=XP_b[c], W=(hT_b[c],))
            spq.dma(I("dma_start", out=pTs[:, :, :], in_=pT_d[:, :, t0:t0 + TT]), W=(B["pTs"],))

            if PH < 2:
                return
            for s in range(3):
                wt, wtb = wslot(L0 + s)
                wv = wt[:, :].rearrange("p (k n) -> p k n", k=8)
                for pr in range(2):
                    pX, pXb = bank()
                    pR, pRb = bank()
                    if s == 0 and pr == 0:
                        mm8_split(pX, pXb, wv, 0, wtb)
                    else:
                        pe_group(pe, mm8(pX, wv, 2 * pr), R=tuple(uT_b) + (wtb,), W=(pXb,))
                    pe_group(pe, mm8(pR, wv, 2 * pr + 1), R=tuple(uT_b) + (wtb,), W=(pRb,))
                    if s < 2:
                        ci = 0
                        dst = qT[:, 2 * s + pr, :]
                        dstb = B["qT"]
                    else:
                        ci = 2
                        dst = kT[:, pr, 128:128 + TT]
                        dstb = B["kT"]
                    op(dve, I("tensor_tensor", out=rt1[:, :], in0=pX[:, :], in1=ropeT[:, ci, :],
                                                                    op=ALU.mult),
                       R=(pXb, B["rope"]), W=(B["rt1"],))
                    op(dve, I("tensor_tensor", out=rt2[:, :], in0=pR[:, :],
                                                                    in1=ropeT[:, ci + 1, :], op=ALU.mult),
                       R=(pRb, B["rope"]), W=(B["rt2"],))
                    op(dve, I("tensor_tensor", out=dst, in0=rt1[:, :], in1=rt2[:, :], op=ALU.add),
                       R=(B["rt1"], B["rt2"]), W=(dstb,))
                wdone(L0 + s)

            if PH < 3:
                return
            op(dve, I("tensor_copy", out=pcw[:, :, 0:3], in_=hist[:, :, :]), R=(B["hist"],),
               W=(B["scrA"],) + tuple(pc_b) + tuple(qkn_b))

            def conv_grp(grp):
                for ch in range(grp * 4, grp * 4 + 4):
                    cw0 = C_CW + ch * 4
                    op(act, I("activation", out=y[:, ch, :], in_=pc[:, ch, 0:TT], func=AF.Copy,
                              scale=cpk[:, cw0:cw0 + 1]), R=(pc_b[ch],) + CC, W=(y_b[ch], B["scrB"]))
                    for k in range(1, 4):
                        op(dve, I("scalar_tensor_tensor", out=y[:, ch, :], in0=pc[:, ch, k:k + TT],
                                  scalar=cpk[:, cw0 + k:cw0 + k + 1], in1=y[:, ch, :], op0=ALU.mult, op1=ALU.add),
                           R=(pc_b[ch], y_b[ch]) + CC, W=(y_b[ch],))
                op(dve, I("tensor_copy", out=hist[:, grp * 4:grp * 4 + 4, :], in_=pc[:, grp * 4:grp * 4 + 4, TT:TT + 3]),
                   R=tuple(pc_b[grp * 4:grp * 4 + 4]), W=(B["hist"],))

            def silu_grp(grp):
                for ch in range(grp * 4, grp * 4 + 4):
                    op(act, I("activation", out=y[:, ch, :], in_=y[:, ch, :], func=AF.Silu), R=(y_b[ch],), W=(y_b[ch],))

            def l2norm_grp(half):
                pbs = []
                for j in range(4):
                    ch = half * 4 + j
                    sq = PT[j][:, :, :].rearrange("p a q -> p (a q)")
                    op(act, I("activation", out=sq, in_=y[:, ch, :], func=AF.Square), R=(y_b[ch],), W=(PT_b[j],))
                    pb, pbb = bank(pin=True)
                    pbs.append((pb, pbb))
                    pe_group(pe, [I("matmul", pb[:, :], lhsT=onesb[:, :], rhs=sq, start=True, stop=True)],
                             R=(PT_b[j],) + CC, W=(pbb,))
                for j in range(4):
                    pb, pbb = pbs[j]
                    if half == 0:
                        op(act, I("activation", out=pb[:, :], in_=pb[:, :], func=AF.Ln, bias=eps128, scale=128.0),
                           R=(pbb,) + CC, W=(pbb,))
                    else:
                        op(act, I("activation", out=pb[:, :], in_=pb[:, :], func=AF.Ln, bias=epsc, scale=1.0),
                           R=(pbb,) + CC, W=(pbb,))
                for j in range(4):
                    pb, pbb = pbs[j]
                    op(act, I("activation", out=pb[:, :], in_=pb[:, :], func=AF.Exp, scale=-0.5), R=(pbb,), W=(pbb,))
                for j in range(4):
                    ch = half * 4 + j
                    pb, pbb = pbs[j]
                    wl = (qkn_b[ch], pc_b[ch]) + ((pc_b[ch - 1],) if ch > 0 else ())
                    op(dve, I("tensor_tensor", out=qkn[:, ch, :], in0=y[:, ch, :], in1=pb[:, :], op=ALU.mult),
                       R=(y_b[ch], pbb), W=wl)
                    unpin(pbb)

            for s in range(3, 7):
                wt, wtb = wslot(L0 + s)
                wv = wt[:, :].rearrange("p (k n) -> p k n", k=8)
                for j in range(4):
                    pb, pbb = bank()
                    pe_group(pe, mm8(pb, wv, j), R=tuple(uT_b) + (wtb,), W=(pbb,))
                    if s < 6:
                        ch = (s - 3) * 4 + j
                        op(act, I("activation", out=pcw[:, ch, 3:3 + TT], in_=pb[:, :], func=AF.Copy),
                           R=(pbb,), W=(pc_b[ch],))
                    else:
                        op(act, I("activation", out=zs[:, j, :], in_=pb[:, :], func=AF.Silu),
                           R=(pbb,), W=(B["zs"],))
                wdone(L0 + s)
                if s < 6:
                    conv_grp(s - 3)
                if s >= 4:
                    silu_grp(s - 4)
                if s == 5:
                    l2norm_grp(0)
                if s == 6:
                    l2norm_grp(1)
            wt, wtb = wslot(L0 + 7)
            wv = wt[:, :].rearrange("p (k n) -> p k n", k=8)
            for blk in range(4):
                pb, pbb = bank()
                pe_group(pe, [I("matmul",
                    pb[:, 0:128], lhsT=uT[:, kc, blk * 128:(blk + 1) * 128], rhs=wv[:, kc, 0:128],
                    start=(kc == 0), stop=(kc == 7)) for kc in range(8)],
                    R=tuple(uT_b) + (wtb,), W=(pbb,))
                op(act, I("activation", out=vtok[:, 1 + blk, :], in_=pb[:, 0:128], func=AF.Copy),
                   R=(pbb,), W=(B["vtok"],))
            pb, pbb = bank()
            for blk in range(4):
                pe_group(pe, [I("matmul",
                    pb[:, blk * 8:blk * 8 + 8], lhsT=uT[:, kc, blk * 128:(blk + 1) * 128], rhs=wv[:, kc, 128:136],
                    start=(kc == 0), stop=(kc == 7)) for kc in range(8)],
                    R=tuple(uT_b) + (wtb,), W=(pbb,))
            op(act, I("activation", out=ba_sb[:, :, :], in_=pb[:, 0:32].rearrange("p (b n) -> p b n", b=4),
                                                  func=AF.Copy), R=(pbb,), W=(B["ba"],))
            wdone(L0 + 7)
            op(dve, I("tensor_tensor", out=gpre[:, :, 0:4], in0=ba_sb[:, :, 4:8], in1=dtb4, op=ALU.add),
               R=(B["ba"],) + CC, W=(B["gpre"],))
            op(act, I("activation", out=gpre[:, :, 0:4], in_=gpre[:, :, 0:4], func=AF.Exp), R=(B["gpre"],), W=(B["gpre"],))
            op(act, I("activation", out=gpre[:, :, 0:4], in_=gpre[:, :, 0:4], func=AF.Ln, bias=onec, scale=1.0),
               R=(B["gpre"],) + CC, W=(B["gpre"],))
            op(dve, I("tensor_tensor", out=gpre[:, :, 0:4], in0=gpre[:, :, 0:4], in1=negA4, op=ALU.mult),
               R=(B["gpre"],) + CC, W=(B["gpre"],))
            op(act, I("activation", out=gpre[:, :, 4:8], in_=ba_sb[:, :, 0:4], func=AF.Sigmoid), R=(B["ba"],), W=(B["gpre"],))
            op(dve, I("tensor_scalar", out=gpre[:, :, 8:12], in0=gpre[:, :, 4:8], scalar1=-1.0, scalar2=None,
                      op0=ALU.mult), R=(B["gpre"],), W=(B["gpre"],))
            if ti == 0:
                dbg_dump("qT", qT[:, :, :], (B["qT"],))
                dbg_dump("kT", kT[:, :, :], (B["kT"],))
                dbg_dump("vtok", vtok[:, :, :], (B["vtok"],))

            if PH < 4:
                return
            attn_state = {}

            def attn_step(c, bp):
                attn_sc(c, bp)
                attn_pv(c, bp)

            def attn_sc(c, bp):
                kv = c // 2
                scs = [bank(), bank()]
                fns = []
                for bl in range(2):
                    blk = bp * 2 + bl
                    for kb in range(2):
                        k0 = (blk + kb) * 128
                        for hf in range(2):
                            fns.append(I("matmul",
                                         scs[hf][0][:, (bl * 2 + kb) * 128:(bl * 2 + kb + 1) * 128],
                                         lhsT=kT[hf * 64:(hf + 1) * 64, kv, k0:k0 + 128],
                                         rhs=qT[hf * 64:(hf + 1) * 64, c, blk * 128:(blk + 1) * 128],
                                         start=True, stop=True))
                pe_group(pe, fns, R=(B["kT"], B["qT"]), W=(scs[0][1], scs[1][1]))
                ami = 0 if (ti == 0 and bp == 0) else 1
                pis = []
                for hf in range(2):
                    op(act, I("activation", out=Eb[hf][:, :, :],
                              in_=scs[hf][0][:, :].rearrange("p (a q) -> p a q", a=4), func=AF.Exp),
                       R=(scs[hf][1],), W=(Eb_b[hf],))
                    pi = ((c * 2 + bp) % 2) * 2 + hf
                    pis.append(pi)
                    op(dve, I("tensor_tensor", out=PT[pi][:, :, :], in0=Eb[hf][:, :, :],
                              in1=am[:, ami, :, :], op=ALU.mult),
                       R=(Eb_b[hf],) + CC, W=(PT_b[pi],))
                attn_state[(c, bp)] = pis

            def attn_pv(c, bp):
                kv = c // 2
                pis = attn_state[(c, bp)]
                fns = []
                for bl in range(2):
                    blk = bp * 2 + bl
                    for which in range(2):
                        pso = ps[6] if which == 0 else ps[7]
                        for hf in range(2):
                            for kb in range(2):
                                if which == 0:
                                    lt = vtok[:, blk + kb, kv * 64:(kv + 1) * 64]
                                else:
                                    lt = onesb[:, 0:64]
                                fns.append(I("matmul",
                                             pso[hf * 64:(hf + 1) * 64, blk * 128:(blk + 1) * 128], lhsT=lt,
                                             rhs=PT[pis[hf]][:, bl * 2 + kb, :], start=(kb == 0), stop=(kb == 1)))
                pe_group(pe, fns, R=(PT_b[pis[0]], PT_b[pis[1]], B["vtok"]) + CC, W=(ps_b[6], ps_b[7]))

            def attn_fin(c):
                op(dve, I("tensor_scalar", out=rt1[:, :], in0=ps[7][:, :], scalar1=esink[:, c:c + 1],
                                                       scalar2=None, op0=ALU.add),
                   R=(ps_b[7],) + CC, W=(B["rt1"],))
                op(act, I("activation", out=rt1[:, :], in_=rt1[:, :], func=AF.Ln), R=(B["rt1"],), W=(B["rt1"],))
                op(act, I("activation", out=rt1[:, :], in_=rt1[:, :], func=AF.Exp, scale=-1.0), R=(B["rt1"],), W=(B["rt1"],))
                op(dve, I("tensor_tensor", out=aoT[:, c, :], in0=ps[6][:, :], in1=rt1[:, :], op=ALU.mult),
                   R=(ps_b[6], B["rt1"]), W=(B["aoT"],))


            if PH < 6:
                for c in range(4):
                    attn_step(c, 0)
                    attn_step(c, 1)
                    attn_fin(c)
            if PH < 5:
                return
            if PH < 6:
                return
            def dn_stream(blk, g):
                bs = slice(blk * 128, (blk + 1) * 128)
                H = (2 * g, 2 * g + 1)
                hs = slice(2 * g, 2 * g + 2)
                Bg = BS[g]

                def gc(base):
                    return gtok[:, base + 2 * g:base + 2 * g + 2]

                def v2(pb):
                    return pb[:, 0:256].rearrange("p (h t) -> p h t", h=2)

                for h in H:
                    op(dve, I("tensor_scalar", out=gb4[:, h, :], in0=onesf, scalar1=gpre[:, blk, h:h + 1],
                              scalar2=None, op0=ALU.mult), R=(B["gpre"],) + CC, W=(Bg["gb4"],))
                pgb, pgbb = bank(pin=True)
                pe_group(pe, [I("matmul", pgb[:, j * 128:(j + 1) * 128], lhsT=gb4[:, H[j], :], rhs=tri,
                                start=True, stop=True) for j in range(2)], R=(Bg["gb4"],) + CC, W=(pgbb,))
                g2v = v2(pgb)
                yield
                for j, h in enumerate(H):
                    op(dve, I("scalar_tensor_tensor", out=tmpM[:, h, :], in0=g2v[:, j, :], scalar=1.0, in1=ident,
                              op0=ALU.mult, op1=ALU.mult, accum_out=gtok[:, 12 + h:13 + h]),
                       R=(pgbb,) + CC, W=(Bg["tmpM"], Bg["gtok"]))
                    op(dve, I("scalar_tensor_tensor", out=tmpM[:, h, :], in0=g2v[:, j, :], scalar=1.0, in1=endsel,
                              op0=ALU.mult, op1=ALU.mult, accum_out=gtok[:, 44 + h:45 + h]),
                       R=(pgbb,) + CC, W=(Bg["tmpM"], Bg["gtok"]))
                op(act, I("activation", out=egbc[:, hs, :], in_=g2v, func=AF.Exp), R=(pgbb,), W=(Bg["egbc"],))
                op(act, I("activation", out=dl[:, hs, :], in_=g2v[:, :, 63:128:64], func=AF.Exp), R=(pgbb,), W=(Bg["dl"],))
                op(dve, I("tensor_scalar", out=gc(16), in0=gc(12), scalar1=-1.0, scalar2=None, op0=ALU.mult),
                   R=(Bg["gtok"],), W=(Bg["gtok"],))
                op(act, I("activation", out=gc(20), in_=gc(12), func=AF.Exp), R=(Bg["gtok"],), W=(Bg["gtok"],))
                op(dve, I("tensor_tensor", out=gc(32), in0=gc(44), in1=gc(12), op=ALU.subtract),
                   R=(Bg["gtok"],), W=(Bg["gtok"],))
                op(act, I("activation", out=gc(24), in_=gc(32), func=AF.Exp), R=(Bg["gtok"],), W=(Bg["gtok"],))
                op(dve, I("tensor_tensor", out=gc(28), in0=gpre[:, blk, 4 + 2 * g:6 + 2 * g], in1=gc(20), op=ALU.mult),
                   R=(Bg["gtok"], B["gpre"]), W=(Bg["gtok"],))
                op(dve, I("tensor_scalar", out=gc(36), in0=gc(24), scalar1=cpk[:, C_M0:C_M0 + 1], scalar2=None,
                          op0=ALU.mult), R=(Bg["gtok"],) + CC, W=(Bg["gtok"],))
                op(dve, I("tensor_scalar", out=gc(40), in0=gc(24), scalar1=cpk[:, C_M1:C_M1 + 1], scalar2=None,
                          op0=ALU.mult), R=(Bg["gtok"],) + CC, W=(Bg["gtok"],))
                yield
                op(dve, I("scalar_tensor_tensor", out=tmpM[:, hs, :], in0=g2v, scalar=-1.0, in1=mL4[:, hs, :],
                          op0=ALU.mult, op1=ALU.add), R=(pgbb,) + CC, W=(Bg["tmpM"],))
                for h in H:
                    op(act, I("activation", out=Dst[:, h, :], in_=tmpM[:, h, :], func=AF.Exp,
                              bias=gtok[:, 12 + h:13 + h], scale=1.0), R=(Bg["tmpM"], Bg["gtok"]), W=(Bg["Dst"],))
                op(dve, I("tensor_tensor", out=tmpM[:, hs, :], in0=g2v, in1=mU4[:, hs, :], op=ALU.add),
                   R=(pgbb, Bg["Dst"]) + CC, W=(Bg["tmpM"],))
                unpin(pgbb)
                for h in H:
                    op(act, I("activation", out=DTi[:, h, :], in_=tmpM[:, h, :], func=AF.Exp,
                              bias=gtok[:, 16 + h:17 + h], scale=1.0), R=(Bg["tmpM"], Bg["gtok"]), W=(Bg["DTi"],))
                op(dve, I("tensor_tensor", out=qgT[:, hs, :], in0=qkn[:, hs, bs].bitcast(F32), in1=egbc[:, hs, :],
                          op=ALU.mult), R=(qkn_b[H[0]], qkn_b[H[1]], Bg["egbc"]), W=(Bg["qgT"],))
                yield
                pk, pkb = bank(pin=True)
                pe_group(pe, [I("matmul", pk[:, j * 128:(j + 1) * 128], lhsT=qkn[:, 4 + H[j], bs],
                                rhs=qkn[:, 4 + H[j], bs], start=True, stop=True) for j in range(2)] +
                         [I("matmul", pk[:, 256 + j * 128:256 + (j + 1) * 128], lhsT=qkn[:, 4 + H[j], bs],
                            rhs=qkn[:, H[j], bs], start=True, stop=True) for j in range(2)],
                         R=tuple(qkn_b[h] for h in H) + tuple(qkn_b[4 + h] for h in H), W=(pkb,))
                pt, ptb = bank(pin=True)
                pe_group(pe, [I("transpose", pt[:, j * 128:(j + 1) * 128], qkn[:, 4 + H[j], bs].bitcast(F32), ident)
                              for j in range(2)] +
                         [I("transpose", pt[:, 256 + j * 128:256 + (j + 1) * 128], y[:, 8 + H[j], bs], ident)
                          for j in range(2)], R=tuple(qkn_b[4 + h] for h in H) + tuple(y_b[8 + h] for h in H) + CC, W=(ptb,))
                yield
                for j, h in enumerate(H):
                    op(dve, I("scalar_tensor_tensor", out=Bm[0][:, h, :], in0=pk[:, j * 128:(j + 1) * 128],
                              scalar=gpre[:, blk, 8 + h:9 + h], in1=Dst[:, h, :], op0=ALU.mult, op1=ALU.mult),
                       R=(pkb, B["gpre"], Bg["Dst"]), W=(Bg["Bm0"],))
                op(dve, I("tensor_tensor", out=aT[:, hs, :], in0=pk[:, 256:512].rearrange("p (h t) -> p h t", h=2),
                          in1=DTi[:, hs, :], op=ALU.mult), R=(pkb, Bg["DTi"]), W=(Bg["aT"],))
                unpin(pkb)
                pa, pab = bank(pin=True)
                pe_group(pe, [I("transpose", pa[:, j * 128:(j + 1) * 128], Bm[0][:, H[j], :].bitcast(F32), ident)
                              for j in range(2)], R=(Bg["Bm0"],) + CC, W=(pab,))
                for j, h in enumerate(H):
                    for cc2 in range(2):
                        op(dve, I("tensor_scalar", out=kd_t[cc2][:, h, :], in0=pt[:, j * 128:(j + 1) * 128],
                                  scalar1=gtok[:, 36 + 4 * cc2 + h:37 + 4 * cc2 + h], scalar2=None, op0=ALU.mult),
                           R=(ptb, Bg["gtok"]), W=(Bg["kd_t"],))
                    op(dve, I("tensor_scalar", out=kbg_t[:, h, :], in0=pt[:, j * 128:(j + 1) * 128],
                              scalar1=gtok[:, 28 + h:29 + h], scalar2=None, op0=ALU.mult),
                       R=(ptb, Bg["gtok"]), W=(Bg["kbg_t"],))
                    op(dve, I("tensor_scalar", out=vb_t[:, h, :], in0=pt[:, 256 + j * 128:256 + (j + 1) * 128],
                              scalar1=gpre[:, blk, 4 + h:5 + h], scalar2=None, op0=ALU.mult),
                       R=(ptb, B["gpre"]), W=(Bg["vb_t"],))
                unpin(ptb)
                yield
                op(act, I("activation", out=Am[0][:, hs, :], in_=v2(pa), func=AF.Copy), R=(pab,), W=(Bg["Am0"],))
                for j, h in enumerate(H):
                    op(dve, I("tensor_tensor", out=Pm[0][:, h, :], in0=pa[:, j * 128:(j + 1) * 128], in1=ident,
                              op=ALU.add), R=(pab,) + CC, W=(Bg["Pm0"],))
                unpin(pab)
                yield
                for k in range(5):
                    a, b_ = k % 2, (k + 1) % 2
                    AmA, BmA, PmA = Bg["Am%d" % a], Bg["Bm%d" % a], Bg["Pm%d" % a]
                    AmB, BmB, PmB = Bg["Am%d" % b_], Bg["Bm%d" % b_], Bg["Pm%d" % b_]
                    p12, p12b = bank(pin=True)
                    fns = [I("matmul", p12[:, 256 + j * 128:256 + (j + 1) * 128], lhsT=Am[a][:, H[j], :],
                             rhs=Bm[a][:, H[j], :], start=True, stop=True) for j in range(2)]
                    if k < 4:
                        fns = [I("matmul", p12[:, j * 128:(j + 1) * 128], lhsT=Bm[a][:, H[j], :],
                                 rhs=Am[a][:, H[j], :], start=True, stop=True) for j in range(2)] + fns
                    pe_group(pe, fns, R=(AmA, BmA), W=(p12b,))
                    yield
                    if k < 4:
                        op(act, I("activation", out=Am[b_][:, hs, :], in_=v2(p12), func=AF.Copy), R=(p12b,), W=(AmB,))
                    op(dve, I("tensor_copy", out=Bm[b_][:, hs, :],
                              in_=p12[:, 256:512].rearrange("p (h t) -> p h t", h=2)), R=(p12b,), W=(BmB,))
                    unpin(p12b)
                    p3, p3b = bank(pin=True)
                    pe_group(pe, [I("matmul", p3[:, j * 128:(j + 1) * 128], lhsT=Bm[b_][:, H[j], :],
                                    rhs=Pm[a][:, H[j], :], start=True, stop=True) for j in range(2)],
                             R=(BmB, PmA), W=(p3b,))
                    yield
                    op(dve, I("tensor_tensor", out=Pm[b_][:, hs, :], in0=v2(p3), in1=Pm[a][:, hs, :].bitcast(F32),
                              op=ALU.add), R=(p3b, PmA), W=(PmB,))
                    unpin(p3b)
                TTm, TTb = Pm[1], Bg["Pm1"]
                pu, pub = bank(pin=True)
                pe_group(pe, [I("matmul", pu[:, j * 128:(j + 1) * 128], lhsT=TTm[:, H[j], :], rhs=vb_t[:, H[j], :],
                                start=True, stop=True) for j in range(2)] +
                         [I("matmul", pu[:, 256 + j * 128:256 + (j + 1) * 128], lhsT=kbg_t[:, H[j], :],
                            rhs=TTm[:, H[j], :], start=True, stop=True) for j in range(2)],
                         R=(TTb, Bg["vb_t"], Bg["kbg_t"]), W=(pub,))
                yield
                op(act, I("activation", out=u_sb[:, hs, :], in_=v2(pu), func=AF.Copy), R=(pub,), W=(Bg["u_sb"],))
                op(dve, I("tensor_copy", out=wT[:, hs, :], in_=pu[:, 256:512].rearrange("p (h t) -> p h t", h=2)),
                   R=(pub,), W=(Bg["wT"],))
                unpin(pub)
                yield
                po, pob = bank(pin=True)
                for cch in range(2):
                    gch = (ti * 4 + blk) * 2 + cch
                    si, so = gch % 2, (gch + 1) % 2
                    rs = slice(cch * 64, (cch + 1) * 64)
                    pv, pvb = bank(pin=True)
                    pe_group(pe, [I("matmul", pv[:, j * 128:(j + 1) * 128], lhsT=wT[:, H[j], :],
                                    rhs=Sst[si][:, H[j], :], start=True, stop=True) for j in range(2)],
                             R=(Bg["wT"], Bg["S%d" % si]), W=(pvb,))
                    yield
                    op(dve, I("tensor_tensor", out=vnew[rs, hs, :], in0=u_sb[rs, hs, :],
                              in1=pv[rs, 0:256].rearrange("p (h t) -> p h t", h=2), op=ALU.subtract),
                       R=(pvb, Bg["u_sb"]), W=(Bg["vnew"],))
                    unpin(pvb)
                    fns = []
                    for j, h in enumerate(H):
                        fns.append(I("matmul", po[:, j * 128 + cch * 64:j * 128 + cch * 64 + 64],
                                     lhsT=Sst[si][:, h, :], rhs=qgT[:, h, cch * 64:(cch + 1) * 64],
                                     start=True, stop=False))
                        fns.append(I("matmul", po[:, j * 128 + cch * 64:j * 128 + cch * 64 + 64],
                                     lhsT=vnew[:, h, :], rhs=aT[:, h, cch * 64:(cch + 1) * 64],
                                     start=False, stop=True))
                    for j, h in enumerate(H):
                        fns.append(I("matmul", po[:, 256 + j * 128:256 + (j + 1) * 128], lhsT=kd_t[cch][:, h, :],
                                     rhs=vnew[:, h, :], start=True, stop=True))
                    pe_group(pe, fns, R=(Bg["S%d" % si], Bg["qgT"], Bg["vnew"], Bg["aT"], Bg["kd_t"]), W=(pob,))
                    yield
                    for j, h in enumerate(H):
                        op(dve, I("scalar_tensor_tensor", out=Sst[so][:, h, :], in0=Sst[si][:, h, :].bitcast(F32),
                                  scalar=dl[:, h, cch:cch + 1], in1=po[:, 256 + j * 128:256 + (j + 1) * 128],
                                  op0=ALU.mult, op1=ALU.add),
                           R=(pob, Bg["S%d" % si], Bg["dl"]), W=(Bg["S%d" % so],))
                    yield
                op(act, I("activation", out=oTs[:, hs, bs], in_=v2(po), func=AF.Copy), R=(pob,), W=(Bg["oTs"],))
                unpin(pob)

            def attn_stream(blk):
                attn_sc(blk, 0)
                yield
                yield
                attn_pv(blk, 0)
                yield
                attn_sc(blk, 1)
                yield
                yield
                attn_pv(blk, 1)
                yield
                attn_fin(blk)

            def gated(g):
                tmp, tmpb = (rt2, B["rt2"]) if g == 0 else (rstd, B["rstd"])
                pbs = []
                for h in (2 * g, 2 * g + 1):
                    op(act, I("activation", out=uT[:, h, :], in_=oTs[:, h, :], func=AF.Square),
                       R=(BS[g]["oTs"],), W=(uT_b[h],))
                    pb, pbb = bank(pin=True)
                    pbs.append((pb, pbb))
                    pe_group(pe, [I("matmul", pb[:, :], lhsT=onesb[:, :], rhs=uT[:, h, :], start=True, stop=True)],
                             R=(uT_b[h],) + CC, W=(pbb,))
                yield
                for pb, pbb in pbs:
                    op(act, I("activation", out=pb[:, :], in_=pb[:, :], func=AF.Ln, bias=epsc, scale=1.0 / 128.0),
                       R=(pbb,) + CC, W=(pbb,))
                    op(act, I("activation", out=pb[:, :], in_=pb[:, :], func=AF.Exp, scale=-0.5), R=(pbb,), W=(pbb,))
                yield
                for j, h in enumerate((2 * g, 2 * g + 1)):
                    pb, pbb = pbs[j]
                    op(dve, I("scalar_tensor_tensor", out=tmp[:, :], in0=oTs[:, h, :], scalar=cpk[:, C_DNN:C_DNN + 1],
                              in1=pb[:, :], op0=ALU.mult, op1=ALU.mult), R=(BS[g]["oTs"], pbb) + CC, W=(tmpb,))
                    unpin(pbb)
                    op(dve, I("tensor_tensor", out=dnT[:, h, :], in0=tmp[:, :], in1=zs[:, h, :], op=ALU.mult),
                       R=(tmpb, B["zs"]), W=(B["dnT"],))

            def chain(mk, tail=None):
                for blk in range(4):
                    yield from mk(blk)
                if tail is not None:
                    yield from tail

            class St:
                def __init__(self, gen, bias=0.0):
                    self.gen = gen
                    self.clock = 0.0
                    self.bias = bias

            sts = [St(chain(lambda blk: dn_stream(blk, 0), gated(0))), St(chain(lambda blk: dn_stream(blk, 1), gated(1))),
                   St(chain(attn_stream), bias=-6.0)]
            base = max(pe.t if hasattr(pe, "t") else 0.0, 0.0)
            for st in sts:
                st.clock = 0.0
            sts[1].clock = float(SKEW) * 0.0
            alive = list(sts)
            while alive:
                st = min(alive, key=lambda x: x.clock + x.bias)
                SchedState.cur = st
                try:
                    next(st.gen)
                except StopIteration:
                    alive.remove(st)
                SchedState.cur = None

            if PH < 8:
                return
            for s in range(2):
                wt, wtb = wslot(L0 + 8 + s)
                wv = wt[:, :].rearrange("p (k n) -> p k n", k=8)
                for j in range(4):
                    oc = s * 4 + j
                    pb, pbb = bank()
                    fns = []
                    for kc in range(8):
                        src = aoT[:, kc, :] if kc < 4 else dnT[:, kc - 4, :]
                        fns.append(I("matmul",
                            pb[:, :], lhsT=wv[:, kc, j * 128:(j + 1) * 128], rhs=src, start=(kc == 0), stop=(kc == 7)))
                    pe_group(pe, fns, R=(B["aoT"], B["dnT"], wtb), W=(pbb,))
                    op(dve, I("tensor_tensor", out=hT[:, oc, :], in0=pb[:, :], in1=hT[:, oc, :],
                                                                    op=ALU.add), R=(pbb, hT_b[oc]), W=(hT_b[oc],))
                wdone(L0 + 8 + s)

            if PH < 9:
                return
            if ti + 1 < NT:
                for c in range(6):
                    spq.dma(I("dma_start", out=XP[c], in_=xT_d[:, c, t0 + TT:t0 + 2 * TT]), W=XP_b[c])
            rmsnorm(hTc, hT_b, C_G + 8, uT, uT_b)
            for s in range(8):
                wt, wtb = wslot(L0 + 10 + s)
                wv = wt[:, :].rearrange("p (k n) -> p k n", k=8)
                for j in range(4):
                    hc = s * 4 + j
                    pb, pbb = bank()
                    if hc == 0:
                        mm8_split(pb, pbb, wv, j, wtb)
                    else:
                        pe_group(pe, mm8(pb, wv, j), R=tuple(uT_b) + (wtb,), W=(pbb,))
                    op(act, I("activation", out=rt2[:, :], in_=pb[:, :], func=AF.Square),
                       R=(pbb,), W=(B["rt2"],))
                    op(dve, I("scalar_tensor_tensor",
                        out=hid[:, hc, :], in0=pb[:, :], scalar=0.0, in1=rt2[:, :], op0=ALU.is_gt, op1=ALU.mult),
                       R=(pbb, B["rt2"]), W=(B["scrB"],) + tuple(y_b))
                wdone(L0 + 10 + s)
            for oc in range(8):
                wt, wtb = wslot(L0 + 18 + oc)
                wv = wt[:, :].rearrange("p (k n) -> p k n", k=32)
                pb, pbb = bank()
                pe_group(pe, [I("matmul", pb[:, :], lhsT=wv[:, kc, :], rhs=hid[:, kc, :],
                                                                      start=(kc == 0), stop=(kc == 31))
                              for kc in range(32)], R=(B["scrB"], wtb), W=(pbb,))
                op(dve, I("tensor_tensor", out=hT[:, oc, :], in0=pb[:, :], in1=hT[:, oc, :],
                                                                op=ALU.add), R=(pbb, hT_b[oc]), W=(hT_b[oc],))
                wdone(L0 + 18 + oc)

            if PH < 10:
                return
            rmsnorm(hTc, hT_b, C_G + 16, uT, uT_b)
            op(act, I("activation", out=pTb[:, :, :], in_=pTs[:, :, :], func=AF.Copy), R=(B["pTs"],), W=(B["pTb"],))
            if ti + 1 < NT:
                for c in (6, 7):
                    spq.dma(I("dma_start", out=XP[c], in_=xT_d[:, c, t0 + TT:t0 + 2 * TT]), W=XP_b[c])
            wp_t, wp_b = None, None
            for s in range(2):
                wt, wtb = wslot(L0 + 26 + s)
                wv = wt[:, :].rearrange("p (k n) -> p k n", k=8)
                if s == 0:
                    pass
                for j in range(4):
                    oc = s * 4 + j
                    pb, pbb = bank()
                    if oc == 0:
                        mm8_split(pb, pbb, wv, j, wtb)
                    else:
                        pe_group(pe, mm8(pb, wv, j), R=tuple(uT_b) + (wtb,), W=(pbb,))
                    op(act, I("activation", out=sg[:, :], in_=pb[:, :], func=AF.Sigmoid),
                       R=(pbb,), W=(B["rt1"],))
                    wt2, wt2b = wslot(L0 + 28)
                    wv2 = wt2[:, 0:2048].rearrange("p (k n) -> p k n", k=2)
                    pb2, pb2b = bank()
                    pe_group(pe, [I("matmul",
                        pb2[:, :], lhsT=wv2[:, kc, oc * 128:(oc + 1) * 128], rhs=pTb[:, kc, :],
                        start=(kc == 0), stop=(kc == 1)) for kc in range(2)], R=(B["pTb"], wt2b), W=(pb2b,))
                    op(dve, I("tensor_tensor", out=sg[:, :], in0=pb2[:, :], in1=sg[:, :], op=ALU.mult),
                       R=(pb2b, B["rt1"]), W=(B["rt1"],))
                    op(dve, I("tensor_tensor", out=hT[:, oc, :], in0=sg[:, :], in1=hT[:, oc, :],
                                                             op=ALU.add), R=(B["rt1"], hT_b[oc]), W=(hT_b[oc],))
                wdone(L0 + 26 + s) if s == 0 else None
            wdone(L0 + 28)

        def tile_final(ti):
            t0 = ti * TT
            rmsnorm(hTc, hT_b, C_G + 24, outt, None, dst_all_b=B["scrB"])
            out_toks.append(spq.dma(I("dma_start", out=out_d[:, :, t0:t0 + TT], in_=outt),
                                    R=(B["scrB"],), W=()))

        for ti in range(NT):
            tile_body(ti)
            tile_final(ti)
        for tok in out_toks:
            sp.wait(tok)
        with nc.Block() as block:
            @block.sync
            def _(e):
                for f in sp.ops:
                    f(e)

            @block.tensor
            def _(e):
                for f in pe.ops:
                    f(e)

            @block.scalar
            def _(e):
                for f in act.ops:
                    f(e)

            @block.vector
            def _(e):
                for f in dve.ops:
                    f(e)

            @block.gpsimd
            def _(e):
                for f in pool.ops:
                    f(e)
    return nc


def _fm(a, nchunk):
    T = a.shape[0]
    return np.ascontiguousarray(a.reshape(T, nchunk, 128).transpose(2, 1, 0))


def _slot8(G):
    return np.ascontiguousarray(G.reshape(8, 128, 512).transpose(1, 0, 2).reshape(128, 4096))


def host_prepare(inputs, S):
    w_in = np.asarray(inputs["w_in"][0], dtype=np.float32)
    w_o = np.asarray(inputs["w_o"][0], dtype=np.float32)
    w_up = np.asarray(inputs["w_up"][0], dtype=np.float32)
    w_down = np.asarray(inputs["w_down"][0], dtype=np.float32)
    w_g = np.asarray(inputs["w_ple_gate"][0], dtype=np.float32)
    w_p = np.asarray(inputs["w_ple_proj"][0], dtype=np.float32)
    wpack = np.zeros((NSLOT, 128, 4096), np.float32)
    perm = np.arange(64)
    perm = (perm + 32) % 64
    aq = w_in[:, 0:512]
    ak = w_in[:, 512:640]

    def rotcols(G):
        nh = G.shape[1] // 64
        return np.concatenate([G[:, h * 64:(h + 1) * 64][:, perm] for h in range(nh)], axis=1)

    for s in range(2):
        cols = []
        for pr in range(2):
            c = 2 * s + pr
            g = aq[:, c * 128:(c + 1) * 128]
            cols += [g, rotcols(g)]
        wpack[s] = _slot8(np.concatenate(cols, axis=1))
    kA = np.concatenate([ak[:, 0:64], ak[:, 0:64]], axis=1)
    kB = np.concatenate([ak[:, 64:128], ak[:, 64:128]], axis=1)
    wpack[2] = _slot8(np.concatenate([kA, rotcols(kA), kB, rotcols(kB)], axis=1))
    wpack[3] = _slot8(w_in[:, 768:1280])
    wpack[4] = _slot8(w_in[:, 1280:1792])
    wpack[5] = _slot8(w_in[:, 1792:2304])
    wpack[6] = _slot8(w_in[:, 2304:2816])
    g7 = np.zeros((1024, 512), np.float32)
    g7[:, 0:128] = w_in[:, 640:768]
    g7[:, 128:136] = w_in[:, 2816:2824]
    wpack[7] = _slot8(g7)
    for s in range(2):
        wpack[8 + s] = _slot8(w_o[:, s * 512:(s + 1) * 512])
        wpack[26 + s] = _slot8(w_g[:, s * 512:(s + 1) * 512])
    for s in range(8):
        wpack[10 + s] = _slot8(w_up[:, s * 512:(s + 1) * 512])
        wpack[18 + s] = w_down[:, s * 128:(s + 1) * 128].reshape(32, 128, 128).transpose(1, 0, 2).reshape(128, 4096)
    wpack[28, :, 0:2048] = w_p.reshape(2, 128, 1024).transpose(1, 0, 2).reshape(128, 2048)

    cp = np.zeros((128, NCP), np.float32)
    i = np.arange(128)
    ii, jj = i[:, None], i[None, :]
    same = (ii // 64) == (jj // 64)
    cp[:, C_ID:C_ID + 128] = np.eye(128, dtype=np.float32)
    cp[:, C_ONE:C_ONE + 128] = 1.0
    cp[:, C_TRI:C_TRI + 128] = (same & (ii <= jj)).astype(np.float32)
    cp[:, C_BON:C_BON + 128] = same.astype(np.float32)
    cp[:, C_ML:C_ML + 128] = np.where(same & (ii > jj), 0.0, NEG)
    cp[:, C_MU:C_MU + 128] = np.where(same & (jj >= ii), 0.0, NEG)
    cp[:, C_M0] = (i < 64)
    cp[:, C_M1] = (i >= 64)
    cp[:, C_END:C_END + 128] = (jj == (ii // 64) * 64 + 63).astype(np.float32)
    cp[:, C_AP:C_AP + 128] = (jj < ii).astype(np.float32)
    cp[:, C_AC:C_AC + 128] = (jj >= ii).astype(np.float32)
    for n, nm in enumerate(["norm_mix", "norm_mlp", "norm_ple"]):
        cp[:, C_G + 8 * n:C_G + 8 * n + 8] = np.asarray(inputs[nm][0], np.float32).reshape(8, 128).T
    cp[:, C_G + 24:C_G + 32] = np.asarray(inputs["norm_final"], np.float32).reshape(8, 128).T
    cw = np.asarray(inputs["conv_w"][0], np.float32)
    cp[:, C_CW:C_CW + 48] = cw.reshape(4, 12, 128).transpose(2, 1, 0).reshape(128, 48)
    cp[:, C_DNN] = np.asarray(inputs["dn_norm"][0], np.float32)
    cp[:, C_ALOG:C_ALOG + 4] = np.asarray(inputs["a_log"][0], np.float32)[None, :]
    cp[:, C_DTB:C_DTB + 4] = np.asarray(inputs["dt_bias"][0], np.float32)[None, :]
    sk = np.asarray(inputs["sinks"][0], np.float32)
    cp[:, C_SNK:C_SNK + 4] = sk.reshape(4, 2)[:, (np.arange(128) // 64)].T

    half = 32
    inv = 1.0 / (10000.0 ** (np.arange(half, dtype=np.float32) * (2.0 / 64)))
    pos = np.arange(S, dtype=np.float32)
    ang = (pos[None, :] * inv[:, None]).astype(np.float32)
    cos = np.cos(ang).astype(np.float32)
    sin = np.sin(ang).astype(np.float32)
    pidx = np.arange(128) % 64
    fidx = pidx % 32
    sign = np.where(pidx < 32, -1.0, 1.0).astype(np.float32)
    rope = np.zeros((128, 4, S), np.float32)
    rope[:, 0, :] = cos[fidx] * 0.125
    rope[:, 1, :] = sin[fidx] * sign[:, None] * 0.125
    rope[:, 2, :] = cos[fidx]
    rope[:, 3, :] = sin[fidx] * sign[:, None]
    return wpack, cp, rope


_CACHE = {}


def run(inputs, S, ncores, dbg=None):
    x = np.asarray(inputs["x"], np.float32)
    p = np.asarray(inputs["p"], np.float32)[0]
    wpack, cp, rope = host_prepare(inputs, S)
    key = (S, tuple(sorted(dbg.items())) if dbg else None)
    nc = build(S, dbg)
    in_maps = []
    for b in range(ncores):
        in_maps.append({"xT": _fm(x[b, :S], 8), "pT": _fm(p[b, :S], 2), "wpack": wpack, "cpack": cp, "rope": rope})
    res = run_bass_kernel_spmd(nc, in_maps, core_ids=list(range(ncores)))
    outs = []
    for b in range(ncores):
        o = res.results[b]["outT"]
        outs.append(o.transpose(2, 1, 0).reshape(S, D))
    return np.stack(outs, axis=0), res


def kernel(**inputs):
    out, _ = run(inputs, 4096, 8)
    return out.astype(np.float32)
```

```python
import os
import numpy as np
from contextlib import ExitStack
import concourse.bass as bass
import concourse.mybir as mybir
from concourse.bass_utils import run_bass_kernel_spmd

F32 = mybir.dt.float32
BF16 = mybir.dt.bfloat16
F32R = mybir.dt.float32r
AF = mybir.ActivationFunctionType
ALU = mybir.AluOpType

D = 1024
EPS = 1e-6
TT = 512
NSLOT = 29
NW = 3
NEG = -30000.0

C_ID, C_ONE, C_TRI, C_BON, C_ML, C_MU, C_AP, C_AC = [i * 128 for i in range(8)]
C_G = 1024
C_CW = C_G + 32
C_DNN = C_CW + 48
C_ALOG = C_DNN + 1
C_DTB = C_ALOG + 4
C_SNK = C_DTB + 4
C_M0 = C_SNK + 4
C_M1 = C_M0 + 1
C_END = C_M1 + 1
NCP = C_END + 128


class Buf:
    __slots__ = ("w", "r", "excl")

    def __init__(self, excl=False):
        self.w = None
        self.r = {}
        self.excl = excl


class EngRec:
    def __init__(self, name):
        self.name = name
        self.ops = []
        self.sem = None
        self.cnt = 0
        self.waited = {}

    def wait(self, tok):
        sem, val = tok
        if self.waited.get(id(sem), 0) >= val:
            return
        self.waited[id(sem)] = val
        self.ops.append(I("wait_ge", sem, val))


def _deps(reads, writes, mysem=None):
    deps = []
    for b in reads:
        if b.w is not None:
            deps.append(b.w)
        if b.excl:
            deps.extend(t for t in b.r.values() if t[0] is not mysem)
    for b in writes:
        if b.w is not None:
            deps.append(b.w)
        deps.extend(b.r.values())
    return deps


def _update(tok, reads, writes):
    for b in writes:
        b.w = tok
        b.r = {}
    for b in reads:
        sem, val = tok
        b.r[id(sem)] = tok


class Ins:
    __slots__ = ("name", "a", "k")

    def __init__(self, name, a, k):
        self.name, self.a, self.k = name, a, k

    def __call__(self, e):
        return getattr(e, self.name)(*self.a, **self.k)

    def cost(self, engname):
        try:
            if self.name == "matmul":
                rhs = self.k["rhs"]
                n = 1
                for d in rhs.shape[1:]:
                    n *= d
                dt = self.k["lhsT"].dtype
                if dt == BF16:
                    return max(0.06, n / 2100.0)
                if dt == F32R:
                    return max(0.213, n / 600.0)
                return max(0.45, n / 300.0)
            if self.name == "transpose":
                return 0.12
            out = self.k.get("out")
            n = 1
            for d in out.shape[1:]:
                n *= d
            if engname == "act":
                return 0.2 + n / 1000.0
            return 0.15 + n / 950.0
        except Exception:
            return 0.3


def I(name, *a, **k):
    return Ins(name, a, k)


class SchedState:
    cur = None
    fin = {}
    LAT = 0.3


def _est(eng, deps, cost):
    start = getattr(eng, "t", 0.0)
    for tok in deps:
        start = max(start, SchedState.fin.get((id(tok[0]), tok[1]), 0.0) + SchedState.LAT)
    fin = start + cost
    eng.t = fin
    if SchedState.cur is not None:
        SchedState.cur.clock = max(SchedState.cur.clock, fin)
    return fin


def op(eng, fn, R=(), W=()):
    deps = _deps(R, W, eng.sem)
    for tok in deps:
        if eng.name == "pe" and tok[0] is eng.sem:
            continue
        eng.wait(tok)
    eng.cnt += 1
    tok = (eng.sem, eng.cnt)
    SchedState.fin[(id(eng.sem), eng.cnt)] = _est(eng, deps, fn.cost(eng.name) if isinstance(fn, Ins) else 0.3)
    sem = eng.sem
    eng.ops.append(lambda e, fn=fn, sem=sem: fn(e).then_inc(sem, 1))
    _update(tok, R, W)
    return tok


def pe_group(eng, fns, R=(), W=()):
    deps = _deps(R, W, eng.sem)
    for tok in deps:
        if tok[0] is eng.sem:
            continue
        eng.wait(tok)
    SchedState.fin[(id(eng.sem), eng.cnt + 1)] = _est(
        eng, deps, sum(fn.cost("pe") if isinstance(fn, Ins) else 0.2 for fn in fns))
    for fn in fns[:-1]:
        eng.ops.append(lambda e, fn=fn: fn(e))
    eng.cnt += 1
    tok = (eng.sem, eng.cnt)
    sem = eng.sem
    last = fns[-1]
    eng.ops.append(lambda e, fn=last, sem=sem: fn(e).then_inc(sem, 1))
    _update(tok, R, W)
    return tok


class DmaQ:
    def __init__(self, eng, sems):
        self.eng = eng
        self.sems = sems
        self.vals = [0] * len(sems)
        self.i = 0

    def dma(self, fn, R=(), W=()):
        k = self.i % len(self.sems)
        self.i += 1
        if self.vals[k] > 0:
            self.eng.wait((self.sems[k], self.vals[k]))
        for tok in _deps(R, W):
            self.eng.wait(tok)
        self.vals[k] += 16
        sem = self.sems[k]
        tok = (sem, self.vals[k])
        self.eng.ops.append(lambda e, fn=fn, sem=sem: fn(e).then_inc(sem, 16))
        _update(tok, R, W)
        return tok


def build(S, dbg=None):
    NT = S // TT
    SchedState.fin = {}
    SchedState.cur = None
    PH = int(os.environ.get('KPH', '99'))
    SUB = int(os.environ.get('KSUB', '99'))
    SKEW = int(os.environ.get('KSKEW', '10'))
    nc = bass.Bass("TRN2", target_bir_lowering=False)
    xT_d = nc.dram_tensor("xT", [128, 8, S], F32, kind="ExternalInput").ap()
    pT_d = nc.dram_tensor("pT", [128, 2, S], F32, kind="ExternalInput").ap()
    wp_d = nc.dram_tensor("wpack", [NSLOT, 128, 4096], F32, kind="ExternalInput").ap()
    cp_d = nc.dram_tensor("cpack", [128, NCP], F32, kind="ExternalInput").ap()
    rope_d = nc.dram_tensor("rope", [128, 4, S], F32, kind="ExternalInput").ap()
    out_d = nc.dram_tensor("outT", [128, 8, S], F32, kind="ExternalOutput").ap()
    dbg_d = {}
    if dbg:
        for nm, shp in dbg.items():
            dbg_d[nm] = nc.dram_tensor("dbg_" + nm, list(shp), F32, kind="ExternalOutput").ap()

    es = ExitStack()
    with es:
        def sb(name, shape, dt):
            return es.enter_context(nc.sbuf_tensor(name, list(shape), dt))

        def sem(name):
            return es.enter_context(nc.semaphore(name))

        pe, act, dve, pool, sp = (EngRec(n) for n in ("pe", "act", "dve", "pool", "sp"))
        for e in (pe, act, dve, pool, sp):
            e.sem = sem("s_" + e.name)
        spq = DmaQ(sp, [sem("spd%d" % i) for i in range(12)])
        plq = DmaQ(pool, [sem("pld%d" % i) for i in range(NW)])

        cpk = sb("cpk", [128, NCP], F32)
        onesb = sb("onesb", [128, 128], BF16)
        am = sb("am", [128, 2, 4, 128], BF16)
        mL4 = sb("mL4", [128, 4, 128], F32)
        mU4 = sb("mU4", [128, 4, 128], F32)
        small = sb("small", [128, 64], F32)
        hT = sb("hT", [128, 8, TT], F32)
        uT = sb("uT", [128, 8, TT], BF16)
        rstd = sb("rstd", [128, TT], F32)
        ropeT = sb("ropeT", [128, 4, TT], F32)
        wb = [sb("wb%d" % i, [128, 4096], BF16) for i in range(NW)]
        qT = sb("qT", [128, 4, TT], BF16)
        kT = sb("kT", [128, 2, TT + 128], BF16)
        vtok = sb("vtok", [128, 5, 128], BF16)
        Eb = [sb("Eb%d" % i, [128, 4, 128], BF16) for i in range(2)]
        PT = [sb("PT%d" % i, [128, 4, 128], BF16) for i in range(4)]
        aoT = sb("aoT", [128, 4, TT], BF16)
        rt1 = sb("rt1", [128, TT], F32)
        rt2 = sb("rt2", [128, TT], F32)
        scrA = sb("scrA", [128, 12 * (TT + 3)], F32R)
        scrB = sb("scrB", [128, 16 * TT], F32)
        zs = sb("zs", [128, 4, TT], BF16)
        dnT = sb("dnT", [128, 4, TT], BF16)
        hist = sb("hist", [128, 12, 3], F32)
        ba_sb = sb("ba_sb", [128, 4, 8], F32)
        gtok = sb("gtok", [128, 64], F32)
        gpre = sb("gpre", [128, 4, 12], F32)
        gb4 = sb("gb4", [128, 4, 128], F32)
        tmpM = sb("tmpM", [128, 4, 128], F32)
        Dst = sb("Dst", [128, 4, 128], F32)
        DTi = sb("DTi", [128, 4, 128], F32)
        egbc = sb("egbc", [128, 4, 128], F32)
        dl = sb("dl", [128, 4, 2], F32)
        Am = [sb("Am%d" % i, [128, 4, 128], F32R) for i in range(2)]
        Bm = [sb("Bm%d" % i, [128, 4, 128], F32R) for i in range(2)]
        Pm = [sb("Pm%d" % i, [128, 4, 128], F32R) for i in range(2)]
        u_sb = sb("u_sb", [128, 4, 128], F32)
        wT = sb("wT", [128, 4, 128], F32R)
        aT = sb("aT", [128, 4, 128], F32R)
        qgT = sb("qgT", [128, 4, 128], F32R)
        kd_t = [sb("kd_t%d" % i, [128, 4, 128], F32R) for i in range(2)]
        kbg_t = sb("kbg_t", [128, 4, 128], F32R)
        vb_t = sb("vb_t", [128, 4, 128], F32R)
        vnew = sb("vnew", [128, 4, 128], F32R)
        Sst = [sb("Sst%d" % i, [128, 4, 128], F32R) for i in range(2)]
        pTs = sb("pTs", [128, 2, TT], F32)
        pTb = sb("pTb", [128, 2, TT], BF16)
        sg = rt1

        ps = [es.enter_context(nc.psum_tensor("ps%d" % i, [128, 512], F32)) for i in range(8)]
        ps_b = [Buf(excl=True) for _ in range(8)]
        rot = [0]

        pinned = set()

        def bank(pin=False):
            for _ in range(12):
                i = rot[0] % 6
                rot[0] += 1
                if i not in pinned:
                    break
            else:
                raise RuntimeError("no free PSUM bank")
            if pin:
                pinned.add(i)
            return ps[i], ps_b[i]

        def unpin(pbb):
            pinned.discard(ps_b.index(pbb))

        pcw = scrA[:, :].rearrange("p (c t) -> p c t", c=12)
        pc = scrA[:, :].bitcast(F32).rearrange("p (c t) -> p c t", c=12)
        qkn = scrA[:, 0:8 * TT].rearrange("p (c t) -> p c t", c=8)
        outt = scrB[:, 0:8 * TT].rearrange("p (c t) -> p c t", c=8)
        y = scrB[:, 0:12 * TT].rearrange("p (c t) -> p c t", c=12)
        oTs = scrB[:, 12 * TT:16 * TT].rearrange("p (c t) -> p c t", c=4)
        hid = scrB[:, :].bitcast(BF16).rearrange("p (c t) -> p c t", c=32)

        def c_(off, n=128):
            return cpk[:, off:off + n]

        ident = c_(C_ID)
        onesf = c_(C_ONE)
        tri = c_(C_TRI)
        bones = c_(C_BON)
        endsel = c_(C_END)
        esink = small[:, 0:4]
        negA = small[:, 4:8]
        epsc = small[:, 8:9]
        eps128 = small[:, 9:10]
        onec = small[:, 10:11]
        dtb = c_(C_DTB, 4)
        dtb4 = small[:, 16:32].rearrange("p (b h) -> p b h", b=4)
        negA4 = small[:, 32:48].rearrange("p (b h) -> p b h", b=4)

        B = {}
        for nm in ("gpre", "cpk", "const", "hT", "uT", "rstd", "rope", "qT", "kT", "vtok", "Eb", "aoT", "rt1", "rt2",
                   "scrA", "scrB", "zs", "dnT", "hist", "ba", "gtok", "gb4", "tmpM", "Dst", "DTi", "egbc", "dl",
                   "u_sb", "wT", "aT", "qgT", "kd_t", "kbg_t", "vb_t", "vnew", "pTs", "pTb", "sg", "oTs", "outt"):
            B[nm] = Buf()
        BS = [{nm: Buf() for nm in ("gb4", "tmpM", "gtok", "egbc", "dl", "Dst", "DTi", "qgT", "Bm0", "Bm1", "Am0", "Am1",
                                    "Pm0", "Pm1", "aT", "kd_t", "kbg_t", "vb_t", "u_sb", "wT", "vnew", "S0", "S1", "oTs")}
              for _ in range(2)]
        hT_b = [Buf() for _ in range(8)]
        hTc = [hT[:, c, :] for c in range(8)]
        _xpn = ("gb4", "tmpM", "Dst", "DTi", "egbc", "u_sb")
        XP = [t[:, :, :].rearrange("p h t -> p (h t)") for t in (gb4, tmpM, Dst, DTi, egbc, u_sb)]
        XP += [pTs[:, 0, :], pTs[:, 1, :]]
        XP_b = [(BS[0][nm], BS[1][nm]) for nm in _xpn] + [(B["pTs"],), (B["pTs"],)]
        pc_b = [Buf() for _ in range(12)]
        y_b = [Buf() for _ in range(12)]
        qkn_b = [Buf() for _ in range(8)]
        uT_b = [Buf() for _ in range(8)]
        PT_b = [Buf() for _ in range(4)]
        Eb_b = [Buf(), Buf()]
        Am_b = [Buf(), Buf()]
        Bm_b = [Buf(), Buf()]
        Pm_b = [Buf(), Buf()]
        S_b = [Buf(), Buf()]
        wb_b = [Buf() for _ in range(NW)]

        wstate = {"next": 0}
        total_loads = NT * NSLOT

        def issue_load(L):
            slot = L % NSLOT
            i = L % NW
            if slot == 7:
                src = wp_d[slot].rearrange("p (k n) -> p k n", k=8)[:, :, 0:136]
                dst = wb[i][:, :].rearrange("p (k n) -> p k n", k=8)[:, :, 0:136]
            elif slot == 28:
                src = wp_d[slot][:, 0:2048]
                dst = wb[i][:, 0:2048]
            else:
                src = wp_d[slot]
                dst = wb[i][:, :]
            plq.dma(I("dma_start", out=dst, in_=src), R=(), W=(wb_b[i],))

        def wslot(L):
            while wstate["next"] <= min(L, total_loads - 1):
                issue_load(wstate["next"])
                wstate["next"] += 1
            return wb[L % NW], wb_b[L % NW]

        def wdone(L):
            while wstate["next"] < total_loads and wstate["next"] <= L + NW:
                issue_load(wstate["next"])
                wstate["next"] += 1

        spq.dma(I("dma_start", out=cpk[:, :], in_=cp_d), W=(B["cpk"],))
        CK = (B["cpk"],)
        op(dve, I("tensor_copy", out=onesb[:, :], in_=onesf), R=CK, W=(B["const"],))
        op(dve, I("memset", am[:, :, :, :], 0.0), W=(B["const"],))
        for a_i in range(2):
            for j in range(4):
                if j == 0 and a_i == 0:
                    continue
                src = c_(C_AP) if j % 2 == 0 else c_(C_AC)
                op(dve, I("tensor_copy", out=am[:, a_i, j, :], in_=src),
                   R=CK, W=(B["const"],))
        for j in range(4):
            op(dve, I("tensor_copy", out=mL4[:, j, :], in_=c_(C_ML)), R=CK, W=(B["const"],))
            op(dve, I("tensor_copy", out=mU4[:, j, :], in_=c_(C_MU)), R=CK, W=(B["const"],))
        op(dve, I("memset", small[:, 8:9], EPS), W=(B["const"],))
        op(dve, I("memset", small[:, 9:10], 128.0 * EPS), W=(B["const"],))
        op(dve, I("memset", small[:, 10:11], 1.0), W=(B["const"],))
        op(act, I("activation", out=small[:, 0:4], in_=c_(C_SNK, 4), func=AF.Exp), R=CK, W=(B["const"],))
        op(act, I("activation", out=small[:, 4:8], in_=c_(C_ALOG, 4), func=AF.Exp), R=CK, W=(B["const"],))
        op(dve, I("tensor_scalar", out=small[:, 4:8], in0=small[:, 4:8], scalar1=-1.0, scalar2=None,
                                          op0=ALU.mult), R=CK, W=(B["const"],))
        op(dve, I("tensor_scalar", out=Sst[0][:, :, :], in0=mL4[:, :, :], scalar1=0.0, scalar2=None, op0=ALU.mult),
           R=(B["const"],), W=(BS[0]["S0"], BS[1]["S0"]))
        for bb in range(4):
            op(dve, I("tensor_copy", out=small[:, 16 + 4 * bb:20 + 4 * bb], in_=c_(C_DTB, 4)), R=CK, W=(B["const"],))
            op(dve, I("tensor_copy", out=small[:, 32 + 4 * bb:36 + 4 * bb], in_=small[:, 4:8]), R=(B["const"],), W=(B["const"],))
        op(dve, I("tensor_scalar", out=vnew[:, :, :], in0=mL4[:, :, :], scalar1=0.0, scalar2=None, op0=ALU.mult),
           R=(B["const"],), W=(BS[0]["vnew"], BS[1]["vnew"]))
        op(dve, I("memset", hist[:, :, :], 0.0), W=(B["hist"],))
        op(dve, I("memset", kT[:, :, :], 0.0), W=(B["kT"],))
        op(dve, I("memset", vtok[:, :, :], 0.0), W=(B["vtok"],))
        CC = (B["cpk"], B["const"])

        def _tb(x):
            return x if isinstance(x, tuple) else (x,)

        def rmsnorm(srcs, src_b, gcol, dst, dst_b, dst_all_b=None):
            for c in range(8):
                op(act, I("activation", out=uT[:, c, :], in_=srcs[c], func=AF.Square), R=_tb(src_b[c]), W=(uT_b[c],))
            pb, pbb = bank()
            for c in range(8):
                pe_group(pe, [I("matmul", pb[:, :], lhsT=onesb[:, :], rhs=uT[:, c, :], start=(c == 0), stop=(c == 7))],
                         R=(uT_b[c],) + CC, W=(pbb,))
            op(act, I("activation", out=rstd[:, :], in_=pb[:, :], func=AF.Ln, bias=epsc, scale=1.0 / D),
               R=(pbb,) + CC, W=(B["rstd"],))
            op(act, I("activation", out=rstd[:, :], in_=rstd[:, :], func=AF.Exp, scale=-0.5),
               R=(B["rstd"],), W=(B["rstd"],))
            for c in range(8):
                wbufs = (dst_b[c],) if dst_all_b is None else (dst_all_b,)
                op(dve, I("scalar_tensor_tensor",
                    out=dst[:, c, :], in0=srcs[c], scalar=cpk[:, gcol + c:gcol + c + 1], in1=rstd[:, :],
                    op0=ALU.mult, op1=ALU.mult), R=_tb(src_b[c]) + (B["rstd"],) + CC, W=wbufs)

        def mm8(pb, wv, j, src=None, n0=0, n1=TT):
            return [I("matmul", pb[:, 0:n1 - n0], lhsT=wv[:, kc, j * 128:(j + 1) * 128],
                                              rhs=uT[:, kc, n0:n1], start=(kc == 0), stop=(kc == 7))
                    for kc in range(8)]

        def mm8_split(pb, pbb, wv, j, wtb):
            fns = mm8(pb, wv, j)
            for kc in range(8):
                pe_group(pe, [fns[kc]], R=(uT_b[kc], wtb), W=(pbb,))

        def dbg_dump(name, ap_src, bufs, t0=None):
            if name not in dbg_d:
                return
            d = dbg_d[name]
            plq.dma(I("dma_start", out=d, in_=ap_src), R=bufs, W=())

        out_toks = []

        def tile_body(ti):
            t0 = ti * TT
            L0 = ti * NSLOT
            NPF = 8 if ti > 0 else 0
            for c in range(NPF, 8):
                spq.dma(I("dma_start", out=hT[:, c, :], in_=xT_d[:, c, t0:t0 + TT]), W=(hT_b[c],))
            spq.dma(I("dma_start", out=ropeT[:, :, :], in_=rope_d[:, :, t0:t0 + TT]), W=(B["rope"],))
            if ti > 0:
                op(dve, I("tensor_copy", out=kT[:, :, 0:128], in_=kT[:, :, TT:TT + 128]),
                   R=(B["kT"],), W=(B["kT"],))
                op(dve, I("tensor_copy", out=vtok[:, 0, :], in_=vtok[:, 4, :]), R=(B["vtok"],), W=(B["vtok"],))
            rmsnorm([XP[c] if c < NPF else hTc[c] for c in range(8)],
                    [XP_b[c] if c < NPF else hT_b[c] for c in range(8)], C_G + 0, uT, uT_b)
            for c in range(NPF):
                op(act, I("activation", out=hT[:, c, :], in_=XP[c], func=AF.Copy), R=XP_b[c], W=(hT_b[c],))
            spq.dma(I("dma_start", out=pTs[:, :, :], in_=pT_d[:, :, t0:t0 + TT]), W=(B["pTs"],))

            if PH < 2:
                return
            for s in range(3):
                wt, wtb = wslot(L0 + s)
                wv = wt[:, :].rearrange("p (k n) -> p k n", k=8)
                for pr in range(2):
                    pX, pXb = bank()
                    pR, pRb = bank()
                    if s == 0 and pr == 0:
                        mm8_split(pX, pXb, wv, 0, wtb)
                    else:
                        pe_group(pe, mm8(pX, wv, 2 * pr), R=tuple(uT_b) + (wtb,), W=(pXb,))
                    pe_group(pe, mm8(pR, wv, 2 * pr + 1), R=tuple(uT_b) + (wtb,), W=(pRb,))
                    if s < 2:
                        ci = 0
                        dst = qT[:, 2 * s + pr, :]
                        dstb = B["qT"]
                    else:
                        ci = 2
                        dst = kT[:, pr, 128:128 + TT]
                        dstb = B["kT"]
                    op(dve, I("tensor_tensor", out=rt1[:, :], in0=pX[:, :], in1=ropeT[:, ci, :],
                                                                    op=ALU.mult),
                       R=(pXb, B["rope"]), W=(B["rt1"],))
                    op(dve, I("tensor_tensor", out=rt2[:, :], in0=pR[:, :],
                                                                    in1=ropeT[:, ci + 1, :], op=ALU.mult),
                       R=(pRb, B["rope"]), W=(B["rt2"],))
                    op(dve, I("tensor_tensor", out=dst, in0=rt1[:, :], in1=rt2[:, :], op=ALU.add),
                       R=(B["rt1"], B["rt2"]), W=(dstb,))
                wdone(L0 + s)

            if PH < 3:
                return
            op(dve, I("tensor_copy", out=pcw[:, :, 0:3], in_=hist[:, :, :]), R=(B["hist"],),
               W=(B["scrA"],) + tuple(pc_b) + tuple(qkn_b))

            def conv_grp(grp):
                for ch in range(grp * 4, grp * 4 + 4):
                    cw0 = C_CW + ch * 4
                    op(act, I("activation", out=y[:, ch, :], in_=pc[:, ch, 0:TT], func=AF.Copy,
                              scale=cpk[:, cw0:cw0 + 1]), R=(pc_b[ch],) + CC, W=(y_b[ch], B["scrB"]))
                    for k in range(1, 4):
                        op(dve, I("scalar_tensor_tensor", out=y[:, ch, :], in0=pc[:, ch, k:k + TT],
                                  scalar=cpk[:, cw0 + k:cw0 + k + 1], in1=y[:, ch, :], op0=ALU.mult, op1=ALU.add),
                           R=(pc_b[ch], y_b[ch]) + CC, W=(y_b[ch],))
                op(dve, I("tensor_copy", out=hist[:, grp * 4:grp * 4 + 4, :], in_=pc[:, grp * 4:grp * 4 + 4, TT:TT + 3]),
                   R=tuple(pc_b[grp * 4:grp * 4 + 4]), W=(B["hist"],))

            def silu_grp(grp):
                for ch in range(grp * 4, grp * 4 + 4):
                    op(act, I("activation", out=y[:, ch, :], in_=y[:, ch, :], func=AF.Silu), R=(y_b[ch],), W=(y_b[ch],))

            def l2norm_grp(half):
                pbs = []
                for j in range(4):
                    ch = half * 4 + j
                    sq = PT[j][:, :, :].rearrange("p a q -> p (a q)")
                    op(act, I("activation", out=sq, in_=y[:, ch, :], func=AF.Square), R=(y_b[ch],), W=(PT_b[j],))
                    pb, pbb = bank(pin=True)
                    pbs.append((pb, pbb))
                    pe_group(pe, [I("matmul", pb[:, :], lhsT=onesb[:, :], rhs=sq, start=True, stop=True)],
                             R=(PT_b[j],) + CC, W=(pbb,))
                for j in range(4):
                    pb, pbb = pbs[j]
                    if half == 0:
                        op(act, I("activation", out=pb[:, :], in_=pb[:, :], func=AF.Ln, bias=eps128, scale=128.0),
                           R=(pbb,) + CC, W=(pbb,))
                    else:
                        op(act, I("activation", out=pb[:, :], in_=pb[:, :], func=AF.Ln, bias=epsc, scale=1.0),
                           R=(pbb,) + CC, W=(pbb,))
                for j in range(4):
                    pb, pbb = pbs[j]
                    op(act, I("activation", out=pb[:, :], in_=pb[:, :], func=AF.Exp, scale=-0.5), R=(pbb,), W=(pbb,))
                for j in range(4):
                    ch = half * 4 + j
                    pb, pbb = pbs[j]
                    wl = (qkn_b[ch], pc_b[ch]) + ((pc_b[ch - 1],) if ch > 0 else ())
                    op(dve, I("tensor_tensor", out=qkn[:, ch, :], in0=y[:, ch, :], in1=pb[:, :], op=ALU.mult),
                       R=(y_b[ch], pbb), W=wl)
                    unpin(pbb)

            for s in range(3, 7):
                wt, wtb = wslot(L0 + s)
                wv = wt[:, :].rearrange("p (k n) -> p k n", k=8)
                for j in range(4):
                    pb, pbb = bank()
                    pe_group(pe, mm8(pb, wv, j), R=tuple(uT_b) + (wtb,), W=(pbb,))
                    if s < 6:
                        ch = (s - 3) * 4 + j
                        op(act, I("activation", out=pcw[:, ch, 3:3 + TT], in_=pb[:, :], func=AF.Copy),
                           R=(pbb,), W=(pc_b[ch],))
                    else:
                        op(act, I("activation", out=zs[:, j, :], in_=pb[:, :], func=AF.Silu),
                           R=(pbb,), W=(B["zs"],))
                wdone(L0 + s)
                if s < 6:
                    conv_grp(s - 3)
                if s >= 4:
                    silu_grp(s - 4)
                if s == 5:
                    l2norm_grp(0)
                if s == 6:
                    l2norm_grp(1)
            wt, wtb = wslot(L0 + 7)
            wv = wt[:, :].rearrange("p (k n) -> p k n", k=8)
            for blk in range(4):
                pb, pbb = bank()
                pe_group(pe, [I("matmul",
                    pb[:, 0:128], lhsT=uT[:, kc, blk * 128:(blk + 1) * 128], rhs=wv[:, kc, 0:128],
                    start=(kc == 0), stop=(kc == 7)) for kc in range(8)],
                    R=tuple(uT_b) + (wtb,), W=(pbb,))
                op(act, I("activation", out=vtok[:, 1 + blk, :], in_=pb[:, 0:128], func=AF.Copy),
                   R=(pbb,), W=(B["vtok"],))
            pb, pbb = bank()
            for blk in range(4):
                pe_group(pe, [I("matmul",
                    pb[:, blk * 8:blk * 8 + 8], lhsT=uT[:, kc, blk * 128:(blk + 1) * 128], rhs=wv[:, kc, 128:136],
                    start=(kc == 0), stop=(kc == 7)) for kc in range(8)],
                    R=tuple(uT_b) + (wtb,), W=(pbb,))
            op(act, I("activation", out=ba_sb[:, :, :], in_=pb[:, 0:32].rearrange("p (b n) -> p b n", b=4),
                                                  func=AF.Copy), R=(pbb,), W=(B["ba"],))
            wdone(L0 + 7)
            op(dve, I("tensor_tensor", out=gpre[:, :, 0:4], in0=ba_sb[:, :, 4:8], in1=dtb4, op=ALU.add),
               R=(B["ba"],) + CC, W=(B["gpre"],))
            op(act, I("activation", out=gpre[:, :, 0:4], in_=gpre[:, :, 0:4], func=AF.Exp), R=(B["gpre"],), W=(B["gpre"],))
            op(act, I("activation", out=gpre[:, :, 0:4], in_=gpre[:, :, 0:4], func=AF.Ln, bias=onec, scale=1.0),
               R=(B["gpre"],) + CC, W=(B["gpre"],))
            op(dve, I("tensor_tensor", out=gpre[:, :, 0:4], in0=gpre[:, :, 0:4], in1=negA4, op=ALU.mult),
               R=(B["gpre"],) + CC, W=(B["gpre"],))
            op(act, I("activation", out=gpre[:, :, 4:8], in_=ba_sb[:, :, 0:4], func=AF.Sigmoid), R=(B["ba"],), W=(B["gpre"],))
            op(dve, I("tensor_scalar", out=gpre[:, :, 8:12], in0=gpre[:, :, 4:8], scalar1=-1.0, scalar2=None,
                      op0=ALU.mult), R=(B["gpre"],), W=(B["gpre"],))
            if ti == 0:
                dbg_dump("qT", qT[:, :, :], (B["qT"],))
                dbg_dump("kT", kT[:, :, :], (B["kT"],))
                dbg_dump("vtok", vtok[:, :, :], (B["vtok"],))

            if PH < 4:
                return
            attn_state = {}

            def attn_step(c, bp):
                attn_sc(c, bp)
                attn_pv(c, bp)

            def attn_sc(c, bp):
                kv = c // 2
                scs = [bank(), bank()]
                fns = []
                for bl in range(2):
                    blk = bp * 2 + bl
                    for kb in range(2):
                        k0 = (blk + kb) * 128
                        for hf in range(2):
                            fns.append(I("matmul",
                                         scs[hf][0][:, (bl * 2 + kb) * 128:(bl * 2 + kb + 1) * 128],
                                         lhsT=kT[hf * 64:(hf + 1) * 64, kv, k0:k0 + 128],
                                         rhs=qT[hf * 64:(hf + 1) * 64, c, blk * 128:(blk + 1) * 128],
                                         start=True, stop=True))
                pe_group(pe, fns, R=(B["kT"], B["qT"]), W=(scs[0][1], scs[1][1]))
                ami = 0 if (ti == 0 and bp == 0) else 1
                pis = []
                for hf in range(2):
                    op(act, I("activation", out=Eb[hf][:, :, :],
                              in_=scs[hf][0][:, :].rearrange("p (a q) -> p a q", a=4), func=AF.Exp),
                       R=(scs[hf][1],), W=(Eb_b[hf],))
                    pi = ((c * 2 + bp) % 2) * 2 + hf
                    pis.append(pi)
                    op(dve, I("tensor_tensor", out=PT[pi][:, :, :], in0=Eb[hf][:, :, :],
                              in1=am[:, ami, :, :], op=ALU.mult),
                       R=(Eb_b[hf],) + CC, W=(PT_b[pi],))
                attn_state[(c, bp)] = pis

            def attn_pv(c, bp):
                kv = c // 2
                pis = attn_state[(c, bp)]
                fns = []
                for bl in range(2):
                    blk = bp * 2 + bl
                    for which in range(2):
                        pso = ps[6] if which == 0 else ps[7]
                        for hf in range(2):
                            for kb in range(2):
                                if which == 0:
                                    lt = vtok[:, blk + kb, kv * 64:(kv + 1) * 64]
                                else:
                                    lt = onesb[:, 0:64]
                                fns.append(I("matmul",
                                             pso[hf * 64:(hf + 1) * 64, blk * 128:(blk + 1) * 128], lhsT=lt,
                                             rhs=PT[pis[hf]][:, bl * 2 + kb, :], start=(kb == 0), stop=(kb == 1)))
                pe_group(pe, fns, R=(PT_b[pis[0]], PT_b[pis[1]], B["vtok"]) + CC, W=(ps_b[6], ps_b[7]))

            def attn_fin(c):
                op(dve, I("tensor_scalar", out=rt1[:, :], in0=ps[7][:, :], scalar1=esink[:, c:c + 1],
                                                       scalar2=None, op0=ALU.add),
                   R=(ps_b[7],) + CC, W=(B["rt1"],))
                op(act, I("activation", out=rt1[:, :], in_=rt1[:, :], func=AF.Ln), R=(B["rt1"],), W=(B["rt1"],))
                op(act, I("activation", out=rt1[:, :], in_=rt1[:, :], func=AF.Exp, scale=-1.0), R=(B["rt1"],), W=(B["rt1"],))
                op(dve, I("tensor_tensor", out=aoT[:, c, :], in0=ps[6][:, :], in1=rt1[:, :], op=ALU.mult),
                   R=(ps_b[6], B["rt1"]), W=(B["aoT"],))


            if PH < 6:
                for c in range(4):
                    attn_step(c, 0)
                    attn_step(c, 1)
                    attn_fin(c)
            if PH < 5:
                return
            if PH < 6:
                return
            def dn_stream(blk, g):
                bs = slice(blk * 128, (blk + 1) * 128)
                H = (2 * g, 2 * g + 1)
                hs = slice(2 * g, 2 * g + 2)
                Bg = BS[g]

                def gc(base):
                    return gtok[:, base + 2 * g:base + 2 * g + 2]

                def v2(pb):
                    return pb[:, 0:256].rearrange("p (h t) -> p h t", h=2)

                for h in H:
                    op(dve, I("tensor_scalar", out=gb4[:, h, :], in0=onesf, scalar1=gpre[:, blk, h:h + 1],
                              scalar2=None, op0=ALU.mult), R=(B["gpre"],) + CC, W=(Bg["gb4"],))
                pgb, pgbb = bank(pin=True)
                pe_group(pe, [I("matmul", pgb[:, j * 128:(j + 1) * 128], lhsT=gb4[:, H[j], :], rhs=tri,
                                start=True, stop=True) for j in range(2)], R=(Bg["gb4"],) + CC, W=(pgbb,))
                g2v = v2(pgb)
                yield
                for j, h in enumerate(H):
                    op(dve, I("scalar_tensor_tensor", out=tmpM[:, h, :], in0=g2v[:, j, :], scalar=1.0, in1=ident,
                              op0=ALU.mult, op1=ALU.mult, accum_out=gtok[:, 12 + h:13 + h]),
                       R=(pgbb,) + CC, W=(Bg["tmpM"], Bg["gtok"]))
                    op(dve, I("scalar_tensor_tensor", out=tmpM[:, h, :], in0=g2v[:, j, :], scalar=1.0, in1=endsel,
                              op0=ALU.mult, op1=ALU.mult, accum_out=gtok[:, 44 + h:45 + h]),
                       R=(pgbb,) + CC, W=(Bg["tmpM"], Bg["gtok"]))
                op(act, I("activation", out=egbc[:, hs, :], in_=g2v, func=AF.Exp), R=(pgbb,), W=(Bg["egbc"],))
                op(act, I("activation", out=dl[:, hs, :], in_=g2v[:, :, 63:128:64], func=AF.Exp), R=(pgbb,), W=(Bg["dl"],))
                op(dve, I("tensor_scalar", out=gc(16), in0=gc(12), scalar1=-1.0, scalar2=None, op0=ALU.mult),
                   R=(Bg["gtok"],), W=(Bg["gtok"],))
                op(act, I("activation", out=gc(20), in_=gc(12), func=AF.Exp), R=(Bg["gtok"],), W=(Bg["gtok"],))
                op(dve, I("tensor_tensor", out=gc(32), in0=gc(44), in1=gc(12), op=ALU.subtract),
                   R=(Bg["gtok"],), W=(Bg["gtok"],))
                op(act, I("activation", out=gc(24), in_=gc(32), func=AF.Exp), R=(Bg["gtok"],), W=(Bg["gtok"],))
                op(dve, I("tensor_tensor", out=gc(28), in0=gpre[:, blk, 4 + 2 * g:6 + 2 * g], in1=gc(20), op=ALU.mult),
                   R=(Bg["gtok"], B["gpre"]), W=(Bg["gtok"],))
                op(dve, I("tensor_scalar", out=gc(36), in0=gc(24), scalar1=cpk[:, C_M0:C_M0 + 1], scalar2=None,
                          op0=ALU.mult), R=(Bg["gtok"],) + CC, W=(Bg["gtok"],))
                op(dve, I("tensor_scalar", out=gc(40), in0=gc(24), scalar1=cpk[:, C_M1:C_M1 + 1], scalar2=None,
                          op0=ALU.mult), R=(Bg["gtok"],) + CC, W=(Bg["gtok"],))
                yield
                op(dve, I("scalar_tensor_tensor", out=tmpM[:, hs, :], in0=g2v, scalar=-1.0, in1=mL4[:, hs, :],
                          op0=ALU.mult, op1=ALU.add), R=(pgbb,) + CC, W=(Bg["tmpM"],))
                for h in H:
                    op(act, I("activation", out=Dst[:, h, :], in_=tmpM[:, h, :], func=AF.Exp,
                              bias=gtok[:, 12 + h:13 + h], scale=1.0), R=(Bg["tmpM"], Bg["gtok"]), W=(Bg["Dst"],))
                op(dve, I("tensor_tensor", out=tmpM[:, hs, :], in0=g2v, in1=mU4[:, hs, :], op=ALU.add),
                   R=(pgbb, Bg["Dst"]) + CC, W=(Bg["tmpM"],))
                unpin(pgbb)
                for h in H:
                    op(act, I("activation", out=DTi[:, h, :], in_=tmpM[:, h, :], func=AF.Exp,
                              bias=gtok[:, 16 + h:17 + h], scale=1.0), R=(Bg["tmpM"], Bg["gtok"]), W=(Bg["DTi"],))
                op(dve, I("tensor_tensor", out=qgT[:, hs, :], in0=qkn[:, hs, bs].bitcast(F32), in1=egbc[:, hs, :],
                          op=ALU.mult), R=(qkn_b[H[0]], qkn_b[H[1]], Bg["egbc"]), W=(Bg["qgT"],))
                yield
                pk, pkb = bank(pin=True)
                pe_group(pe, [I("matmul", pk[:, j * 128:(j + 1) * 128], lhsT=qkn[:, 4 + H[j], bs],
                                rhs=qkn[:, 4 + H[j], bs], start=True, stop=True) for j in range(2)] +
                         [I("matmul", pk[:, 256 + j * 128:256 + (j + 1) * 128], lhsT=qkn[:, 4 + H[j], bs],
                            rhs=qkn[:, H[j], bs], start=True, stop=True) for j in range(2)],
                         R=tuple(qkn_b[h] for h in H) + tuple(qkn_b[4 + h] for h in H), W=(pkb,))
                pt, ptb = bank(pin=True)
                pe_group(pe, [I("transpose", pt[:, j * 128:(j + 1) * 128], qkn[:, 4 + H[j], bs].bitcast(F32), ident)
                              for j in range(2)] +
                         [I("transpose", pt[:, 256 + j * 128:256 + (j + 1) * 128], y[:, 8 + H[j], bs], ident)
                          for j in range(2)], R=tuple(qkn_b[4 + h] for h in H) + tuple(y_b[8 + h] for h in H) + CC, W=(ptb,))
                yield
                for j, h in enumerate(H):
                    op(dve, I("scalar_tensor_tensor", out=Bm[0][:, h, :], in0=pk[:, j * 128:(j + 1) * 128],
                              scalar=gpre[:, blk, 8 + h:9 + h], in1=Dst[:, h, :], op0=ALU.mult, op1=ALU.mult),
                       R=(pkb, B["gpre"], Bg["Dst"]), W=(Bg["Bm0"],))
                op(dve, I("tensor_tensor", out=aT[:, hs, :], in0=pk[:, 256:512].rearrange("p (h t) -> p h t", h=2),
                          in1=DTi[:, hs, :], op=ALU.mult), R=(pkb, Bg["DTi"]), W=(Bg["aT"],))
                unpin(pkb)
                pa, pab = bank(pin=True)
                pe_group(pe, [I("transpose", pa[:, j * 128:(j + 1) * 128], Bm[0][:, H[j], :].bitcast(F32), ident)
                              for j in range(2)], R=(Bg["Bm0"],) + CC, W=(pab,))
                for j, h in enumerate(H):
                    for cc2 in range(2):
                        op(act, I("activation", out=kd_t[cc2][:, h, :], in_=pt[:, j * 128:(j + 1) * 128], func=AF.Copy,
                                  scale=gtok[:, 36 + 4 * cc2 + h:37 + 4 * cc2 + h]),
                           R=(ptb, Bg["gtok"]), W=(Bg["kd_t"],))
                    op(dve, I("tensor_scalar", out=kbg_t[:, h, :], in0=pt[:, j * 128:(j + 1) * 128],
                              scalar1=gtok[:, 28 + h:29 + h], scalar2=None, op0=ALU.mult),
                       R=(ptb, Bg["gtok"]), W=(Bg["kbg_t"],))
                    op(dve, I("tensor_scalar", out=vb_t[:, h, :], in0=pt[:, 256 + j * 128:256 + (j + 1) * 128],
                              scalar1=gpre[:, blk, 4 + h:5 + h], scalar2=None, op0=ALU.mult),
                       R=(ptb, B["gpre"]), W=(Bg["vb_t"],))
                unpin(ptb)
                yield
                op(act, I("activation", out=Am[0][:, hs, :], in_=v2(pa), func=AF.Copy), R=(pab,), W=(Bg["Am0"],))
                for j, h in enumerate(H):
                    op(dve, I("tensor_tensor", out=Pm[0][:, h, :], in0=pa[:, j * 128:(j + 1) * 128], in1=ident,
                              op=ALU.add), R=(pab,) + CC, W=(Bg["Pm0"],))
                unpin(pab)
                yield
                for k in range(5):
                    a, b_ = k % 2, (k + 1) % 2
                    AmA, BmA, PmA = Bg["Am%d" % a], Bg["Bm%d" % a], Bg["Pm%d" % a]
                    AmB, BmB, PmB = Bg["Am%d" % b_], Bg["Bm%d" % b_], Bg["Pm%d" % b_]
                    p12, p12b = bank(pin=True)
                    fns = [I("matmul", p12[:, 256 + j * 128:256 + (j + 1) * 128], lhsT=Am[a][:, H[j], :],
                             rhs=Bm[a][:, H[j], :], start=True, stop=True) for j in range(2)]
                    if k < 4:
                        fns = [I("matmul", p12[:, j * 128:(j + 1) * 128], lhsT=Bm[a][:, H[j], :],
                                 rhs=Am[a][:, H[j], :], start=True, stop=True) for j in range(2)] + fns
                    pe_group(pe, fns, R=(AmA, BmA), W=(p12b,))
                    yield
                    if k < 4:
                        op(act, I("activation", out=Am[b_][:, hs, :], in_=v2(p12), func=AF.Copy), R=(p12b,), W=(AmB,))
                    op(dve, I("tensor_copy", out=Bm[b_][:, hs, :],
                              in_=p12[:, 256:512].rearrange("p (h t) -> p h t", h=2)), R=(p12b,), W=(BmB,))
                    unpin(p12b)
                    p3, p3b = bank(pin=True)
                    pe_group(pe, [I("matmul", p3[:, j * 128:(j + 1) * 128], lhsT=Bm[b_][:, H[j], :],
                                    rhs=Pm[a][:, H[j], :], start=True, stop=True) for j in range(2)],
                             R=(BmB, PmA), W=(p3b,))
                    yield
                    op(dve, I("tensor_tensor", out=Pm[b_][:, hs, :], in0=v2(p3), in1=Pm[a][:, hs, :].bitcast(F32),
                              op=ALU.add), R=(p3b, PmA), W=(PmB,))
                    unpin(p3b)
                TTm, TTb = Pm[1], Bg["Pm1"]
                pu, pub = bank(pin=True)
                pe_group(pe, [I("matmul", pu[:, j * 128:(j + 1) * 128], lhsT=TTm[:, H[j], :], rhs=vb_t[:, H[j], :],
                                start=True, stop=True) for j in range(2)] +
                         [I("matmul", pu[:, 256 + j * 128:256 + (j + 1) * 128], lhsT=kbg_t[:, H[j], :],
                            rhs=TTm[:, H[j], :], start=True, stop=True) for j in range(2)],
                         R=(TTb, Bg["vb_t"], Bg["kbg_t"]), W=(pub,))
                yield
                op(act, I("activation", out=u_sb[:, hs, :], in_=v2(pu), func=AF.Copy), R=(pub,), W=(Bg["u_sb"],))
                op(dve, I("tensor_copy", out=wT[:, hs, :], in_=pu[:, 256:512].rearrange("p (h t) -> p h t", h=2)),
                   R=(pub,), W=(Bg["wT"],))
                unpin(pub)
                yield
                po, pob = bank(pin=True)
                for cch in range(2):
                    gch = (ti * 4 + blk) * 2 + cch
                    si, so = gch % 2, (gch + 1) % 2
                    rs = slice(cch * 64, (cch + 1) * 64)
                    pv, pvb = bank(pin=True)
                    pe_group(pe, [I("matmul", pv[:, j * 128:(j + 1) * 128], lhsT=wT[:, H[j], :],
                                    rhs=Sst[si][:, H[j], :], start=True, stop=True) for j in range(2)],
                             R=(Bg["wT"], Bg["S%d" % si]), W=(pvb,))
                    yield
                    op(dve, I("tensor_tensor", out=vnew[rs, hs, :], in0=u_sb[rs, hs, :],
                              in1=pv[rs, 0:256].rearrange("p (h t) -> p h t", h=2), op=ALU.subtract),
                       R=(pvb, Bg["u_sb"]), W=(Bg["vnew"],))
                    unpin(pvb)
                    fns = []
                    for j, h in enumerate(H):
                        fns.append(I("matmul", po[:, j * 128 + cch * 64:j * 128 + cch * 64 + 64],
                                     lhsT=Sst[si][:, h, :], rhs=qgT[:, h, cch * 64:(cch + 1) * 64],
                                     start=True, stop=False))
                        fns.append(I("matmul", po[:, j * 128 + cch * 64:j * 128 + cch * 64 + 64],
                                     lhsT=vnew[:, h, :], rhs=aT[:, h, cch * 64:(cch + 1) * 64],
                                     start=False, stop=True))
                    for j, h in enumerate(H):
                        fns.append(I("matmul", po[:, 256 + j * 128:256 + (j + 1) * 128], lhsT=kd_t[cch][:, h, :],
                                     rhs=vnew[:, h, :], start=True, stop=True))
                    pe_group(pe, fns, R=(Bg["S%d" % si], Bg["qgT"], Bg["vnew"], Bg["aT"], Bg["kd_t"]), W=(pob,))
                    yield
                    for j, h in enumerate(H):
                        op(dve, I("scalar_tensor_tensor", out=Sst[so][:, h, :], in0=Sst[si][:, h, :].bitcast(F32),
                                  scalar=dl[:, h, cch:cch + 1], in1=po[:, 256 + j * 128:256 + (j + 1) * 128],
                                  op0=ALU.mult, op1=ALU.add),
                           R=(pob, Bg["S%d" % si], Bg["dl"]), W=(Bg["S%d" % so],))
                    yield
                op(act, I("activation", out=oTs[:, hs, bs], in_=v2(po), func=AF.Copy), R=(pob,), W=(Bg["oTs"],))
                unpin(pob)

            def attn_stream(blk):
                attn_sc(blk, 0)
                yield
                yield
                attn_pv(blk, 0)
                yield
                attn_sc(blk, 1)
                yield
                yield
                attn_pv(blk, 1)
                yield
                attn_fin(blk)

            def gated(g):
                tmp, tmpb = (rt2, B["rt2"]) if g == 0 else (rstd, B["rstd"])
                pbs = []
                for h in (2 * g, 2 * g + 1):
                    op(act, I("activation", out=uT[:, h, :], in_=oTs[:, h, :], func=AF.Square),
                       R=(BS[g]["oTs"],), W=(uT_b[h],))
                    pb, pbb = bank(pin=True)
                    pbs.append((pb, pbb))
                    pe_group(pe, [I("matmul", pb[:, :], lhsT=onesb[:, :], rhs=uT[:, h, :], start=True, stop=True)],
                             R=(uT_b[h],) + CC, W=(pbb,))
                yield
                for pb, pbb in pbs:
                    op(act, I("activation", out=pb[:, :], in_=pb[:, :], func=AF.Ln, bias=epsc, scale=1.0 / 128.0),
                       R=(pbb,) + CC, W=(pbb,))
                    op(act, I("activation", out=pb[:, :], in_=pb[:, :], func=AF.Exp, scale=-0.5), R=(pbb,), W=(pbb,))
                yield
                for j, h in enumerate((2 * g, 2 * g + 1)):
                    pb, pbb = pbs[j]
                    op(dve, I("scalar_tensor_tensor", out=tmp[:, :], in0=oTs[:, h, :], scalar=cpk[:, C_DNN:C_DNN + 1],
                              in1=pb[:, :], op0=ALU.mult, op1=ALU.mult), R=(BS[g]["oTs"], pbb) + CC, W=(tmpb,))
                    unpin(pbb)
                    op(dve, I("tensor_tensor", out=dnT[:, h, :], in0=tmp[:, :], in1=zs[:, h, :], op=ALU.mult),
                       R=(tmpb, B["zs"]), W=(B["dnT"],))

            def chain(mk, tail=None):
                for blk in range(4):
                    yield from mk(blk)
                if tail is not None:
                    yield from tail

            class St:
                def __init__(self, gen):
                    self.gen = gen
                    self.clock = 0.0

            sts = [St(chain(lambda blk: dn_stream(blk, 0), gated(0))), St(chain(lambda blk: dn_stream(blk, 1), gated(1))),
                   St(chain(attn_stream))]
            base = max(pe.t if hasattr(pe, "t") else 0.0, 0.0)
            for st in sts:
                st.clock = 0.0
            sts[1].clock = float(SKEW) * 0.0
            alive = list(sts)
            while alive:
                st = min(alive, key=lambda x: x.clock)
                SchedState.cur = st
                try:
                    next(st.gen)
                except StopIteration:
                    alive.remove(st)
                SchedState.cur = None

            if PH < 8:
                return
            for s in range(2):
                wt, wtb = wslot(L0 + 8 + s)
                wv = wt[:, :].rearrange("p (k n) -> p k n", k=8)
                for j in range(4):
                    oc = s * 4 + j
                    pb, pbb = bank()
                    fns = []
                    for kc in range(8):
                        src = aoT[:, kc, :] if kc < 4 else dnT[:, kc - 4, :]
                        fns.append(I("matmul",
                            pb[:, :], lhsT=wv[:, kc, j * 128:(j + 1) * 128], rhs=src, start=(kc == 0), stop=(kc == 7)))
                    pe_group(pe, fns, R=(B["aoT"], B["dnT"], wtb), W=(pbb,))
                    op(dve, I("tensor_tensor", out=hT[:, oc, :], in0=pb[:, :], in1=hT[:, oc, :],
                                                                    op=ALU.add), R=(pbb, hT_b[oc]), W=(hT_b[oc],))
                wdone(L0 + 8 + s)

            if PH < 9:
                return
            if ti + 1 < NT:
                for c in range(6):
                    spq.dma(I("dma_start", out=XP[c], in_=xT_d[:, c, t0 + TT:t0 + 2 * TT]), W=XP_b[c])
            rmsnorm(hTc, hT_b, C_G + 8, uT, uT_b)
            for s in range(8):
                wt, wtb = wslot(L0 + 10 + s)
                wv = wt[:, :].rearrange("p (k n) -> p k n", k=8)
                for j in range(4):
                    hc = s * 4 + j
                    pb, pbb = bank()
                    if hc == 0:
                        mm8_split(pb, pbb, wv, j, wtb)
                    else:
                        pe_group(pe, mm8(pb, wv, j), R=tuple(uT_b) + (wtb,), W=(pbb,))
                    op(act, I("activation", out=rt2[:, :], in_=pb[:, :], func=AF.Square),
                       R=(pbb,), W=(B["rt2"],))
                    op(dve, I("scalar_tensor_tensor",
                        out=hid[:, hc, :], in0=pb[:, :], scalar=0.0, in1=rt2[:, :], op0=ALU.is_gt, op1=ALU.mult),
                       R=(pbb, B["rt2"]), W=(B["scrB"],) + tuple(y_b))
                wdone(L0 + 10 + s)
            for oc in range(8):
                wt, wtb = wslot(L0 + 18 + oc)
                wv = wt[:, :].rearrange("p (k n) -> p k n", k=32)
                pb, pbb = bank()
                pe_group(pe, [I("matmul", pb[:, :], lhsT=wv[:, kc, :], rhs=hid[:, kc, :],
                                                                      start=(kc == 0), stop=(kc == 31))
                              for kc in range(32)], R=(B["scrB"], wtb), W=(pbb,))
                op(dve, I("tensor_tensor", out=hT[:, oc, :], in0=pb[:, :], in1=hT[:, oc, :],
                                                                op=ALU.add), R=(pbb, hT_b[oc]), W=(hT_b[oc],))
                wdone(L0 + 18 + oc)

            if PH < 10:
                return
            rmsnorm(hTc, hT_b, C_G + 16, uT, uT_b)
            op(act, I("activation", out=pTb[:, :, :], in_=pTs[:, :, :], func=AF.Copy), R=(B["pTs"],), W=(B["pTb"],))
            if ti + 1 < NT:
                for c in (6, 7):
                    spq.dma(I("dma_start", out=XP[c], in_=xT_d[:, c, t0 + TT:t0 + 2 * TT]), W=XP_b[c])
            wp_t, wp_b = None, None
            for s in range(2):
                wt, wtb = wslot(L0 + 26 + s)
                wv = wt[:, :].rearrange("p (k n) -> p k n", k=8)
                if s == 0:
                    pass
                for j in range(4):
                    oc = s * 4 + j
                    pb, pbb = bank()
                    if oc == 0:
                        mm8_split(pb, pbb, wv, j, wtb)
                    else:
                        pe_group(pe, mm8(pb, wv, j), R=tuple(uT_b) + (wtb,), W=(pbb,))
                    op(act, I("activation", out=sg[:, :], in_=pb[:, :], func=AF.Sigmoid),
                       R=(pbb,), W=(B["rt1"],))
                    wt2, wt2b = wslot(L0 + 28)
                    wv2 = wt2[:, 0:2048].rearrange("p (k n) -> p k n", k=2)
                    pb2, pb2b = bank()
                    pe_group(pe, [I("matmul",
                        pb2[:, :], lhsT=wv2[:, kc, oc * 128:(oc + 1) * 128], rhs=pTb[:, kc, :],
                        start=(kc == 0), stop=(kc == 1)) for kc in range(2)], R=(B["pTb"], wt2b), W=(pb2b,))
                    op(dve, I("tensor_tensor", out=sg[:, :], in0=pb2[:, :], in1=sg[:, :], op=ALU.mult),
                       R=(pb2b, B["rt1"]), W=(B["rt1"],))
                    op(dve, I("tensor_tensor", out=hT[:, oc, :], in0=sg[:, :], in1=hT[:, oc, :],
                                                             op=ALU.add), R=(B["rt1"], hT_b[oc]), W=(hT_b[oc],))
                wdone(L0 + 26 + s) if s == 0 else None
            wdone(L0 + 28)

        def tile_final(ti):
            t0 = ti * TT
            rmsnorm(hTc, hT_b, C_G + 24, outt, None, dst_all_b=B["scrB"])
            out_toks.append(spq.dma(I("dma_start", out=out_d[:, :, t0:t0 + TT], in_=outt),
                                    R=(B["scrB"],), W=()))

        for ti in range(NT):
            tile_body(ti)
            tile_final(ti)
        for tok in out_toks:
            sp.wait(tok)
        with nc.Block() as block:
            @block.sync
            def _(e):
                for f in sp.ops:
                    f(e)

            @block.tensor
            def _(e):
                for f in pe.ops:
                    f(e)

            @block.scalar
            def _(e):
                for f in act.ops:
                    f(e)

            @block.vector
            def _(e):
                for f in dve.ops:
                    f(e)

            @block.gpsimd
            def _(e):
                for f in pool.ops:
                    f(e)
    return nc


def _fm(a, nchunk):
    T = a.shape[0]
    return np.ascontiguousarray(a.reshape(T, nchunk, 128).transpose(2, 1, 0))


def _slot8(G):
    return np.ascontiguousarray(G.reshape(8, 128, 512).transpose(1, 0, 2).reshape(128, 4096))


def host_prepare(inputs, S):
    w_in = np.asarray(inputs["w_in"][0], dtype=np.float32)
    w_o = np.asarray(inputs["w_o"][0], dtype=np.float32)
    w_up = np.asarray(inputs["w_up"][0], dtype=np.float32)
    w_down = np.asarray(inputs["w_down"][0], dtype=np.float32)
    w_g = np.asarray(inputs["w_ple_gate"][0], dtype=np.float32)
    w_p = np.asarray(inputs["w_ple_proj"][0], dtype=np.float32)
    wpack = np.zeros((NSLOT, 128, 4096), np.float32)
    perm = np.arange(64)
    perm = (perm + 32) % 64
    aq = w_in[:, 0:512]
    ak = w_in[:, 512:640]

    def rotcols(G):
        nh = G.shape[1] // 64
        return np.concatenate([G[:, h * 64:(h + 1) * 64][:, perm] for h in range(nh)], axis=1)

    for s in range(2):
        cols = []
        for pr in range(2):
            c = 2 * s + pr
            g = aq[:, c * 128:(c + 1) * 128]
            cols += [g, rotcols(g)]
        wpack[s] = _slot8(np.concatenate(cols, axis=1))
    kA = np.concatenate([ak[:, 0:64], ak[:, 0:64]], axis=1)
    kB = np.concatenate([ak[:, 64:128], ak[:, 64:128]], axis=1)
    wpack[2] = _slot8(np.concatenate([kA, rotcols(kA), kB, rotcols(kB)], axis=1))
    wpack[3] = _slot8(w_in[:, 768:1280])
    wpack[4] = _slot8(w_in[:, 1280:1792])
    wpack[5] = _slot8(w_in[:, 1792:2304])
    wpack[6] = _slot8(w_in[:, 2304:2816])
    g7 = np.zeros((1024, 512), np.float32)
    g7[:, 0:128] = w_in[:, 640:768]
    g7[:, 128:136] = w_in[:, 2816:2824]
    wpack[7] = _slot8(g7)
    for s in range(2):
        wpack[8 + s] = _slot8(w_o[:, s * 512:(s + 1) * 512])
        wpack[26 + s] = _slot8(w_g[:, s * 512:(s + 1) * 512])
    for s in range(8):
        wpack[10 + s] = _slot8(w_up[:, s * 512:(s + 1) * 512])
        wpack[18 + s] = w_down[:, s * 128:(s + 1) * 128].reshape(32, 128, 128).transpose(1, 0, 2).reshape(128, 4096)
    wpack[28, :, 0:2048] = w_p.reshape(2, 128, 1024).transpose(1, 0, 2).reshape(128, 2048)

    cp = np.zeros((128, NCP), np.float32)
    i = np.arange(128)
    ii, jj = i[:, None], i[None, :]
    same = (ii // 64) == (jj // 64)
    cp[:, C_ID:C_ID + 128] = np.eye(128, dtype=np.float32)
    cp[:, C_ONE:C_ONE + 128] = 1.0
    cp[:, C_TRI:C_TRI + 128] = (same & (ii <= jj)).astype(np.float32)
    cp[:, C_BON:C_BON + 128] = same.astype(np.float32)
    cp[:, C_ML:C_ML + 128] = np.where(same & (ii > jj), 0.0, NEG)
    cp[:, C_MU:C_MU + 128] = np.where(same & (jj >= ii), 0.0, NEG)
    cp[:, C_M0] = (i < 64)
    cp[:, C_M1] = (i >= 64)
    cp[:, C_END:C_END + 128] = (jj == (ii // 64) * 64 + 63).astype(np.float32)
    cp[:, C_AP:C_AP + 128] = (jj < ii).astype(np.float32)
    cp[:, C_AC:C_AC + 128] = (jj >= ii).astype(np.float32)
    for n, nm in enumerate(["norm_mix", "norm_mlp", "norm_ple"]):
        cp[:, C_G + 8 * n:C_G + 8 * n + 8] = np.asarray(inputs[nm][0], np.float32).reshape(8, 128).T
    cp[:, C_G + 24:C_G + 32] = np.asarray(inputs["norm_final"], np.float32).reshape(8, 128).T
    cw = np.asarray(inputs["conv_w"][0], np.float32)
    cp[:, C_CW:C_CW + 48] = cw.reshape(4, 12, 128).transpose(2, 1, 0).reshape(128, 48)
    cp[:, C_DNN] = np.asarray(inputs["dn_norm"][0], np.float32)
    cp[:, C_ALOG:C_ALOG + 4] = np.asarray(inputs["a_log"][0], np.float32)[None, :]
    cp[:, C_DTB:C_DTB + 4] = np.asarray(inputs["dt_bias"][0], np.float32)[None, :]
    sk = np.asarray(inputs["sinks"][0], np.float32)
    cp[:, C_SNK:C_SNK + 4] = sk.reshape(4, 2)[:, (np.arange(128) // 64)].T

    half = 32
    inv = 1.0 / (10000.0 ** (np.arange(half, dtype=np.float32) * (2.0 / 64)))
    pos = np.arange(S, dtype=np.float32)
    ang = (pos[None, :] * inv[:, None]).astype(np.float32)
    cos = np.cos(ang).astype(np.float32)
    sin = np.sin(ang).astype(np.float32)
    pidx = np.arange(128) % 64
    fidx = pidx % 32
    sign = np.where(pidx < 32, -1.0, 1.0).astype(np.float32)
    rope = np.zeros((128, 4, S), np.float32)
    rope[:, 0, :] = cos[fidx] * 0.125
    rope[:, 1, :] = sin[fidx] * sign[:, None] * 0.125
    rope[:, 2, :] = cos[fidx]
    rope[:, 3, :] = sin[fidx] * sign[:, None]
    return wpack, cp, rope


_CACHE = {}


def run(inputs, S, ncores, dbg=None):
    x = np.asarray(inputs["x"], np.float32)
    p = np.asarray(inputs["p"], np.float32)[0]
    wpack, cp, rope = host_prepare(inputs, S)
    key = (S, tuple(sorted(dbg.items())) if dbg else None)
    nc = build(S, dbg)
    in_maps = []
    for b in range(ncores):
        in_maps.append({"xT": _fm(x[b, :S], 8), "pT": _fm(p[b, :S], 2), "wpack": wpack, "cpack": cp, "rope": rope})
    res = run_bass_kernel_spmd(nc, in_maps, core_ids=list(range(ncores)))
    outs = []
    for b in range(ncores):
        o = res.results[b]["outT"]
        outs.append(o.transpose(2, 1, 0).reshape(S, D))
    return np.stack(outs, axis=0), res


def kernel(**inputs):
    out, _ = run(inputs, 4096, 8)
    return out.astype(np.float32)
```

```python
import os
import numpy as np
from contextlib import ExitStack
import concourse.bass as bass
import concourse.mybir as mybir
from concourse.bass_utils import run_bass_kernel_spmd

F32 = mybir.dt.float32
BF16 = mybir.dt.bfloat16
F32R = mybir.dt.float32r
AF = mybir.ActivationFunctionType
ALU = mybir.AluOpType

D = 1024
EPS = 1e-6
TT = 512
NSLOT = 29
NW = 3
NEG = -30000.0

C_ID, C_ONE, C_TRI, C_BON, C_ML, C_MU, C_AP, C_AC = [i * 128 for i in range(8)]
C_G = 1024
C_CW = C_G + 32
C_DNN = C_CW + 48
C_ALOG = C_DNN + 1
C_DTB = C_ALOG + 4
C_SNK = C_DTB + 4
C_M0 = C_SNK + 4
C_M1 = C_M0 + 1
C_END = C_M1 + 1
NCP = C_END + 128


class Buf:
    __slots__ = ("w", "r", "excl")

    def __init__(self, excl=False):
        self.w = None
        self.r = {}
        self.excl = excl


class EngRec:
    def __init__(self, name):
        self.name = name
        self.ops = []
        self.sem = None
        self.cnt = 0
        self.waited = {}

    def wait(self, tok):
        sem, val = tok
        if self.waited.get(id(sem), 0) >= val:
            return
        self.waited[id(sem)] = val
        self.ops.append(I("wait_ge", sem, val))


def _deps(reads, writes, mysem=None):
    deps = []
    for b in reads:
        if b.w is not None:
            deps.append(b.w)
        if b.excl:
            deps.extend(t for t in b.r.values() if t[0] is not mysem)
    for b in writes:
        if b.w is not None:
            deps.append(b.w)
        deps.extend(b.r.values())
    return deps


def _update(tok, reads, writes):
    for b in writes:
        b.w = tok
        b.r = {}
    for b in reads:
        sem, val = tok
        b.r[id(sem)] = tok


class Ins:
    __slots__ = ("name", "a", "k")

    def __init__(self, name, a, k):
        self.name, self.a, self.k = name, a, k

    def __call__(self, e):
        return getattr(e, self.name)(*self.a, **self.k)

    def cost(self, engname):
        try:
            if self.name == "matmul":
                rhs = self.k["rhs"]
                n = 1
                for d in rhs.shape[1:]:
                    n *= d
                dt = self.k["lhsT"].dtype
                if dt == BF16:
                    return max(0.06, n / 2100.0)
                if dt == F32R:
                    return max(0.213, n / 600.0)
                return max(0.45, n / 300.0)
            if self.name == "transpose":
                return 0.12
            out = self.k.get("out")
            n = 1
            for d in out.shape[1:]:
                n *= d
            if engname == "act":
                return 0.2 + n / 1000.0
            return 0.15 + n / 950.0
        except Exception:
            return 0.3


def I(name, *a, **k):
    return Ins(name, a, k)


class SchedState:
    cur = None
    fin = {}
    LAT = 0.3


def _est(eng, deps, cost):
    start = getattr(eng, "t", 0.0)
    for tok in deps:
        start = max(start, SchedState.fin.get((id(tok[0]), tok[1]), 0.0) + SchedState.LAT)
    fin = start + cost
    eng.t = fin
    if SchedState.cur is not None:
        SchedState.cur.clock = max(SchedState.cur.clock, fin)
    return fin


def op(eng, fn, R=(), W=()):
    deps = _deps(R, W, eng.sem)
    for tok in deps:
        if eng.name == "pe" and tok[0] is eng.sem:
            continue
        eng.wait(tok)
    eng.cnt += 1
    tok = (eng.sem, eng.cnt)
    SchedState.fin[(id(eng.sem), eng.cnt)] = _est(eng, deps, fn.cost(eng.name) if isinstance(fn, Ins) else 0.3)
    sem = eng.sem
    eng.ops.append(lambda e, fn=fn, sem=sem: fn(e).then_inc(sem, 1))
    _update(tok, R, W)
    return tok


def pe_group(eng, fns, R=(), W=()):
    deps = _deps(R, W, eng.sem)
    for tok in deps:
        if tok[0] is eng.sem:
            continue
        eng.wait(tok)
    SchedState.fin[(id(eng.sem), eng.cnt + 1)] = _est(
        eng, deps, sum(fn.cost("pe") if isinstance(fn, Ins) else 0.2 for fn in fns))
    for fn in fns[:-1]:
        eng.ops.append(lambda e, fn=fn: fn(e))
    eng.cnt += 1
    tok = (eng.sem, eng.cnt)
    sem = eng.sem
    last = fns[-1]
    eng.ops.append(lambda e, fn=last, sem=sem: fn(e).then_inc(sem, 1))
    _update(tok, R, W)
    return tok


class DmaQ:
    def __init__(self, eng, sems):
        self.eng = eng
        self.sems = sems
        self.vals = [0] * len(sems)
        self.i = 0

    def dma(self, fn, R=(), W=()):
        k = self.i % len(self.sems)
        self.i += 1
        if self.vals[k] > 0:
            self.eng.wait((self.sems[k], self.vals[k]))
        for tok in _deps(R, W):
            self.eng.wait(tok)
        self.vals[k] += 16
        sem = self.sems[k]
        tok = (sem, self.vals[k])
        self.eng.ops.append(lambda e, fn=fn, sem=sem: fn(e).then_inc(sem, 16))
        _update(tok, R, W)
        return tok


def build(S, dbg=None):
    NT = S // TT
    SchedState.fin = {}
    SchedState.cur = None
    PH = int(os.environ.get('KPH', '99'))
    SUB = int(os.environ.get('KSUB', '99'))
    SKEW = int(os.environ.get('KSKEW', '10'))
    nc = bass.Bass("TRN2", target_bir_lowering=False)
    xT_d = nc.dram_tensor("xT", [128, 8, S], F32, kind="ExternalInput").ap()
    pT_d = nc.dram_tensor("pT", [128, 2, S], F32, kind="ExternalInput").ap()
    wp_d = nc.dram_tensor("wpack", [NSLOT, 128, 4096], F32, kind="ExternalInput").ap()
    cp_d = nc.dram_tensor("cpack", [128, NCP], F32, kind="ExternalInput").ap()
    rope_d = nc.dram_tensor("rope", [128, 4, S], F32, kind="ExternalInput").ap()
    out_d = nc.dram_tensor("outT", [128, 8, S], F32, kind="ExternalOutput").ap()
    dbg_d = {}
    if dbg:
        for nm, shp in dbg.items():
            dbg_d[nm] = nc.dram_tensor("dbg_" + nm, list(shp), F32, kind="ExternalOutput").ap()

    es = ExitStack()
    with es:
        def sb(name, shape, dt):
            return es.enter_context(nc.sbuf_tensor(name, list(shape), dt))

        def sem(name):
            return es.enter_context(nc.semaphore(name))

        pe, act, dve, pool, sp = (EngRec(n) for n in ("pe", "act", "dve", "pool", "sp"))
        for e in (pe, act, dve, pool, sp):
            e.sem = sem("s_" + e.name)
        spq = DmaQ(sp, [sem("spd%d" % i) for i in range(12)])
        plq = DmaQ(pool, [sem("pld%d" % i) for i in range(NW)])

        cpk = sb("cpk", [128, NCP], F32)
        onesb = sb("onesb", [128, 128], BF16)
        am = sb("am", [128, 2, 4, 128], BF16)
        mL4 = sb("mL4", [128, 4, 128], F32)
        mU4 = sb("mU4", [128, 4, 128], F32)
        small = sb("small", [128, 64], F32)
        hT = sb("hT", [128, 8, TT], F32)
        uT = sb("uT", [128, 8, TT], BF16)
        rstd = sb("rstd", [128, TT], F32)
        ropeT = sb("ropeT", [128, 4, TT], F32)
        wb = [sb("wb%d" % i, [128, 4096], BF16) for i in range(NW)]
        qT = sb("qT", [128, 4, TT], BF16)
        kT = sb("kT", [128, 2, TT + 128], BF16)
        vtok = sb("vtok", [128, 5, 128], BF16)
        Eb = [sb("Eb%d" % i, [128, 4, 128], BF16) for i in range(2)]
        PT = [sb("PT%d" % i, [128, 4, 128], BF16) for i in range(4)]
        aoT = sb("aoT", [128, 4, TT], BF16)
        rt1 = sb("rt1", [128, TT], F32)
        rt2 = sb("rt2", [128, TT], F32)
        scrA = sb("scrA", [128, 12 * (TT + 3)], F32R)
        scrB = sb("scrB", [128, 16 * TT], F32)
        zs = sb("zs", [128, 4, TT], BF16)
        dnT = sb("dnT", [128, 4, TT], BF16)
        hist = sb("hist", [128, 12, 3], F32)
        ba_sb = sb("ba_sb", [128, 4, 8], F32)
        gtok = sb("gtok", [128, 64], F32)
        gpre = sb("gpre", [128, 4, 12], F32)
        gb4 = sb("gb4", [128, 4, 128], F32)
        tmpM = sb("tmpM", [128, 4, 128], F32)
        Dst = sb("Dst", [128, 4, 128], F32)
        DTi = sb("DTi", [128, 4, 128], F32)
        egbc = sb("egbc", [128, 4, 128], F32)
        dl = sb("dl", [128, 4, 2], F32)
        Am = [sb("Am%d" % i, [128, 4, 128], F32R) for i in range(2)]
        Bm = [sb("Bm%d" % i, [128, 4, 128], F32R) for i in range(2)]
        Pm = [sb("Pm%d" % i, [128, 4, 128], F32R) for i in range(2)]
        u_sb = sb("u_sb", [128, 4, 128], F32)
        wT = sb("wT", [128, 4, 128], F32R)
        aT = sb("aT", [128, 4, 128], F32R)
        qgT = sb("qgT", [128, 4, 128], F32R)
        kd_t = [sb("kd_t%d" % i, [128, 4, 128], F32R) for i in range(2)]
        kbg_t = sb("kbg_t", [128, 4, 128], F32R)
        vb_t = sb("vb_t", [128, 4, 128], F32R)
        vnew = sb("vnew", [128, 4, 128], F32R)
        Sst = [sb("Sst%d" % i, [128, 4, 128], F32R) for i in range(2)]
        pTs = sb("pTs", [128, 2, TT], F32)
        pTb = sb("pTb", [128, 2, TT], BF16)
        sg = rt1

        ps = [es.enter_context(nc.psum_tensor("ps%d" % i, [128, 512], F32)) for i in range(8)]
        ps_b = [Buf(excl=True) for _ in range(8)]
        rot = [0]

        pinned = set()

        def bank(pin=False):
            for _ in range(12):
                i = rot[0] % 6
                rot[0] += 1
                if i not in pinned:
                    break
            else:
                raise RuntimeError("no free PSUM bank")
            if pin:
                pinned.add(i)
            return ps[i], ps_b[i]

        def unpin(pbb):
            pinned.discard(ps_b.index(pbb))

        pcw = scrA[:, :].rearrange("p (c t) -> p c t", c=12)
        pc = scrA[:, :].bitcast(F32).rearrange("p (c t) -> p c t", c=12)
        qkn = scrA[:, 0:8 * TT].rearrange("p (c t) -> p c t", c=8)
        outt = scrB[:, 0:8 * TT].rearrange("p (c t) -> p c t", c=8)
        y = scrB[:, 0:12 * TT].rearrange("p (c t) -> p c t", c=12)
        oTs = scrB[:, 12 * TT:16 * TT].rearrange("p (c t) -> p c t", c=4)
        hid = scrB[:, :].bitcast(BF16).rearrange("p (c t) -> p c t", c=32)

        def c_(off, n=128):
            return cpk[:, off:off + n]

        ident = c_(C_ID)
        onesf = c_(C_ONE)
        tri = c_(C_TRI)
        bones = c_(C_BON)
        endsel = c_(C_END)
        esink = small[:, 0:4]
        negA = small[:, 4:8]
        epsc = small[:, 8:9]
        eps128 = small[:, 9:10]
        onec = small[:, 10:11]
        dtb = c_(C_DTB, 4)
        dtb4 = small[:, 16:32].rearrange("p (b h) -> p b h", b=4)
        negA4 = small[:, 32:48].rearrange("p (b h) -> p b h", b=4)

        B = {}
        for nm in ("gpre", "cpk", "const", "hT", "uT", "rstd", "rope", "qT", "kT", "vtok", "Eb", "aoT", "rt1", "rt2",
                   "scrA", "scrB", "zs", "dnT", "hist", "ba", "gtok", "gb4", "tmpM", "Dst", "DTi", "egbc", "dl",
                   "u_sb", "wT", "aT", "qgT", "kd_t", "kbg_t", "vb_t", "vnew", "pTs", "pTb", "sg", "oTs", "outt"):
            B[nm] = Buf()
        BS = [{nm: Buf() for nm in ("gb4", "tmpM", "gtok", "egbc", "dl", "Dst", "DTi", "qgT", "Bm0", "Bm1", "Am0", "Am1",
                                    "Pm0", "Pm1", "aT", "kd_t", "kbg_t", "vb_t", "u_sb", "wT", "vnew", "S0", "S1", "oTs")}
              for _ in range(2)]
        hT_b = [Buf() for _ in range(8)]
        hTc = [hT[:, c, :] for c in range(8)]
        _xpn = ("gb4", "tmpM", "Dst", "DTi", "egbc", "u_sb")
        XP = [t[:, :, :].rearrange("p h t -> p (h t)") for t in (gb4, tmpM, Dst, DTi, egbc, u_sb)]
        XP += [pTs[:, 0, :], pTs[:, 1, :]]
        XP_b = [(BS[0][nm], BS[1][nm]) for nm in _xpn] + [(B["pTs"],), (B["pTs"],)]
        pc_b = [Buf() for _ in range(12)]
        y_b = [Buf() for _ in range(12)]
        qkn_b = [Buf() for _ in range(8)]
        uT_b = [Buf() for _ in range(8)]
        PT_b = [Buf() for _ in range(4)]
        Eb_b = [Buf(), Buf()]
        Am_b = [Buf(), Buf()]
        Bm_b = [Buf(), Buf()]
        Pm_b = [Buf(), Buf()]
        S_b = [Buf(), Buf()]
        wb_b = [Buf() for _ in range(NW)]

        wstate = {"next": 0}
        total_loads = NT * NSLOT

        def issue_load(L):
            slot = L % NSLOT
            i = L % NW
            if slot == 7:
                src = wp_d[slot].rearrange("p (k n) -> p k n", k=8)[:, :, 0:136]
                dst = wb[i][:, :].rearrange("p (k n) -> p k n", k=8)[:, :, 0:136]
            elif slot == 28:
                src = wp_d[slot][:, 0:2048]
                dst = wb[i][:, 0:2048]
            else:
                src = wp_d[slot]
                dst = wb[i][:, :]
            plq.dma(I("dma_start", out=dst, in_=src), R=(), W=(wb_b[i],))

        def wslot(L):
            while wstate["next"] <= min(L, total_loads - 1):
                issue_load(wstate["next"])
                wstate["next"] += 1
            return wb[L % NW], wb_b[L % NW]

        def wdone(L):
            while wstate["next"] < total_loads and wstate["next"] <= L + NW:
                issue_load(wstate["next"])
                wstate["next"] += 1

        spq.dma(I("dma_start", out=cpk[:, :], in_=cp_d), W=(B["cpk"],))
        CK = (B["cpk"],)
        op(dve, I("tensor_copy", out=onesb[:, :], in_=onesf), R=CK, W=(B["const"],))
        op(dve, I("memset", am[:, :, :, :], 0.0), W=(B["const"],))
        for a_i in range(2):
            for j in range(4):
                if j == 0 and a_i == 0:
                    continue
                src = c_(C_AP) if j % 2 == 0 else c_(C_AC)
                op(dve, I("tensor_copy", out=am[:, a_i, j, :], in_=src),
                   R=CK, W=(B["const"],))
        for j in range(4):
            op(dve, I("tensor_copy", out=mL4[:, j, :], in_=c_(C_ML)), R=CK, W=(B["const"],))
            op(dve, I("tensor_copy", out=mU4[:, j, :], in_=c_(C_MU)), R=CK, W=(B["const"],))
        op(dve, I("memset", small[:, 8:9], EPS), W=(B["const"],))
        op(dve, I("memset", small[:, 9:10], 128.0 * EPS), W=(B["const"],))
        op(dve, I("memset", small[:, 10:11], 1.0), W=(B["const"],))
        op(act, I("activation", out=small[:, 0:4], in_=c_(C_SNK, 4), func=AF.Exp), R=CK, W=(B["const"],))
        op(act, I("activation", out=small[:, 4:8], in_=c_(C_ALOG, 4), func=AF.Exp), R=CK, W=(B["const"],))
        op(dve, I("tensor_scalar", out=small[:, 4:8], in0=small[:, 4:8], scalar1=-1.0, scalar2=None,
                                          op0=ALU.mult), R=CK, W=(B["const"],))
        op(dve, I("tensor_scalar", out=Sst[0][:, :, :], in0=mL4[:, :, :], scalar1=0.0, scalar2=None, op0=ALU.mult),
           R=(B["const"],), W=(BS[0]["S0"], BS[1]["S0"]))
        for bb in range(4):
            op(dve, I("tensor_copy", out=small[:, 16 + 4 * bb:20 + 4 * bb], in_=c_(C_DTB, 4)), R=CK, W=(B["const"],))
            op(dve, I("tensor_copy", out=small[:, 32 + 4 * bb:36 + 4 * bb], in_=small[:, 4:8]), R=(B["const"],), W=(B["const"],))
        op(dve, I("tensor_scalar", out=vnew[:, :, :], in0=mL4[:, :, :], scalar1=0.0, scalar2=None, op0=ALU.mult),
           R=(B["const"],), W=(BS[0]["vnew"], BS[1]["vnew"]))
        op(dve, I("memset", hist[:, :, :], 0.0), W=(B["hist"],))
        op(dve, I("memset", kT[:, :, :], 0.0), W=(B["kT"],))
        op(dve, I("memset", vtok[:, :, :], 0.0), W=(B["vtok"],))
        CC = (B["cpk"], B["const"])

        def _tb(x):
            return x if isinstance(x, tuple) else (x,)

        def rmsnorm(srcs, src_b, gcol, dst, dst_b, dst_all_b=None):
            for c in range(8):
                op(act, I("activation", out=uT[:, c, :], in_=srcs[c], func=AF.Square), R=_tb(src_b[c]), W=(uT_b[c],))
            pb, pbb = bank()
            for c in range(8):
                pe_group(pe, [I("matmul", pb[:, :], lhsT=onesb[:, :], rhs=uT[:, c, :], start=(c == 0), stop=(c == 7))],
                         R=(uT_b[c],) + CC, W=(pbb,))
            op(act, I("activation", out=rstd[:, :], in_=pb[:, :], func=AF.Ln, bias=epsc, scale=1.0 / D),
               R=(pbb,) + CC, W=(B["rstd"],))
            op(act, I("activation", out=rstd[:, :], in_=rstd[:, :], func=AF.Exp, scale=-0.5),
               R=(B["rstd"],), W=(B["rstd"],))
            for c in range(8):
                wbufs = (dst_b[c],) if dst_all_b is None else (dst_all_b,)
                op(dve, I("scalar_tensor_tensor",
                    out=dst[:, c, :], in0=srcs[c], scalar=cpk[:, gcol + c:gcol + c + 1], in1=rstd[:, :],
                    op0=ALU.mult, op1=ALU.mult), R=_tb(src_b[c]) + (B["rstd"],) + CC, W=wbufs)

        def mm8(pb, wv, j, src=None, n0=0, n1=TT):
            return [I("matmul", pb[:, 0:n1 - n0], lhsT=wv[:, kc, j * 128:(j + 1) * 128],
                                              rhs=uT[:, kc, n0:n1], start=(kc == 0), stop=(kc == 7))
                    for kc in range(8)]

        def mm8_split(pb, pbb, wv, j, wtb):
            fns = mm8(pb, wv, j)
            for kc in range(8):
                pe_group(pe, [fns[kc]], R=(uT_b[kc], wtb), W=(pbb,))

        def dbg_dump(name, ap_src, bufs, t0=None):
            if name not in dbg_d:
                return
            d = dbg_d[name]
            plq.dma(I("dma_start", out=d, in_=ap_src), R=bufs, W=())

        out_toks = []

        def tile_body(ti):
            t0 = ti * TT
            L0 = ti * NSLOT
            NPF = 8 if ti > 0 else 0
            for c in range(NPF, 8):
                spq.dma(I("dma_start", out=hT[:, c, :], in_=xT_d[:, c, t0:t0 + TT]), W=(hT_b[c],))
            spq.dma(I("dma_start", out=ropeT[:, :, :], in_=rope_d[:, :, t0:t0 + TT]), W=(B["rope"],))
            if ti > 0:
                op(dve, I("tensor_copy", out=kT[:, :, 0:128], in_=kT[:, :, TT:TT + 128]),
                   R=(B["kT"],), W=(B["kT"],))
                op(dve, I("tensor_copy", out=vtok[:, 0, :], in_=vtok[:, 4, :]), R=(B["vtok"],), W=(B["vtok"],))
            rmsnorm([XP[c] if c < NPF else hTc[c] for c in range(8)],
                    [XP_b[c] if c < NPF else hT_b[c] for c in range(8)], C_G + 0, uT, uT_b)
            for c in range(NPF):
                op(act, I("activation", out=hT[:, c, :], in_=XP[c], func=AF.Copy), R=XP_b[c], W=(hT_b[c],))
            spq.dma(I("dma_start", out=pTs[:, :, :], in_=pT_d[:, :, t0:t0 + TT]), W=(B["pTs"],))

            if PH < 2:
                return
            for s in range(3):
                wt, wtb = wslot(L0 + s)
                wv = wt[:, :].rearrange("p (k n) -> p k n", k=8)
                for pr in range(2):
                    pX, pXb = bank()
                    pR, pRb = bank()
                    if s == 0 and pr == 0:
                        mm8_split(pX, pXb, wv, 0, wtb)
                    else:
                        pe_group(pe, mm8(pX, wv, 2 * pr), R=tuple(uT_b) + (wtb,), W=(pXb,))
                    pe_group(pe, mm8(pR, wv, 2 * pr + 1), R=tuple(uT_b) + (wtb,), W=(pRb,))
                    if s < 2:
                        ci = 0
                        dst = qT[:, 2 * s + pr, :]
                        dstb = B["qT"]
                    else:
                        ci = 2
                        dst = kT[:, pr, 128:128 + TT]
                        dstb = B["kT"]
                    op(dve, I("tensor_tensor", out=rt1[:, :], in0=pX[:, :], in1=ropeT[:, ci, :],
                                                                    op=ALU.mult),
                       R=(pXb, B["rope"]), W=(B["rt1"],))
                    op(dve, I("tensor_tensor", out=rt2[:, :], in0=pR[:, :],
                                                                    in1=ropeT[:, ci + 1, :], op=ALU.mult),
                       R=(pRb, B["rope"]), W=(B["rt2"],))
                    op(dve, I("tensor_tensor", out=dst, in0=rt1[:, :], in1=rt2[:, :], op=ALU.add),
                       R=(B["rt1"], B["rt2"]), W=(dstb,))
                wdone(L0 + s)

            if PH < 3:
                return
            op(dve, I("tensor_copy", out=pcw[:, :, 0:3], in_=hist[:, :, :]), R=(B["hist"],),
               W=(B["scrA"],) + tuple(pc_b) + tuple(qkn_b))

            def conv_grp(grp):
                for ch in range(grp * 4, grp * 4 + 4):
                    cw0 = C_CW + ch * 4
                    op(act, I("activation", out=y[:, ch, :], in_=pc[:, ch, 0:TT], func=AF.Copy,
                              scale=cpk[:, cw0:cw0 + 1]), R=(pc_b[ch],) + CC, W=(y_b[ch], B["scrB"]))
                for k in range(1, 4):
                    for ch in range(grp * 4, grp * 4 + 4):
                        cw0 = C_CW + ch * 4
                        op(dve, I("scalar_tensor_tensor", out=y[:, ch, :], in0=pc[:, ch, k:k + TT],
                                  scalar=cpk[:, cw0 + k:cw0 + k + 1], in1=y[:, ch, :], op0=ALU.mult, op1=ALU.add),
                           R=(pc_b[ch], y_b[ch]) + CC, W=(y_b[ch],))
                op(dve, I("tensor_copy", out=hist[:, grp * 4:grp * 4 + 4, :], in_=pc[:, grp * 4:grp * 4 + 4, TT:TT + 3]),
                   R=tuple(pc_b[grp * 4:grp * 4 + 4]), W=(B["hist"],))

            def silu_grp(grp):
                for ch in range(grp * 4, grp * 4 + 4):
                    op(act, I("activation", out=y[:, ch, :], in_=y[:, ch, :], func=AF.Silu), R=(y_b[ch],), W=(y_b[ch],))

            def l2norm_grp(half):
                pbs = []
                for j in range(4):
                    ch = half * 4 + j
                    sq = PT[j][:, :, :].rearrange("p a q -> p (a q)")
                    op(act, I("activation", out=sq, in_=y[:, ch, :], func=AF.Square), R=(y_b[ch],), W=(PT_b[j],))
                    pb, pbb = bank(pin=True)
                    pbs.append((pb, pbb))
                    pe_group(pe, [I("matmul", pb[:, :], lhsT=onesb[:, :], rhs=sq, start=True, stop=True)],
                             R=(PT_b[j],) + CC, W=(pbb,))
                for j in range(4):
                    pb, pbb = pbs[j]
                    if half == 0:
                        op(act, I("activation", out=pb[:, :], in_=pb[:, :], func=AF.Ln, bias=eps128, scale=128.0),
                           R=(pbb,) + CC, W=(pbb,))
                    else:
                        op(act, I("activation", out=pb[:, :], in_=pb[:, :], func=AF.Ln, bias=epsc, scale=1.0),
                           R=(pbb,) + CC, W=(pbb,))
                for j in range(4):
                    pb, pbb = pbs[j]
                    op(act, I("activation", out=pb[:, :], in_=pb[:, :], func=AF.Exp, scale=-0.5), R=(pbb,), W=(pbb,))
                for j in range(4):
                    ch = half * 4 + j
                    pb, pbb = pbs[j]
                    wl = (qkn_b[ch], pc_b[ch]) + ((pc_b[ch - 1],) if ch > 0 else ())
                    op(dve, I("tensor_tensor", out=qkn[:, ch, :], in0=y[:, ch, :], in1=pb[:, :], op=ALU.mult),
                       R=(y_b[ch], pbb), W=wl)
                    unpin(pbb)

            for s in range(3, 7):
                wt, wtb = wslot(L0 + s)
                wv = wt[:, :].rearrange("p (k n) -> p k n", k=8)
                for j in range(4):
                    pb, pbb = bank()
                    pe_group(pe, mm8(pb, wv, j), R=tuple(uT_b) + (wtb,), W=(pbb,))
                    if s < 6:
                        ch = (s - 3) * 4 + j
                        op(act, I("activation", out=pcw[:, ch, 3:3 + TT], in_=pb[:, :], func=AF.Copy),
                           R=(pbb,), W=(pc_b[ch],))
                    else:
                        op(act, I("activation", out=zs[:, j, :], in_=pb[:, :], func=AF.Silu),
                           R=(pbb,), W=(B["zs"],))
                wdone(L0 + s)
                if s < 6:
                    conv_grp(s - 3)
                if s >= 4:
                    silu_grp(s - 4)
                if s == 5:
                    l2norm_grp(0)
                if s == 6:
                    l2norm_grp(1)
            wt, wtb = wslot(L0 + 7)
            wv = wt[:, :].rearrange("p (k n) -> p k n", k=8)
            for blk in range(4):
                pb, pbb = bank()
                pe_group(pe, [I("matmul",
                    pb[:, 0:128], lhsT=uT[:, kc, blk * 128:(blk + 1) * 128], rhs=wv[:, kc, 0:128],
                    start=(kc == 0), stop=(kc == 7)) for kc in range(8)],
                    R=tuple(uT_b) + (wtb,), W=(pbb,))
                op(act, I("activation", out=vtok[:, 1 + blk, :], in_=pb[:, 0:128], func=AF.Copy),
                   R=(pbb,), W=(B["vtok"],))
            pb, pbb = bank()
            for blk in range(4):
                pe_group(pe, [I("matmul",
                    pb[:, blk * 8:blk * 8 + 8], lhsT=uT[:, kc, blk * 128:(blk + 1) * 128], rhs=wv[:, kc, 128:136],
                    start=(kc == 0), stop=(kc == 7)) for kc in range(8)],
                    R=tuple(uT_b) + (wtb,), W=(pbb,))
            op(act, I("activation", out=ba_sb[:, :, :], in_=pb[:, 0:32].rearrange("p (b n) -> p b n", b=4),
                                                  func=AF.Copy), R=(pbb,), W=(B["ba"],))
            wdone(L0 + 7)
            op(dve, I("tensor_tensor", out=gpre[:, :, 0:4], in0=ba_sb[:, :, 4:8], in1=dtb4, op=ALU.add),
               R=(B["ba"],) + CC, W=(B["gpre"],))
            op(act, I("activation", out=gpre[:, :, 0:4], in_=gpre[:, :, 0:4], func=AF.Exp), R=(B["gpre"],), W=(B["gpre"],))
            op(act, I("activation", out=gpre[:, :, 0:4], in_=gpre[:, :, 0:4], func=AF.Ln, bias=onec, scale=1.0),
               R=(B["gpre"],) + CC, W=(B["gpre"],))
            op(dve, I("tensor_tensor", out=gpre[:, :, 0:4], in0=gpre[:, :, 0:4], in1=negA4, op=ALU.mult),
               R=(B["gpre"],) + CC, W=(B["gpre"],))
            op(act, I("activation", out=gpre[:, :, 4:8], in_=ba_sb[:, :, 0:4], func=AF.Sigmoid), R=(B["ba"],), W=(B["gpre"],))
            op(dve, I("tensor_scalar", out=gpre[:, :, 8:12], in0=gpre[:, :, 4:8], scalar1=-1.0, scalar2=None,
                      op0=ALU.mult), R=(B["gpre"],), W=(B["gpre"],))
            if ti == 0:
                dbg_dump("qT", qT[:, :, :], (B["qT"],))
                dbg_dump("kT", kT[:, :, :], (B["kT"],))
                dbg_dump("vtok", vtok[:, :, :], (B["vtok"],))

            if PH < 4:
                return
            attn_state = {}

            def attn_step(c, bp):
                attn_sc(c, bp)
                attn_pv(c, bp)

            def attn_sc(c, bp):
                kv = c // 2
                scs = [bank(), bank()]
                fns = []
                for bl in range(2):
                    blk = bp * 2 + bl
                    for kb in range(2):
                        k0 = (blk + kb) * 128
                        for hf in range(2):
                            fns.append(I("matmul",
                                         scs[hf][0][:, (bl * 2 + kb) * 128:(bl * 2 + kb + 1) * 128],
                                         lhsT=kT[hf * 64:(hf + 1) * 64, kv, k0:k0 + 128],
                                         rhs=qT[hf * 64:(hf + 1) * 64, c, blk * 128:(blk + 1) * 128],
                                         start=True, stop=True))
                pe_group(pe, fns, R=(B["kT"], B["qT"]), W=(scs[0][1], scs[1][1]))
                ami = 0 if (ti == 0 and bp == 0) else 1
                pis = []
                for hf in range(2):
                    op(act, I("activation", out=Eb[hf][:, :, :],
                              in_=scs[hf][0][:, :].rearrange("p (a q) -> p a q", a=4), func=AF.Exp),
                       R=(scs[hf][1],), W=(Eb_b[hf],))
                    pi = ((c * 2 + bp) % 2) * 2 + hf
                    pis.append(pi)
                    op(dve, I("tensor_tensor", out=PT[pi][:, :, :], in0=Eb[hf][:, :, :],
                              in1=am[:, ami, :, :], op=ALU.mult),
                       R=(Eb_b[hf],) + CC, W=(PT_b[pi],))
                attn_state[(c, bp)] = pis

            def attn_pv(c, bp):
                kv = c // 2
                pis = attn_state[(c, bp)]
                fns = []
                for bl in range(2):
                    blk = bp * 2 + bl
                    for which in range(2):
                        pso = ps[6] if which == 0 else ps[7]
                        for hf in range(2):
                            for kb in range(2):
                                if which == 0:
                                    lt = vtok[:, blk + kb, kv * 64:(kv + 1) * 64]
                                else:
                                    lt = onesb[:, 0:64]
                                fns.append(I("matmul",
                                             pso[hf * 64:(hf + 1) * 64, blk * 128:(blk + 1) * 128], lhsT=lt,
                                             rhs=PT[pis[hf]][:, bl * 2 + kb, :], start=(kb == 0), stop=(kb == 1)))
                pe_group(pe, fns, R=(PT_b[pis[0]], PT_b[pis[1]], B["vtok"]) + CC, W=(ps_b[6], ps_b[7]))

            def attn_fin(c):
                op(dve, I("tensor_scalar", out=rt1[:, :], in0=ps[7][:, :], scalar1=esink[:, c:c + 1],
                                                       scalar2=None, op0=ALU.add),
                   R=(ps_b[7],) + CC, W=(B["rt1"],))
                op(act, I("activation", out=rt1[:, :], in_=rt1[:, :], func=AF.Ln), R=(B["rt1"],), W=(B["rt1"],))
                op(act, I("activation", out=rt1[:, :], in_=rt1[:, :], func=AF.Exp, scale=-1.0), R=(B["rt1"],), W=(B["rt1"],))
                op(dve, I("tensor_tensor", out=aoT[:, c, :], in0=ps[6][:, :], in1=rt1[:, :], op=ALU.mult),
                   R=(ps_b[6], B["rt1"]), W=(B["aoT"],))


            if PH < 6:
                for c in range(4):
                    attn_step(c, 0)
                    attn_step(c, 1)
                    attn_fin(c)
            if PH < 5:
                return
            if PH < 6:
                return
            def dn_stream(blk, g):
                bs = slice(blk * 128, (blk + 1) * 128)
                H = (2 * g, 2 * g + 1)
                hs = slice(2 * g, 2 * g + 2)
                Bg = BS[g]

                def gc(base):
                    return gtok[:, base + 2 * g:base + 2 * g + 2]

                def v2(pb):
                    return pb[:, 0:256].rearrange("p (h t) -> p h t", h=2)

                for h in H:
                    op(dve, I("tensor_scalar", out=gb4[:, h, :], in0=onesf, scalar1=gpre[:, blk, h:h + 1],
                              scalar2=None, op0=ALU.mult), R=(B["gpre"],) + CC, W=(Bg["gb4"],))
                pgb, pgbb = bank(pin=True)
                pe_group(pe, [I("matmul", pgb[:, j * 128:(j + 1) * 128], lhsT=gb4[:, H[j], :], rhs=tri,
                                start=True, stop=True) for j in range(2)], R=(Bg["gb4"],) + CC, W=(pgbb,))
                g2v = v2(pgb)
                yield
                for j, h in enumerate(H):
                    op(dve, I("scalar_tensor_tensor", out=tmpM[:, h, :], in0=g2v[:, j, :], scalar=1.0, in1=ident,
                              op0=ALU.mult, op1=ALU.mult, accum_out=gtok[:, 12 + h:13 + h]),
                       R=(pgbb,) + CC, W=(Bg["tmpM"], Bg["gtok"]))
                    op(dve, I("scalar_tensor_tensor", out=tmpM[:, h, :], in0=g2v[:, j, :], scalar=1.0, in1=endsel,
                              op0=ALU.mult, op1=ALU.mult, accum_out=gtok[:, 44 + h:45 + h]),
                       R=(pgbb,) + CC, W=(Bg["tmpM"], Bg["gtok"]))
                op(act, I("activation", out=egbc[:, hs, :], in_=g2v, func=AF.Exp), R=(pgbb,), W=(Bg["egbc"],))
                op(act, I("activation", out=dl[:, hs, :], in_=g2v[:, :, 63:128:64], func=AF.Exp), R=(pgbb,), W=(Bg["dl"],))
                op(dve, I("tensor_scalar", out=gc(16), in0=gc(12), scalar1=-1.0, scalar2=None, op0=ALU.mult),
                   R=(Bg["gtok"],), W=(Bg["gtok"],))
                op(act, I("activation", out=gc(20), in_=gc(12), func=AF.Exp), R=(Bg["gtok"],), W=(Bg["gtok"],))
                op(dve, I("tensor_tensor", out=gc(32), in0=gc(44), in1=gc(12), op=ALU.subtract),
                   R=(Bg["gtok"],), W=(Bg["gtok"],))
                op(act, I("activation", out=gc(24), in_=gc(32), func=AF.Exp), R=(Bg["gtok"],), W=(Bg["gtok"],))
                op(dve, I("tensor_tensor", out=gc(28), in0=gpre[:, blk, 4 + 2 * g:6 + 2 * g], in1=gc(20), op=ALU.mult),
                   R=(Bg["gtok"], B["gpre"]), W=(Bg["gtok"],))
                op(dve, I("tensor_scalar", out=gc(36), in0=gc(24), scalar1=cpk[:, C_M0:C_M0 + 1], scalar2=None,
                          op0=ALU.mult), R=(Bg["gtok"],) + CC, W=(Bg["gtok"],))
                op(dve, I("tensor_scalar", out=gc(40), in0=gc(24), scalar1=cpk[:, C_M1:C_M1 + 1], scalar2=None,
                          op0=ALU.mult), R=(Bg["gtok"],) + CC, W=(Bg["gtok"],))
                yield
                op(dve, I("scalar_tensor_tensor", out=tmpM[:, hs, :], in0=g2v, scalar=-1.0, in1=mL4[:, hs, :],
                          op0=ALU.mult, op1=ALU.add), R=(pgbb,) + CC, W=(Bg["tmpM"],))
                for h in H:
                    op(act, I("activation", out=Dst[:, h, :], in_=tmpM[:, h, :], func=AF.Exp,
                              bias=gtok[:, 12 + h:13 + h], scale=1.0), R=(Bg["tmpM"], Bg["gtok"]), W=(Bg["Dst"],))
                op(dve, I("tensor_tensor", out=tmpM[:, hs, :], in0=g2v, in1=mU4[:, hs, :], op=ALU.add),
                   R=(pgbb, Bg["Dst"]) + CC, W=(Bg["tmpM"],))
                unpin(pgbb)
                for h in H:
                    op(act, I("activation", out=DTi[:, h, :], in_=tmpM[:, h, :], func=AF.Exp,
                              bias=gtok[:, 16 + h:17 + h], scale=1.0), R=(Bg["tmpM"], Bg["gtok"]), W=(Bg["DTi"],))
                op(dve, I("tensor_tensor", out=qgT[:, hs, :], in0=qkn[:, hs, bs].bitcast(F32), in1=egbc[:, hs, :],
                          op=ALU.mult), R=(qkn_b[H[0]], qkn_b[H[1]], Bg["egbc"]), W=(Bg["qgT"],))
                yield
                pk, pkb = bank(pin=True)
                pe_group(pe, [I("matmul", pk[:, j * 128:(j + 1) * 128], lhsT=qkn[:, 4 + H[j], bs],
                                rhs=qkn[:, 4 + H[j], bs], start=True, stop=True) for j in range(2)] +
                         [I("matmul", pk[:, 256 + j * 128:256 + (j + 1) * 128], lhsT=qkn[:, 4 + H[j], bs],
                            rhs=qkn[:, H[j], bs], start=True, stop=True) for j in range(2)],
                         R=tuple(qkn_b[h] for h in H) + tuple(qkn_b[4 + h] for h in H), W=(pkb,))
                pt, ptb = bank(pin=True)
                pe_group(pe, [I("transpose", pt[:, j * 128:(j + 1) * 128], qkn[:, 4 + H[j], bs].bitcast(F32), ident)
                              for j in range(2)] +
                         [I("transpose", pt[:, 256 + j * 128:256 + (j + 1) * 128], y[:, 8 + H[j], bs], ident)
                          for j in range(2)], R=tuple(qkn_b[4 + h] for h in H) + tuple(y_b[8 + h] for h in H) + CC, W=(ptb,))
                yield
                for j, h in enumerate(H):
                    op(dve, I("scalar_tensor_tensor", out=Bm[0][:, h, :], in0=pk[:, j * 128:(j + 1) * 128],
                              scalar=gpre[:, blk, 8 + h:9 + h], in1=Dst[:, h, :], op0=ALU.mult, op1=ALU.mult),
                       R=(pkb, B["gpre"], Bg["Dst"]), W=(Bg["Bm0"],))
                op(dve, I("tensor_tensor", out=aT[:, hs, :], in0=pk[:, 256:512].rearrange("p (h t) -> p h t", h=2),
                          in1=DTi[:, hs, :], op=ALU.mult), R=(pkb, Bg["DTi"]), W=(Bg["aT"],))
                unpin(pkb)
                pa, pab = bank(pin=True)
                pe_group(pe, [I("transpose", pa[:, j * 128:(j + 1) * 128], Bm[0][:, H[j], :].bitcast(F32), ident)
                              for j in range(2)], R=(Bg["Bm0"],) + CC, W=(pab,))
                for j, h in enumerate(H):
                    for cc2 in range(2):
                        op(dve, I("tensor_scalar", out=kd_t[cc2][:, h, :], in0=pt[:, j * 128:(j + 1) * 128],
                                  scalar1=gtok[:, 36 + 4 * cc2 + h:37 + 4 * cc2 + h], scalar2=None, op0=ALU.mult),
                           R=(ptb, Bg["gtok"]), W=(Bg["kd_t"],))
                    op(dve, I("tensor_scalar", out=kbg_t[:, h, :], in0=pt[:, j * 128:(j + 1) * 128],
                              scalar1=gtok[:, 28 + h:29 + h], scalar2=None, op0=ALU.mult),
                       R=(ptb, Bg["gtok"]), W=(Bg["kbg_t"],))
                    op(dve, I("tensor_scalar", out=vb_t[:, h, :], in0=pt[:, 256 + j * 128:256 + (j + 1) * 128],
                              scalar1=gpre[:, blk, 4 + h:5 + h], scalar2=None, op0=ALU.mult),
                       R=(ptb, B["gpre"]), W=(Bg["vb_t"],))
                unpin(ptb)
                yield
                op(act, I("activation", out=Am[0][:, hs, :], in_=v2(pa), func=AF.Copy), R=(pab,), W=(Bg["Am0"],))
                for j, h in enumerate(H):
                    op(dve, I("tensor_tensor", out=Pm[0][:, h, :], in0=pa[:, j * 128:(j + 1) * 128], in1=ident,
                              op=ALU.add), R=(pab,) + CC, W=(Bg["Pm0"],))
                unpin(pab)
                yield
                for k in range(5):
                    a, b_ = k % 2, (k + 1) % 2
                    AmA, BmA, PmA = Bg["Am%d" % a], Bg["Bm%d" % a], Bg["Pm%d" % a]
                    AmB, BmB, PmB = Bg["Am%d" % b_], Bg["Bm%d" % b_], Bg["Pm%d" % b_]
                    p12, p12b = bank(pin=True)
                    fns = [I("matmul", p12[:, 256 + j * 128:256 + (j + 1) * 128], lhsT=Am[a][:, H[j], :],
                             rhs=Bm[a][:, H[j], :], start=True, stop=True) for j in range(2)]
                    if k < 4:
                        fns = [I("matmul", p12[:, j * 128:(j + 1) * 128], lhsT=Bm[a][:, H[j], :],
                                 rhs=Am[a][:, H[j], :], start=True, stop=True) for j in range(2)] + fns
                    pe_group(pe, fns, R=(AmA, BmA), W=(p12b,))
                    yield
                    if k < 4:
                        op(act, I("activation", out=Am[b_][:, hs, :], in_=v2(p12), func=AF.Copy), R=(p12b,), W=(AmB,))
                    op(dve, I("tensor_copy", out=Bm[b_][:, hs, :],
                              in_=p12[:, 256:512].rearrange("p (h t) -> p h t", h=2)), R=(p12b,), W=(BmB,))
                    unpin(p12b)
                    p3, p3b = bank(pin=True)
                    pe_group(pe, [I("matmul", p3[:, j * 128:(j + 1) * 128], lhsT=Bm[b_][:, H[j], :],
                                    rhs=Pm[a][:, H[j], :], start=True, stop=True) for j in range(2)],
                             R=(BmB, PmA), W=(p3b,))
                    yield
                    op(dve, I("tensor_tensor", out=Pm[b_][:, hs, :], in0=v2(p3), in1=Pm[a][:, hs, :].bitcast(F32),
                              op=ALU.add), R=(p3b, PmA), W=(PmB,))
                    unpin(p3b)
                TTm, TTb = Pm[1], Bg["Pm1"]
                pu, pub = bank(pin=True)
                pe_group(pe, [I("matmul", pu[:, j * 128:(j + 1) * 128], lhsT=TTm[:, H[j], :], rhs=vb_t[:, H[j], :],
                                start=True, stop=True) for j in range(2)] +
                         [I("matmul", pu[:, 256 + j * 128:256 + (j + 1) * 128], lhsT=kbg_t[:, H[j], :],
                            rhs=TTm[:, H[j], :], start=True, stop=True) for j in range(2)],
                         R=(TTb, Bg["vb_t"], Bg["kbg_t"]), W=(pub,))
                yield
                op(act, I("activation", out=u_sb[:, hs, :], in_=v2(pu), func=AF.Copy), R=(pub,), W=(Bg["u_sb"],))
                op(dve, I("tensor_copy", out=wT[:, hs, :], in_=pu[:, 256:512].rearrange("p (h t) -> p h t", h=2)),
                   R=(pub,), W=(Bg["wT"],))
                unpin(pub)
                yield
                po, pob = bank(pin=True)
                for cch in range(2):
                    gch = (ti * 4 + blk) * 2 + cch
                    si, so = gch % 2, (gch + 1) % 2
                    rs = slice(cch * 64, (cch + 1) * 64)
                    pv, pvb = bank(pin=True)
                    pe_group(pe, [I("matmul", pv[:, j * 128:(j + 1) * 128], lhsT=wT[:, H[j], :],
                                    rhs=Sst[si][:, H[j], :], start=True, stop=True) for j in range(2)],
                             R=(Bg["wT"], Bg["S%d" % si]), W=(pvb,))
                    yield
                    op(dve, I("tensor_tensor", out=vnew[rs, hs, :], in0=u_sb[rs, hs, :],
                              in1=pv[rs, 0:256].rearrange("p (h t) -> p h t", h=2), op=ALU.subtract),
                       R=(pvb, Bg["u_sb"]), W=(Bg["vnew"],))
                    unpin(pvb)
                    fns = []
                    for j, h in enumerate(H):
                        fns.append(I("matmul", po[:, j * 128 + cch * 64:j * 128 + cch * 64 + 64],
                                     lhsT=Sst[si][:, h, :], rhs=qgT[:, h, cch * 64:(cch + 1) * 64],
                                     start=True, stop=False))
                        fns.append(I("matmul", po[:, j * 128 + cch * 64:j * 128 + cch * 64 + 64],
                                     lhsT=vnew[:, h, :], rhs=aT[:, h, cch * 64:(cch + 1) * 64],
                                     start=False, stop=True))
                    for j, h in enumerate(H):
                        fns.append(I("matmul", po[:, 256 + j * 128:256 + (j + 1) * 128], lhsT=kd_t[cch][:, h, :],
                                     rhs=vnew[:, h, :], start=True, stop=True))
                    pe_group(pe, fns, R=(Bg["S%d" % si], Bg["qgT"], Bg["vnew"], Bg["aT"], Bg["kd_t"]), W=(pob,))
                    yield
                    for j, h in enumerate(H):
                        op(dve, I("scalar_tensor_tensor", out=Sst[so][:, h, :], in0=Sst[si][:, h, :].bitcast(F32),
                                  scalar=dl[:, h, cch:cch + 1], in1=po[:, 256 + j * 128:256 + (j + 1) * 128],
                                  op0=ALU.mult, op1=ALU.add),
                           R=(pob, Bg["S%d" % si], Bg["dl"]), W=(Bg["S%d" % so],))
                    yield
                op(act, I("activation", out=oTs[:, hs, bs], in_=v2(po), func=AF.Copy), R=(pob,), W=(Bg["oTs"],))
                unpin(pob)

            def attn_stream(blk):
                attn_sc(blk, 0)
                yield
                yield
                attn_pv(blk, 0)
                yield
                attn_sc(blk, 1)
                yield
                yield
                attn_pv(blk, 1)
                yield
                attn_fin(blk)

            def gated(g):
                tmp, tmpb = (rt2, B["rt2"]) if g == 0 else (rstd, B["rstd"])
                pbs = []
                for h in (2 * g, 2 * g + 1):
                    op(act, I("activation", out=uT[:, h, :], in_=oTs[:, h, :], func=AF.Square),
                       R=(BS[g]["oTs"],), W=(uT_b[h],))
                    pb, pbb = bank(pin=True)
                    pbs.append((pb, pbb))
                    pe_group(pe, [I("matmul", pb[:, :], lhsT=onesb[:, :], rhs=uT[:, h, :], start=True, stop=True)],
                             R=(uT_b[h],) + CC, W=(pbb,))
                yield
                for pb, pbb in pbs:
                    op(act, I("activation", out=pb[:, :], in_=pb[:, :], func=AF.Ln, bias=epsc, scale=1.0 / 128.0),
                       R=(pbb,) + CC, W=(pbb,))
                    op(act, I("activation", out=pb[:, :], in_=pb[:, :], func=AF.Exp, scale=-0.5), R=(pbb,), W=(pbb,))
                yield
                for j, h in enumerate((2 * g, 2 * g + 1)):
                    pb, pbb = pbs[j]
                    op(dve, I("scalar_tensor_tensor", out=tmp[:, :], in0=oTs[:, h, :], scalar=cpk[:, C_DNN:C_DNN + 1],
                              in1=pb[:, :], op0=ALU.mult, op1=ALU.mult), R=(BS[g]["oTs"], pbb) + CC, W=(tmpb,))
                    unpin(pbb)
                    op(dve, I("tensor_tensor", out=dnT[:, h, :], in0=tmp[:, :], in1=zs[:, h, :], op=ALU.mult),
                       R=(tmpb, B["zs"]), W=(B["dnT"],))

            def chain(mk, tail=None):
                for blk in range(4):
                    yield from mk(blk)
                if tail is not None:
                    yield from tail

            class St:
                def __init__(self, gen):
                    self.gen = gen
                    self.clock = 0.0

            sts = [St(chain(lambda blk: dn_stream(blk, 0), gated(0))), St(chain(lambda blk: dn_stream(blk, 1), gated(1))),
                   St(chain(attn_stream))]
            base = max(pe.t if hasattr(pe, "t") else 0.0, 0.0)
            for st in sts:
                st.clock = 0.0
            sts[1].clock = float(SKEW) * 0.0
            alive = list(sts)
            while alive:
                st = min(alive, key=lambda x: x.clock)
                SchedState.cur = st
                try:
                    next(st.gen)
                except StopIteration:
                    alive.remove(st)
                SchedState.cur = None

            if PH < 8:
                return
            for s in range(2):
                wt, wtb = wslot(L0 + 8 + s)
                wv = wt[:, :].rearrange("p (k n) -> p k n", k=8)
                for j in range(4):
                    oc = s * 4 + j
                    pb, pbb = bank()
                    fns = []
                    for kc in range(8):
                        src = aoT[:, kc, :] if kc < 4 else dnT[:, kc - 4, :]
                        fns.append(I("matmul",
                            pb[:, :], lhsT=wv[:, kc, j * 128:(j + 1) * 128], rhs=src, start=(kc == 0), stop=(kc == 7)))
                    pe_group(pe, fns, R=(B["aoT"], B["dnT"], wtb), W=(pbb,))
                    op(dve, I("tensor_tensor", out=hT[:, oc, :], in0=pb[:, :], in1=hT[:, oc, :],
                                                                    op=ALU.add), R=(pbb, hT_b[oc]), W=(hT_b[oc],))
                wdone(L0 + 8 + s)

            if PH < 9:
                return
            if ti + 1 < NT:
                for c in range(6):
                    spq.dma(I("dma_start", out=XP[c], in_=xT_d[:, c, t0 + TT:t0 + 2 * TT]), W=XP_b[c])
            rmsnorm(hTc, hT_b, C_G + 8, uT, uT_b)
            for s in range(8):
                wt, wtb = wslot(L0 + 10 + s)
                wv = wt[:, :].rearrange("p (k n) -> p k n", k=8)
                for j in range(4):
                    hc = s * 4 + j
                    pb, pbb = bank()
                    if hc == 0:
                        mm8_split(pb, pbb, wv, j, wtb)
                    else:
                        pe_group(pe, mm8(pb, wv, j), R=tuple(uT_b) + (wtb,), W=(pbb,))
                    op(act, I("activation", out=rt2[:, :], in_=pb[:, :], func=AF.Square),
                       R=(pbb,), W=(B["rt2"],))
                    op(dve, I("scalar_tensor_tensor",
                        out=hid[:, hc, :], in0=pb[:, :], scalar=0.0, in1=rt2[:, :], op0=ALU.is_gt, op1=ALU.mult),
                       R=(pbb, B["rt2"]), W=(B["scrB"],) + tuple(y_b))
                wdone(L0 + 10 + s)
            for oc in range(8):
                wt, wtb = wslot(L0 + 18 + oc)
                wv = wt[:, :].rearrange("p (k n) -> p k n", k=32)
                pb, pbb = bank()
                pe_group(pe, [I("matmul", pb[:, :], lhsT=wv[:, kc, :], rhs=hid[:, kc, :],
                                                                      start=(kc == 0), stop=(kc == 31))
                              for kc in range(32)], R=(B["scrB"], wtb), W=(pbb,))
                op(dve, I("tensor_tensor", out=hT[:, oc, :], in0=pb[:, :], in1=hT[:, oc, :],
                                                                op=ALU.add), R=(pbb, hT_b[oc]), W=(hT_b[oc],))
                wdone(L0 + 18 + oc)

            if PH < 10:
                return
            rmsnorm(hTc, hT_b, C_G + 16, uT, uT_b)
            op(act, I("activation", out=pTb[:, :, :], in_=pTs[:, :, :], func=AF.Copy), R=(B["pTs"],), W=(B["pTb"],))
            if ti + 1 < NT:
                for c in (6, 7):
                    spq.dma(I("dma_start", out=XP[c], in_=xT_d[:, c, t0 + TT:t0 + 2 * TT]), W=XP_b[c])
            wp_t, wp_b = None, None
            for s in range(2):
                wt, wtb = wslot(L0 + 26 + s)
                wv = wt[:, :].rearrange("p (k n) -> p k n", k=8)
                if s == 0:
                    pass
                for j in range(4):
                    oc = s * 4 + j
                    pb, pbb = bank()
                    if oc == 0:
                        mm8_split(pb, pbb, wv, j, wtb)
                    else:
                        pe_group(pe, mm8(pb, wv, j), R=tuple(uT_b) + (wtb,), W=(pbb,))
                    op(act, I("activation", out=sg[:, :], in_=pb[:, :], func=AF.Sigmoid),
                       R=(pbb,), W=(B["rt1"],))
                    wt2, wt2b = wslot(L0 + 28)
                    wv2 = wt2[:, 0:2048].rearrange("p (k n) -> p k n", k=2)
                    pb2, pb2b = bank()
                    pe_group(pe, [I("matmul",
                        pb2[:, :], lhsT=wv2[:, kc, oc * 128:(oc + 1) * 128], rhs=pTb[:, kc, :],
                        start=(kc == 0), stop=(kc == 1)) for kc in range(2)], R=(B["pTb"], wt2b), W=(pb2b,))
                    op(dve, I("tensor_tensor", out=sg[:, :], in0=pb2[:, :], in1=sg[:, :], op=ALU.mult),
                       R=(pb2b, B["rt1"]), W=(B["rt1"],))
                    op(dve, I("tensor_tensor", out=hT[:, oc, :], in0=sg[:, :], in1=hT[:, oc, :],
                                                             op=ALU.add), R=(B["rt1"], hT_b[oc]), W=(hT_b[oc],))
                wdone(L0 + 26 + s) if s == 0 else None
            wdone(L0 + 28)

        def tile_final(ti):
            t0 = ti * TT
            rmsnorm(hTc, hT_b, C_G + 24, outt, None, dst_all_b=B["scrB"])
            out_toks.append(spq.dma(I("dma_start", out=out_d[:, :, t0:t0 + TT], in_=outt),
                                    R=(B["scrB"],), W=()))

        for ti in range(NT):
            tile_body(ti)
            tile_final(ti)
        for tok in out_toks:
            sp.wait(tok)
        with nc.Block() as block:
            @block.sync
            def _(e):
                for f in sp.ops:
                    f(e)

            @block.tensor
            def _(e):
                for f in pe.ops:
                    f(e)

            @block.scalar
            def _(e):
                for f in act.ops:
                    f(e)

            @block.vector
            def _(e):
                for f in dve.ops:
                    f(e)

            @block.gpsimd
            def _(e):
                for f in pool.ops:
                    f(e)
    return nc


def _fm(a, nchunk):
    T = a.shape[0]
    return np.ascontiguousarray(a.reshape(T, nchunk, 128).transpose(2, 1, 0))


def _slot8(G):
    return np.ascontiguousarray(G.reshape(8, 128, 512).transpose(1, 0, 2).reshape(128, 4096))


def host_prepare(inputs, S):
    w_in = np.asarray(inputs["w_in"][0], dtype=np.float32)
    w_o = np.asarray(inputs["w_o"][0], dtype=np.float32)
    w_up = np.asarray(inputs["w_up"][0], dtype=np.float32)
    w_down = np.asarray(inputs["w_down"][0], dtype=np.float32)
    w_g = np.asarray(inputs["w_ple_gate"][0], dtype=np.float32)
    w_p = np.asarray(inputs["w_ple_proj"][0], dtype=np.float32)
    wpack = np.zeros((NSLOT, 128, 4096), np.float32)
    perm = np.arange(64)
    perm = (perm + 32) % 64
    aq = w_in[:, 0:512]
    ak = w_in[:, 512:640]

    def rotcols(G):
        nh = G.shape[1] // 64
        return np.concatenate([G[:, h * 64:(h + 1) * 64][:, perm] for h in range(nh)], axis=1)

    for s in range(2):
        cols = []
        for pr in range(2):
            c = 2 * s + pr
            g = aq[:, c * 128:(c + 1) * 128]
            cols += [g, rotcols(g)]
        wpack[s] = _slot8(np.concatenate(cols, axis=1))
    kA = np.concatenate([ak[:, 0:64], ak[:, 0:64]], axis=1)
    kB = np.concatenate([ak[:, 64:128], ak[:, 64:128]], axis=1)
    wpack[2] = _slot8(np.concatenate([kA, rotcols(kA), kB, rotcols(kB)], axis=1))
    wpack[3] = _slot8(w_in[:, 768:1280])
    wpack[4] = _slot8(w_in[:, 1280:1792])
    wpack[5] = _slot8(w_in[:, 1792:2304])
    wpack[6] = _slot8(w_in[:, 2304:2816])
    g7 = np.zeros((1024, 512), np.float32)
    g7[:, 0:128] = w_in[:, 640:768]
    g7[:, 128:136] = w_in[:, 2816:2824]
    wpack[7] = _slot8(g7)
    for s in range(2):
        wpack[8 + s] = _slot8(w_o[:, s * 512:(s + 1) * 512])
        wpack[26 + s] = _slot8(w_g[:, s * 512:(s + 1) * 512])
    for s in range(8):
        wpack[10 + s] = _slot8(w_up[:, s * 512:(s + 1) * 512])
        wpack[18 + s] = w_down[:, s * 128:(s + 1) * 128].reshape(32, 128, 128).transpose(1, 0, 2).reshape(128, 4096)
    wpack[28, :, 0:2048] = w_p.reshape(2, 128, 1024).transpose(1, 0, 2).reshape(128, 2048)

    cp = np.zeros((128, NCP), np.float32)
    i = np.arange(128)
    ii, jj = i[:, None], i[None, :]
    same = (ii // 64) == (jj // 64)
    cp[:, C_ID:C_ID + 128] = np.eye(128, dtype=np.float32)
    cp[:, C_ONE:C_ONE + 128] = 1.0
    cp[:, C_TRI:C_TRI + 128] = (same & (ii <= jj)).astype(np.float32)
    cp[:, C_BON:C_BON + 128] = same.astype(np.float32)
    cp[:, C_ML:C_ML + 128] = np.where(same & (ii > jj), 0.0, NEG)
    cp[:, C_MU:C_MU + 128] = np.where(same & (jj >= ii), 0.0, NEG)
    cp[:, C_M0] = (i < 64)
    cp[:, C_M1] = (i >= 64)
    cp[:, C_END:C_END + 128] = (jj == (ii // 64) * 64 + 63).astype(np.float32)
    cp[:, C_AP:C_AP + 128] = (jj < ii).astype(np.float32)
    cp[:, C_AC:C_AC + 128] = (jj >= ii).astype(np.float32)
    for n, nm in enumerate(["norm_mix", "norm_mlp", "norm_ple"]):
        cp[:, C_G + 8 * n:C_G + 8 * n + 8] = np.asarray(inputs[nm][0], np.float32).reshape(8, 128).T
    cp[:, C_G + 24:C_G + 32] = np.asarray(inputs["norm_final"], np.float32).reshape(8, 128).T
    cw = np.asarray(inputs["conv_w"][0], np.float32)
    cp[:, C_CW:C_CW + 48] = cw.reshape(4, 12, 128).transpose(2, 1, 0).reshape(128, 48)
    cp[:, C_DNN] = np.asarray(inputs["dn_norm"][0], np.float32)
    cp[:, C_ALOG:C_ALOG + 4] = np.asarray(inputs["a_log"][0], np.float32)[None, :]
    cp[:, C_DTB:C_DTB + 4] = np.asarray(inputs["dt_bias"][0], np.float32)[None, :]
    sk = np.asarray(inputs["sinks"][0], np.float32)
    cp[:, C_SNK:C_SNK + 4] = sk.reshape(4, 2)[:, (np.arange(128) // 64)].T

    half = 32
    inv = 1.0 / (10000.0 ** (np.arange(half, dtype=np.float32) * (2.0 / 64)))
    pos = np.arange(S, dtype=np.float32)
    ang = (pos[None, :] * inv[:, None]).astype(np.float32)
    cos = np.cos(ang).astype(np.float32)
    sin = np.sin(ang).astype(np.float32)
    pidx = np.arange(128) % 64
    fidx = pidx % 32
    sign = np.where(pidx < 32, -1.0, 1.0).astype(np.float32)
    rope = np.zeros((128, 4, S), np.float32)
    rope[:, 0, :] = cos[fidx] * 0.125
    rope[:, 1, :] = sin[fidx] * sign[:, None] * 0.125
    rope[:, 2, :] = cos[fidx]
    rope[:, 3, :] = sin[fidx] * sign[:, None]
    return wpack, cp, rope


_CACHE = {}


def run(inputs, S, ncores, dbg=None):
    x = np.asarray(inputs["x"], np.float32)
    p = np.asarray(inputs["p"], np.float32)[0]
    wpack, cp, rope = host_prepare(inputs, S)
    key = (S, tuple(sorted(dbg.items())) if dbg else None)
    nc = build(S, dbg)
    in_maps = []
    for b in range(ncores):
        in_maps.append({"xT": _fm(x[b, :S], 8), "pT": _fm(p[b, :S], 2), "wpack": wpack, "cpack": cp, "rope": rope})
    res = run_bass_kernel_spmd(nc, in_maps, core_ids=list(range(ncores)))
    outs = []
    for b in range(ncores):
        o = res.results[b]["outT"]
        outs.append(o.transpose(2, 1, 0).reshape(S, D))
    return np.stack(outs, axis=0), res


def kernel(**inputs):
    out, _ = run(inputs, 4096, 8)
    return out.astype(np.float32)
```

```python
import os
import numpy as np
from contextlib import ExitStack
import concourse.bass as bass
import concourse.mybir as mybir
from concourse.bass_utils import run_bass_kernel_spmd

F32 = mybir.dt.float32
BF16 = mybir.dt.bfloat16
F32R = mybir.dt.float32r
AF = mybir.ActivationFunctionType
ALU = mybir.AluOpType

D = 1024
EPS = 1e-6
TT = 512
NSLOT = 29
NW = 3
NEG = -30000.0

C_ID, C_ONE, C_TRI, C_BON, C_ML, C_MU, C_AP, C_AC = [i * 128 for i in range(8)]
C_G = 1024
C_CW = C_G + 32
C_DNN = C_CW + 48
C_ALOG = C_DNN + 1
C_DTB = C_ALOG + 4
C_SNK = C_DTB + 4
C_M0 = C_SNK + 4
C_M1 = C_M0 + 1
C_END = C_M1 + 1
NCP = C_END + 128


class Buf:
    __slots__ = ("w", "r", "excl")

    def __init__(self, excl=False):
        self.w = None
        self.r = {}
        self.excl = excl


class EngRec:
    def __init__(self, name):
        self.name = name
        self.ops = []
        self.sem = None
        self.cnt = 0
        self.waited = {}

    def wait(self, tok):
        sem, val = tok
        if self.waited.get(id(sem), 0) >= val:
            return
        self.waited[id(sem)] = val
        self.ops.append(I("wait_ge", sem, val))


def _deps(reads, writes, mysem=None):
    deps = []
    for b in reads:
        if b.w is not None:
            deps.append(b.w)
        if b.excl:
            deps.extend(t for t in b.r.values() if t[0] is not mysem)
    for b in writes:
        if b.w is not None:
            deps.append(b.w)
        deps.extend(b.r.values())
    return deps


def _update(tok, reads, writes):
    for b in writes:
        b.w = tok
        b.r = {}
    for b in reads:
        sem, val = tok
        b.r[id(sem)] = tok


class Ins:
    __slots__ = ("name", "a", "k")

    def __init__(self, name, a, k):
        self.name, self.a, self.k = name, a, k

    def __call__(self, e):
        return getattr(e, self.name)(*self.a, **self.k)

    def cost(self, engname):
        try:
            if self.name == "matmul":
                rhs = self.k["rhs"]
                n = 1
                for d in rhs.shape[1:]:
                    n *= d
                dt = self.k["lhsT"].dtype
                if dt == BF16:
                    return max(0.06, n / 2100.0)
                if dt == F32R:
                    return max(0.213, n / 600.0)
                return max(0.45, n / 300.0)
            if self.name == "transpose":
                return 0.12
            out = self.k.get("out")
            n = 1
            for d in out.shape[1:]:
                n *= d
            if engname == "act":
                return 0.2 + n / 1000.0
            return 0.15 + n / 950.0
        except Exception:
            return 0.3


def I(name, *a, **k):
    return Ins(name, a, k)


class SchedState:
    cur = None
    fin = {}
    LAT = 0.3


def _est(eng, deps, cost):
    start = getattr(eng, "t", 0.0)
    for tok in deps:
        start = max(start, SchedState.fin.get((id(tok[0]), tok[1]), 0.0) + SchedState.LAT)
    fin = start + cost
    eng.t = fin
    if SchedState.cur is not None:
        SchedState.cur.clock = max(SchedState.cur.clock, fin)
    return fin


def op(eng, fn, R=(), W=()):
    deps = _deps(R, W, eng.sem)
    for tok in deps:
        if eng.name == "pe" and tok[0] is eng.sem:
            continue
        eng.wait(tok)
    eng.cnt += 1
    tok = (eng.sem, eng.cnt)
    SchedState.fin[(id(eng.sem), eng.cnt)] = _est(eng, deps, fn.cost(eng.name) if isinstance(fn, Ins) else 0.3)
    sem = eng.sem
    eng.ops.append(lambda e, fn=fn, sem=sem: fn(e).then_inc(sem, 1))
    _update(tok, R, W)
    return tok


def pe_group(eng, fns, R=(), W=()):
    deps = _deps(R, W, eng.sem)
    for tok in deps:
        if tok[0] is eng.sem:
            continue
        eng.wait(tok)
    SchedState.fin[(id(eng.sem), eng.cnt + 1)] = _est(
        eng, deps, sum(fn.cost("pe") if isinstance(fn, Ins) else 0.2 for fn in fns))
    for fn in fns[:-1]:
        eng.ops.append(lambda e, fn=fn: fn(e))
    eng.cnt += 1
    tok = (eng.sem, eng.cnt)
    sem = eng.sem
    last = fns[-1]
    eng.ops.append(lambda e, fn=last, sem=sem: fn(e).then_inc(sem, 1))
    _update(tok, R, W)
    return tok


class DmaQ:
    def __init__(self, eng, sems):
        self.eng = eng
        self.sems = sems
        self.vals = [0] * len(sems)
        self.i = 0

    def dma(self, fn, R=(), W=()):
        k = self.i % len(self.sems)
        self.i += 1
        if self.vals[k] > 0:
            self.eng.wait((self.sems[k], self.vals[k]))
        for tok in _deps(R, W):
            self.eng.wait(tok)
        self.vals[k] += 16
        sem = self.sems[k]
        tok = (sem, self.vals[k])
        self.eng.ops.append(lambda e, fn=fn, sem=sem: fn(e).then_inc(sem, 16))
        _update(tok, R, W)
        return tok


def build(S, dbg=None):
    NT = S // TT
    SchedState.fin = {}
    SchedState.cur = None
    PH = int(os.environ.get('KPH', '99'))
    SUB = int(os.environ.get('KSUB', '99'))
    SKEW = int(os.environ.get('KSKEW', '10'))
    nc = bass.Bass("TRN2", target_bir_lowering=False)
    xT_d = nc.dram_tensor("xT", [128, 8, S], F32, kind="ExternalInput").ap()
    pT_d = nc.dram_tensor("pT", [128, 2, S], F32, kind="ExternalInput").ap()
    wp_d = nc.dram_tensor("wpack", [NSLOT, 128, 4096], F32, kind="ExternalInput").ap()
    cp_d = nc.dram_tensor("cpack", [128, NCP], F32, kind="ExternalInput").ap()
    rope_d = nc.dram_tensor("rope", [128, 4, S], F32, kind="ExternalInput").ap()
    out_d = nc.dram_tensor("outT", [128, 8, S], F32, kind="ExternalOutput").ap()
    dbg_d = {}
    if dbg:
        for nm, shp in dbg.items():
            dbg_d[nm] = nc.dram_tensor("dbg_" + nm, list(shp), F32, kind="ExternalOutput").ap()

    es = ExitStack()
    with es:
        def sb(name, shape, dt):
            return es.enter_context(nc.sbuf_tensor(name, list(shape), dt))

        def sem(name):
            return es.enter_context(nc.semaphore(name))

        pe, act, dve, pool, sp = (EngRec(n) for n in ("pe", "act", "dve", "pool", "sp"))
        for e in (pe, act, dve, pool, sp):
            e.sem = sem("s_" + e.name)
        spq = DmaQ(sp, [sem("spd%d" % i) for i in range(12)])
        plq = DmaQ(pool, [sem("pld%d" % i) for i in range(NW)])

        cpk = sb("cpk", [128, NCP], F32)
        onesb = sb("onesb", [128, 128], BF16)
        am = sb("am", [128, 2, 4, 128], BF16)
        mL4 = sb("mL4", [128, 4, 128], F32)
        mU4 = sb("mU4", [128, 4, 128], F32)
        small = sb("small", [128, 64], F32)
        hT = sb("hT", [128, 8, TT], F32)
        uT = sb("uT", [128, 8, TT], BF16)
        rstd = sb("rstd", [128, TT], F32)
        ropeT = sb("ropeT", [128, 4, TT], F32)
        wb = [sb("wb%d" % i, [128, 4096], BF16) for i in range(NW)]
        qT = sb("qT", [128, 4, TT], BF16)
        kT = sb("kT", [128, 2, TT + 128], BF16)
        vtok = sb("vtok", [128, 5, 128], BF16)
        Eb = [sb("Eb%d" % i, [128, 4, 128], BF16) for i in range(2)]
        PT = [sb("PT%d" % i, [128, 4, 128], BF16) for i in range(4)]
        aoT = sb("aoT", [128, 4, TT], BF16)
        rt1 = sb("rt1", [128, TT], F32)
        rt2 = sb("rt2", [128, TT], F32)
        scrA = sb("scrA", [128, 12 * (TT + 3)], F32R)
        scrB = sb("scrB", [128, 16 * TT], F32)
        zs = sb("zs", [128, 4, TT], BF16)
        dnT = sb("dnT", [128, 4, TT], BF16)
        hist = sb("hist", [128, 12, 3], F32)
        ba_sb = sb("ba_sb", [128, 4, 8], F32)
        gtok = sb("gtok", [128, 64], F32)
        gpre = sb("gpre", [128, 4, 12], F32)
        gb4 = sb("gb4", [128, 4, 128], F32)
        tmpM = sb("tmpM", [128, 4, 128], F32)
        Dst = sb("Dst", [128, 4, 128], F32)
        DTi = sb("DTi", [128, 4, 128], F32)
        egbc = sb("egbc", [128, 4, 128], F32)
        dl = sb("dl", [128, 4, 2], F32)
        Am = [sb("Am%d" % i, [128, 4, 128], F32R) for i in range(2)]
        Bm = [sb("Bm%d" % i, [128, 4, 128], F32R) for i in range(2)]
        Pm = [sb("Pm%d" % i, [128, 4, 128], F32R) for i in range(2)]
        u_sb = sb("u_sb", [128, 4, 128], F32)
        wT = sb("wT", [128, 4, 128], F32R)
        aT = sb("aT", [128, 4, 128], F32R)
        qgT = sb("qgT", [128, 4, 128], F32R)
        kd_t = [sb("kd_t%d" % i, [128, 4, 128], F32R) for i in range(2)]
        kbg_t = sb("kbg_t", [128, 4, 128], F32R)
        vb_t = sb("vb_t", [128, 4, 128], F32R)
        vnew = sb("vnew", [128, 4, 128], F32R)
        Sst = [sb("Sst%d" % i, [128, 4, 128], F32R) for i in range(2)]
        pTs = sb("pTs", [128, 2, TT], F32)
        pTb = sb("pTb", [128, 2, TT], BF16)
        sg = rt1

        ps = [es.enter_context(nc.psum_tensor("ps%d" % i, [128, 512], F32)) for i in range(8)]
        ps_b = [Buf(excl=True) for _ in range(8)]
        rot = [0]

        pinned = set()

        def bank(pin=False):
            for _ in range(12):
                i = rot[0] % 6
                rot[0] += 1
                if i not in pinned:
                    break
            else:
                raise RuntimeError("no free PSUM bank")
            if pin:
                pinned.add(i)
            return ps[i], ps_b[i]

        def unpin(pbb):
            pinned.discard(ps_b.index(pbb))

        pcw = scrA[:, :].rearrange("p (c t) -> p c t", c=12)
        pc = scrA[:, :].bitcast(F32).rearrange("p (c t) -> p c t", c=12)
        qkn = scrA[:, 0:8 * TT].rearrange("p (c t) -> p c t", c=8)
        outt = scrB[:, 0:8 * TT].rearrange("p (c t) -> p c t", c=8)
        y = scrB[:, 0:12 * TT].rearrange("p (c t) -> p c t", c=12)
        oTs = scrB[:, 12 * TT:16 * TT].rearrange("p (c t) -> p c t", c=4)
        hid = scrB[:, :].bitcast(BF16).rearrange("p (c t) -> p c t", c=32)

        def c_(off, n=128):
            return cpk[:, off:off + n]

        ident = c_(C_ID)
        onesf = c_(C_ONE)
        tri = c_(C_TRI)
        bones = c_(C_BON)
        endsel = c_(C_END)
        esink = small[:, 0:4]
        negA = small[:, 4:8]
        epsc = small[:, 8:9]
        eps128 = small[:, 9:10]
        onec = small[:, 10:11]
        dtb = c_(C_DTB, 4)
        dtb4 = small[:, 16:32].rearrange("p (b h) -> p b h", b=4)
        negA4 = small[:, 32:48].rearrange("p (b h) -> p b h", b=4)

        B = {}
        for nm in ("gpre", "cpk", "const", "hT", "uT", "rstd", "rope", "qT", "kT", "vtok", "Eb", "aoT", "rt1", "rt2",
                   "scrA", "scrB", "zs", "dnT", "hist", "ba", "gtok", "gb4", "tmpM", "Dst", "DTi", "egbc", "dl",
                   "u_sb", "wT", "aT", "qgT", "kd_t", "kbg_t", "vb_t", "vnew", "pTs", "pTb", "sg", "oTs", "outt"):
            B[nm] = Buf()
        BS = [{nm: Buf() for nm in ("gb4", "tmpM", "gtok", "egbc", "dl", "Dst", "DTi", "qgT", "Bm0", "Bm1", "Am0", "Am1",
                                    "Pm0", "Pm1", "aT", "kd_t", "kbg_t", "vb_t", "u_sb", "wT", "vnew", "S0", "S1", "oTs")}
              for _ in range(2)]
        hT_b = [Buf() for _ in range(8)]
        hTc = [hT[:, c, :] for c in range(8)]
        _xpn = ("gb4", "tmpM", "Dst", "DTi", "egbc", "u_sb")
        XP = [t[:, :, :].rearrange("p h t -> p (h t)") for t in (gb4, tmpM, Dst, DTi, egbc, u_sb)]
        XP += [pTs[:, 0, :], pTs[:, 1, :]]
        XP_b = [(BS[0][nm], BS[1][nm]) for nm in _xpn] + [(B["pTs"],), (B["pTs"],)]
        pc_b = [Buf() for _ in range(12)]
        y_b = [Buf() for _ in range(12)]
        qkn_b = [Buf() for _ in range(8)]
        uT_b = [Buf() for _ in range(8)]
        PT_b = [Buf() for _ in range(4)]
        Eb_b = [Buf(), Buf()]
        Am_b = [Buf(), Buf()]
        Bm_b = [Buf(), Buf()]
        Pm_b = [Buf(), Buf()]
        S_b = [Buf(), Buf()]
        wb_b = [Buf() for _ in range(NW)]

        wstate = {"next": 0}
        total_loads = NT * NSLOT

        def issue_load(L):
            slot = L % NSLOT
            i = L % NW
            if slot == 7:
                src = wp_d[slot].rearrange("p (k n) -> p k n", k=8)[:, :, 0:136]
                dst = wb[i][:, :].rearrange("p (k n) -> p k n", k=8)[:, :, 0:136]
            elif slot == 28:
                src = wp_d[slot][:, 0:2048]
                dst = wb[i][:, 0:2048]
            else:
                src = wp_d[slot]
                dst = wb[i][:, :]
            plq.dma(I("dma_start", out=dst, in_=src), R=(), W=(wb_b[i],))

        def wslot(L):
            while wstate["next"] <= min(L, total_loads - 1):
                issue_load(wstate["next"])
                wstate["next"] += 1
            return wb[L % NW], wb_b[L % NW]

        def wdone(L):
            while wstate["next"] < total_loads and wstate["next"] <= L + NW:
                issue_load(wstate["next"])
                wstate["next"] += 1

        spq.dma(I("dma_start", out=cpk[:, :], in_=cp_d), W=(B["cpk"],))
        CK = (B["cpk"],)
        op(dve, I("tensor_copy", out=onesb[:, :], in_=onesf), R=CK, W=(B["const"],))
        op(dve, I("memset", am[:, :, :, :], 0.0), W=(B["const"],))
        for a_i in range(2):
            for j in range(4):
                if j == 0 and a_i == 0:
                    continue
                src = c_(C_AP) if j % 2 == 0 else c_(C_AC)
                op(dve, I("tensor_copy", out=am[:, a_i, j, :], in_=src),
                   R=CK, W=(B["const"],))
        for j in range(4):
            op(dve, I("tensor_copy", out=mL4[:, j, :], in_=c_(C_ML)), R=CK, W=(B["const"],))
            op(dve, I("tensor_copy", out=mU4[:, j, :], in_=c_(C_MU)), R=CK, W=(B["const"],))
        op(dve, I("memset", small[:, 8:9], EPS), W=(B["const"],))
        op(dve, I("memset", small[:, 9:10], 128.0 * EPS), W=(B["const"],))
        op(dve, I("memset", small[:, 10:11], 1.0), W=(B["const"],))
        op(act, I("activation", out=small[:, 0:4], in_=c_(C_SNK, 4), func=AF.Exp), R=CK, W=(B["const"],))
        op(act, I("activation", out=small[:, 4:8], in_=c_(C_ALOG, 4), func=AF.Exp), R=CK, W=(B["const"],))
        op(dve, I("tensor_scalar", out=small[:, 4:8], in0=small[:, 4:8], scalar1=-1.0, scalar2=None,
                                          op0=ALU.mult), R=CK, W=(B["const"],))
        op(dve, I("tensor_scalar", out=Sst[0][:, :, :], in0=mL4[:, :, :], scalar1=0.0, scalar2=None, op0=ALU.mult),
           R=(B["const"],), W=(BS[0]["S0"], BS[1]["S0"]))
        for bb in range(4):
            op(dve, I("tensor_copy", out=small[:, 16 + 4 * bb:20 + 4 * bb], in_=c_(C_DTB, 4)), R=CK, W=(B["const"],))
            op(dve, I("tensor_copy", out=small[:, 32 + 4 * bb:36 + 4 * bb], in_=small[:, 4:8]), R=(B["const"],), W=(B["const"],))
        op(dve, I("tensor_scalar", out=vnew[:, :, :], in0=mL4[:, :, :], scalar1=0.0, scalar2=None, op0=ALU.mult),
           R=(B["const"],), W=(BS[0]["vnew"], BS[1]["vnew"]))
        op(dve, I("memset", hist[:, :, :], 0.0), W=(B["hist"],))
        op(dve, I("memset", kT[:, :, :], 0.0), W=(B["kT"],))
        op(dve, I("memset", vtok[:, :, :], 0.0), W=(B["vtok"],))
        CC = (B["cpk"], B["const"])

        def _tb(x):
            return x if isinstance(x, tuple) else (x,)

        def rmsnorm(srcs, src_b, gcol, dst, dst_b, dst_all_b=None):
            for c in range(8):
                op(act, I("activation", out=uT[:, c, :], in_=srcs[c], func=AF.Square), R=_tb(src_b[c]), W=(uT_b[c],))
            pb, pbb = bank()
            for c in range(8):
                pe_group(pe, [I("matmul", pb[:, :], lhsT=onesb[:, :], rhs=uT[:, c, :], start=(c == 0), stop=(c == 7))],
                         R=(uT_b[c],) + CC, W=(pbb,))
            op(act, I("activation", out=rstd[:, :], in_=pb[:, :], func=AF.Ln, bias=epsc, scale=1.0 / D),
               R=(pbb,) + CC, W=(B["rstd"],))
            op(act, I("activation", out=rstd[:, :], in_=rstd[:, :], func=AF.Exp, scale=-0.5),
               R=(B["rstd"],), W=(B["rstd"],))
            for c in range(8):
                wbufs = (dst_b[c],) if dst_all_b is None else (dst_all_b,)
                op(dve, I("scalar_tensor_tensor",
                    out=dst[:, c, :], in0=srcs[c], scalar=cpk[:, gcol + c:gcol + c + 1], in1=rstd[:, :],
                    op0=ALU.mult, op1=ALU.mult), R=_tb(src_b[c]) + (B["rstd"],) + CC, W=wbufs)

        def mm8(pb, wv, j, src=None, n0=0, n1=TT):
            return [I("matmul", pb[:, 0:n1 - n0], lhsT=wv[:, kc, j * 128:(j + 1) * 128],
                                              rhs=uT[:, kc, n0:n1], start=(kc == 0), stop=(kc == 7))
                    for kc in range(8)]

        def mm8_split(pb, pbb, wv, j, wtb):
            fns = mm8(pb, wv, j)
            for kc in range(8):
                pe_group(pe, [fns[kc]], R=(uT_b[kc], wtb), W=(pbb,))

        def dbg_dump(name, ap_src, bufs, t0=None):
            if name not in dbg_d:
                return
            d = dbg_d[name]
            plq.dma(I("dma_start", out=d, in_=ap_src), R=bufs, W=())

        out_toks = []

        def tile_body(ti):
            t0 = ti * TT
            L0 = ti * NSLOT
            NPF = 8 if ti > 0 else 0
            for c in range(NPF, 8):
                spq.dma(I("dma_start", out=hT[:, c, :], in_=xT_d[:, c, t0:t0 + TT]), W=(hT_b[c],))
            spq.dma(I("dma_start", out=ropeT[:, :, :], in_=rope_d[:, :, t0:t0 + TT]), W=(B["rope"],))
            if ti > 0:
                op(dve, I("tensor_copy", out=kT[:, :, 0:128], in_=kT[:, :, TT:TT + 128]),
                   R=(B["kT"],), W=(B["kT"],))
                op(dve, I("tensor_copy", out=vtok[:, 0, :], in_=vtok[:, 4, :]), R=(B["vtok"],), W=(B["vtok"],))
            rmsnorm([XP[c] if c < NPF else hTc[c] for c in range(8)],
                    [XP_b[c] if c < NPF else hT_b[c] for c in range(8)], C_G + 0, uT, uT_b)
            for c in range(NPF):
                op(act, I("activation", out=hT[:, c, :], in_=XP[c], func=AF.Copy), R=XP_b[c], W=(hT_b[c],))
            spq.dma(I("dma_start", out=pTs[:, :, :], in_=pT_d[:, :, t0:t0 + TT]), W=(B["pTs"],))

            if PH < 2:
                return
            for s in range(3):
                wt, wtb = wslot(L0 + s)
                wv = wt[:, :].rearrange("p (k n) -> p k n", k=8)
                for pr in range(2):
                    pX, pXb = bank()
                    pR, pRb = bank()
                    if s == 0 and pr == 0:
                        mm8_split(pX, pXb, wv, 0, wtb)
                    else:
                        pe_group(pe, mm8(pX, wv, 2 * pr), R=tuple(uT_b) + (wtb,), W=(pXb,))
                    pe_group(pe, mm8(pR, wv, 2 * pr + 1), R=tuple(uT_b) + (wtb,), W=(pRb,))
                    if s < 2:
                        ci = 0
                        dst = qT[:, 2 * s + pr, :]
                        dstb = B["qT"]
                    else:
                        ci = 2
                        dst = kT[:, pr, 128:128 + TT]
                        dstb = B["kT"]
                    op(dve, I("tensor_tensor", out=rt1[:, :], in0=pX[:, :], in1=ropeT[:, ci, :],
                                                                    op=ALU.mult),
                       R=(pXb, B["rope"]), W=(B["rt1"],))
                    op(dve, I("tensor_tensor", out=rt2[:, :], in0=pR[:, :],
                                                                    in1=ropeT[:, ci + 1, :], op=ALU.mult),
                       R=(pRb, B["rope"]), W=(B["rt2"],))
                    op(dve, I("tensor_tensor", out=dst, in0=rt1[:, :], in1=rt2[:, :], op=ALU.add),
                       R=(B["rt1"], B["rt2"]), W=(dstb,))
                wdone(L0 + s)

            if PH < 3:
                return
            op(dve, I("tensor_copy", out=pcw[:, :, 0:3], in_=hist[:, :, :]), R=(B["hist"],),
               W=(B["scrA"],) + tuple(pc_b) + tuple(qkn_b))

            def conv_grp(grp):
                for ch in range(grp * 4, grp * 4 + 4):
                    cw0 = C_CW + ch * 4
                    op(act, I("activation", out=y[:, ch, :], in_=pc[:, ch, 0:TT], func=AF.Copy,
                              scale=cpk[:, cw0:cw0 + 1]), R=(pc_b[ch],) + CC, W=(y_b[ch], B["scrB"]))
                    for k in range(1, 4):
                        op(dve, I("scalar_tensor_tensor", out=y[:, ch, :], in0=pc[:, ch, k:k + TT],
                                  scalar=cpk[:, cw0 + k:cw0 + k + 1], in1=y[:, ch, :], op0=ALU.mult, op1=ALU.add),
                           R=(pc_b[ch], y_b[ch]) + CC, W=(y_b[ch],))
                op(dve, I("tensor_copy", out=hist[:, grp * 4:grp * 4 + 4, :], in_=pc[:, grp * 4:grp * 4 + 4, TT:TT + 3]),
                   R=tuple(pc_b[grp * 4:grp * 4 + 4]), W=(B["hist"],))

            def silu_grp(grp):
                for ch in range(grp * 4, grp * 4 + 4):
                    op(act, I("activation", out=y[:, ch, :], in_=y[:, ch, :], func=AF.Silu), R=(y_b[ch],), W=(y_b[ch],))

            def l2norm_grp(half):
                pbs = []
                for j in range(4):
                    ch = half * 4 + j
                    sq = PT[j][:, :, :].rearrange("p a q -> p (a q)")
                    op(act, I("activation", out=sq, in_=y[:, ch, :], func=AF.Square), R=(y_b[ch],), W=(PT_b[j],))
                    pb, pbb = bank(pin=True)
                    pbs.append((pb, pbb))
                    pe_group(pe, [I("matmul", pb[:, :], lhsT=onesb[:, :], rhs=sq, start=True, stop=True)],
                             R=(PT_b[j],) + CC, W=(pbb,))
                for j in range(4):
                    pb, pbb = pbs[j]
                    if half == 0:
                        op(act, I("activation", out=pb[:, :], in_=pb[:, :], func=AF.Ln, bias=eps128, scale=128.0),
                           R=(pbb,) + CC, W=(pbb,))
                    else:
                        op(act, I("activation", out=pb[:, :], in_=pb[:, :], func=AF.Ln, bias=epsc, scale=1.0),
                           R=(pbb,) + CC, W=(pbb,))
                for j in range(4):
                    pb, pbb = pbs[j]
                    op(act, I("activation", out=pb[:, :], in_=pb[:, :], func=AF.Exp, scale=-0.5), R=(pbb,), W=(pbb,))
                for j in range(4):
                    ch = half * 4 + j
                    pb, pbb = pbs[j]
                    wl = (qkn_b[ch], pc_b[ch]) + ((pc_b[ch - 1],) if ch > 0 else ())
                    op(dve, I("tensor_tensor", out=qkn[:, ch, :], in0=y[:, ch, :], in1=pb[:, :], op=ALU.mult),
                       R=(y_b[ch], pbb), W=wl)
                    unpin(pbb)

            for s in range(3, 7):
                wt, wtb = wslot(L0 + s)
                wv = wt[:, :].rearrange("p (k n) -> p k n", k=8)
                for j in range(4):
                    pb, pbb = bank()
                    pe_group(pe, mm8(pb, wv, j), R=tuple(uT_b) + (wtb,), W=(pbb,))
                    if s < 6:
                        ch = (s - 3) * 4 + j
                        op(act, I("activation", out=pcw[:, ch, 3:3 + TT], in_=pb[:, :], func=AF.Copy),
                           R=(pbb,), W=(pc_b[ch],))
                    else:
                        op(act, I("activation", out=zs[:, j, :], in_=pb[:, :], func=AF.Silu),
                           R=(pbb,), W=(B["zs"],))
                wdone(L0 + s)
                if s < 6:
                    conv_grp(s - 3)
                if s >= 4:
                    silu_grp(s - 4)
                if s == 5:
                    l2norm_grp(0)
                if s == 6:
                    l2norm_grp(1)
            wt, wtb = wslot(L0 + 7)
            wv = wt[:, :].rearrange("p (k n) -> p k n", k=8)
            for blk in range(4):
                pb, pbb = bank()
                pe_group(pe, [I("matmul",
                    pb[:, 0:128], lhsT=uT[:, kc, blk * 128:(blk + 1) * 128], rhs=wv[:, kc, 0:128],
                    start=(kc == 0), stop=(kc == 7)) for kc in range(8)],
                    R=tuple(uT_b) + (wtb,), W=(pbb,))
                op(act, I("activation", out=vtok[:, 1 + blk, :], in_=pb[:, 0:128], func=AF.Copy),
                   R=(pbb,), W=(B["vtok"],))
            pb, pbb = bank()
            for blk in range(4):
                pe_group(pe, [I("matmul",
                    pb[:, blk * 8:blk * 8 + 8], lhsT=uT[:, kc, blk * 128:(blk + 1) * 128], rhs=wv[:, kc, 128:136],
                    start=(kc == 0), stop=(kc == 7)) for kc in range(8)],
                    R=tuple(uT_b) + (wtb,), W=(pbb,))
            op(act, I("activation", out=ba_sb[:, :, :], in_=pb[:, 0:32].rearrange("p (b n) -> p b n", b=4),
                                                  func=AF.Copy), R=(pbb,), W=(B["ba"],))
            wdone(L0 + 7)
            op(dve, I("tensor_tensor", out=gpre[:, :, 0:4], in0=ba_sb[:, :, 4:8], in1=dtb4, op=ALU.add),
               R=(B["ba"],) + CC, W=(B["gpre"],))
            op(act, I("activation", out=gpre[:, :, 0:4], in_=gpre[:, :, 0:4], func=AF.Exp), R=(B["gpre"],), W=(B["gpre"],))
            op(act, I("activation", out=gpre[:, :, 0:4], in_=gpre[:, :, 0:4], func=AF.Ln, bias=onec, scale=1.0),
               R=(B["gpre"],) + CC, W=(B["gpre"],))
            op(dve, I("tensor_tensor", out=gpre[:, :, 0:4], in0=gpre[:, :, 0:4], in1=negA4, op=ALU.mult),
               R=(B["gpre"],) + CC, W=(B["gpre"],))
            op(act, I("activation", out=gpre[:, :, 4:8], in_=ba_sb[:, :, 0:4], func=AF.Sigmoid), R=(B["ba"],), W=(B["gpre"],))
            op(dve, I("tensor_scalar", out=gpre[:, :, 8:12], in0=gpre[:, :, 4:8], scalar1=-1.0, scalar2=None,
                      op0=ALU.mult), R=(B["gpre"],), W=(B["gpre"],))
            if ti == 0:
                dbg_dump("qT", qT[:, :, :], (B["qT"],))
                dbg_dump("kT", kT[:, :, :], (B["kT"],))
                dbg_dump("vtok", vtok[:, :, :], (B["vtok"],))

            if PH < 4:
                return
            attn_state = {}

            def attn_step(c, bp):
                attn_sc(c, bp)
                attn_pv(c, bp)

            def attn_sc(c, bp):
                kv = c // 2
                scs = [bank(), bank()]
                fns = []
                for bl in range(2):
                    blk = bp * 2 + bl
                    for kb in range(2):
                        k0 = (blk + kb) * 128
                        for hf in range(2):
                            fns.append(I("matmul",
                                         scs[hf][0][:, (bl * 2 + kb) * 128:(bl * 2 + kb + 1) * 128],
                                         lhsT=kT[hf * 64:(hf + 1) * 64, kv, k0:k0 + 128],
                                         rhs=qT[hf * 64:(hf + 1) * 64, c, blk * 128:(blk + 1) * 128],
                                         start=True, stop=True))
                pe_group(pe, fns, R=(B["kT"], B["qT"]), W=(scs[0][1], scs[1][1]))
                ami = 0 if (ti == 0 and bp == 0) else 1
                pis = []
                for hf in range(2):
                    op(act, I("activation", out=Eb[hf][:, :, :],
                              in_=scs[hf][0][:, :].rearrange("p (a q) -> p a q", a=4), func=AF.Exp),
                       R=(scs[hf][1],), W=(Eb_b[hf],))
                    pi = ((c * 2 + bp) % 2) * 2 + hf
                    pis.append(pi)
                    op(dve, I("tensor_tensor", out=PT[pi][:, :, :], in0=Eb[hf][:, :, :],
                              in1=am[:, ami, :, :], op=ALU.mult),
                       R=(Eb_b[hf],) + CC, W=(PT_b[pi],))
                attn_state[(c, bp)] = pis

            def attn_pv(c, bp):
                kv = c // 2
                pis = attn_state[(c, bp)]
                fns = []
                for bl in range(2):
                    blk = bp * 2 + bl
                    for which in range(2):
                        pso = ps[6] if which == 0 else ps[7]
                        for hf in range(2):
                            for kb in range(2):
                                if which == 0:
                                    lt = vtok[:, blk + kb, kv * 64:(kv + 1) * 64]
                                else:
                                    lt = onesb[:, 0:64]
                                fns.append(I("matmul",
                                             pso[hf * 64:(hf + 1) * 64, blk * 128:(blk + 1) * 128], lhsT=lt,
                                             rhs=PT[pis[hf]][:, bl * 2 + kb, :], start=(kb == 0), stop=(kb == 1)))
                pe_group(pe, fns, R=(PT_b[pis[0]], PT_b[pis[1]], B["vtok"]) + CC, W=(ps_b[6], ps_b[7]))

            def attn_fin(c):
                op(dve, I("tensor_scalar", out=rt1[:, :], in0=ps[7][:, :], scalar1=esink[:, c:c + 1],
                                                       scalar2=None, op0=ALU.add),
                   R=(ps_b[7],) + CC, W=(B["rt1"],))
                op(act, I("activation", out=rt1[:, :], in_=rt1[:, :], func=AF.Ln), R=(B["rt1"],), W=(B["rt1"],))
                op(act, I("activation", out=rt1[:, :], in_=rt1[:, :], func=AF.Exp, scale=-1.0), R=(B["rt1"],), W=(B["rt1"],))
                op(dve, I("tensor_tensor", out=aoT[:, c, :], in0=ps[6][:, :], in1=rt1[:, :], op=ALU.mult),
                   R=(ps_b[6], B["rt1"]), W=(B["aoT"],))


            if PH < 6:
                for c in range(4):
                    attn_step(c, 0)
                    attn_step(c, 1)
                    attn_fin(c)
            if PH < 5:
                return
            if PH < 6:
                return
            def dn_stream(blk, g):
                bs = slice(blk * 128, (blk + 1) * 128)
                H = (2 * g, 2 * g + 1)
                hs = slice(2 * g, 2 * g + 2)
                Bg = BS[g]

                def gc(base):
                    return gtok[:, base + 2 * g:base + 2 * g + 2]

                def v2(pb):
                    return pb[:, 0:256].rearrange("p (h t) -> p h t", h=2)

                for h in H:
                    op(dve, I("tensor_scalar", out=gb4[:, h, :], in0=onesf, scalar1=gpre[:, blk, h:h + 1],
                              scalar2=None, op0=ALU.mult), R=(B["gpre"],) + CC, W=(Bg["gb4"],))
                pgb, pgbb = bank(pin=True)
                pe_group(pe, [I("matmul", pgb[:, j * 128:(j + 1) * 128], lhsT=gb4[:, H[j], :], rhs=tri,
                                start=True, stop=True) for j in range(2)], R=(Bg["gb4"],) + CC, W=(pgbb,))
                g2v = v2(pgb)
                yield
                for j, h in enumerate(H):
                    op(dve, I("scalar_tensor_tensor", out=tmpM[:, h, :], in0=g2v[:, j, :], scalar=1.0, in1=ident,
                              op0=ALU.mult, op1=ALU.mult, accum_out=gtok[:, 12 + h:13 + h]),
                       R=(pgbb,) + CC, W=(Bg["tmpM"], Bg["gtok"]))
                    op(dve, I("scalar_tensor_tensor", out=tmpM[:, h, :], in0=g2v[:, j, :], scalar=1.0, in1=endsel,
                              op0=ALU.mult, op1=ALU.mult, accum_out=gtok[:, 44 + h:45 + h]),
                       R=(pgbb,) + CC, W=(Bg["tmpM"], Bg["gtok"]))
                op(act, I("activation", out=egbc[:, hs, :], in_=g2v, func=AF.Exp), R=(pgbb,), W=(Bg["egbc"],))
                op(act, I("activation", out=dl[:, hs, :], in_=g2v[:, :, 63:128:64], func=AF.Exp), R=(pgbb,), W=(Bg["dl"],))
                op(dve, I("tensor_scalar", out=gc(16), in0=gc(12), scalar1=-1.0, scalar2=None, op0=ALU.mult),
                   R=(Bg["gtok"],), W=(Bg["gtok"],))
                op(act, I("activation", out=gc(20), in_=gc(12), func=AF.Exp), R=(Bg["gtok"],), W=(Bg["gtok"],))
                op(dve, I("tensor_tensor", out=gc(32), in0=gc(44), in1=gc(12), op=ALU.subtract),
                   R=(Bg["gtok"],), W=(Bg["gtok"],))
                op(act, I("activation", out=gc(24), in_=gc(32), func=AF.Exp), R=(Bg["gtok"],), W=(Bg["gtok"],))
                op(dve, I("tensor_tensor", out=gc(28), in0=gpre[:, blk, 4 + 2 * g:6 + 2 * g], in1=gc(20), op=ALU.mult),
                   R=(Bg["gtok"], B["gpre"]), W=(Bg["gtok"],))
                op(dve, I("tensor_scalar", out=gc(36), in0=gc(24), scalar1=cpk[:, C_M0:C_M0 + 1], scalar2=None,
                          op0=ALU.mult), R=(Bg["gtok"],) + CC, W=(Bg["gtok"],))
                op(dve, I("tensor_scalar", out=gc(40), in0=gc(24), scalar1=cpk[:, C_M1:C_M1 + 1], scalar2=None,
                          op0=ALU.mult), R=(Bg["gtok"],) + CC, W=(Bg["gtok"],))
                yield
                op(dve, I("scalar_tensor_tensor", out=tmpM[:, hs, :], in0=g2v, scalar=-1.0, in1=mL4[:, hs, :],
                          op0=ALU.mult, op1=ALU.add), R=(pgbb,) + CC, W=(Bg["tmpM"],))
                for h in H:
                    op(act, I("activation", out=Dst[:, h, :], in_=tmpM[:, h, :], func=AF.Exp,
                              bias=gtok[:, 12 + h:13 + h], scale=1.0), R=(Bg["tmpM"], Bg["gtok"]), W=(Bg["Dst"],))
                op(dve, I("tensor_tensor", out=tmpM[:, hs, :], in0=g2v, in1=mU4[:, hs, :], op=ALU.add),
                   R=(pgbb, Bg["Dst"]) + CC, W=(Bg["tmpM"],))
                unpin(pgbb)
                for h in H:
                    op(act, I("activation", out=DTi[:, h, :], in_=tmpM[:, h, :], func=AF.Exp,
                              bias=gtok[:, 16 + h:17 + h], scale=1.0), R=(Bg["tmpM"], Bg["gtok"]), W=(Bg["DTi"],))
                op(dve, I("tensor_tensor", out=qgT[:, hs, :], in0=qkn[:, hs, bs].bitcast(F32), in1=egbc[:, hs, :],
                          op=ALU.mult), R=(qkn_b[H[0]], qkn_b[H[1]], Bg["egbc"]), W=(Bg["qgT"],))
                yield
                pk, pkb = bank(pin=True)
                pe_group(pe, [I("matmul", pk[:, j * 128:(j + 1) * 128], lhsT=qkn[:, 4 + H[j], bs],
                                rhs=qkn[:, 4 + H[j], bs], start=True, stop=True) for j in range(2)] +
                         [I("matmul", pk[:, 256 + j * 128:256 + (j + 1) * 128], lhsT=qkn[:, 4 + H[j], bs],
                            rhs=qkn[:, H[j], bs], start=True, stop=True) for j in range(2)],
                         R=tuple(qkn_b[h] for h in H) + tuple(qkn_b[4 + h] for h in H), W=(pkb,))
                pt, ptb = bank(pin=True)
                pe_group(pe, [I("transpose", pt[:, j * 128:(j + 1) * 128], qkn[:, 4 + H[j], bs].bitcast(F32), ident)
                              for j in range(2)] +
                         [I("transpose", pt[:, 256 + j * 128:256 + (j + 1) * 128], y[:, 8 + H[j], bs], ident)
                          for j in range(2)], R=tuple(qkn_b[4 + h] for h in H) + tuple(y_b[8 + h] for h in H) + CC, W=(ptb,))
                yield
                for j, h in enumerate(H):
                    op(dve, I("scalar_tensor_tensor", out=Bm[0][:, h, :], in0=pk[:, j * 128:(j + 1) * 128],
                              scalar=gpre[:, blk, 8 + h:9 + h], in1=Dst[:, h, :], op0=ALU.mult, op1=ALU.mult),
                       R=(pkb, B["gpre"], Bg["Dst"]), W=(Bg["Bm0"],))
                op(dve, I("tensor_tensor", out=aT[:, hs, :], in0=pk[:, 256:512].rearrange("p (h t) -> p h t", h=2),
                          in1=DTi[:, hs, :], op=ALU.mult), R=(pkb, Bg["DTi"]), W=(Bg["aT"],))
                unpin(pkb)
                pa, pab = bank(pin=True)
                pe_group(pe, [I("transpose", pa[:, j * 128:(j + 1) * 128], Bm[0][:, H[j], :].bitcast(F32), ident)
                              for j in range(2)], R=(Bg["Bm0"],) + CC, W=(pab,))
                for j, h in enumerate(H):
                    for cc2 in range(2):
                        op(dve, I("tensor_scalar", out=kd_t[cc2][:, h, :], in0=pt[:, j * 128:(j + 1) * 128],
                                  scalar1=gtok[:, 36 + 4 * cc2 + h:37 + 4 * cc2 + h], scalar2=None, op0=ALU.mult),
                           R=(ptb, Bg["gtok"]), W=(Bg["kd_t"],))
                    op(dve, I("tensor_scalar", out=kbg_t[:, h, :], in0=pt[:, j * 128:(j + 1) * 128],
                              scalar1=gtok[:, 28 + h:29 + h], scalar2=None, op0=ALU.mult),
                       R=(ptb, Bg["gtok"]), W=(Bg["kbg_t"],))
                    op(dve, I("tensor_scalar", out=vb_t[:, h, :], in0=pt[:, 256 + j * 128:256 + (j + 1) * 128],
                              scalar1=gpre[:, blk, 4 + h:5 + h], scalar2=None, op0=ALU.mult),
                       R=(ptb, B["gpre"]), W=(Bg["vb_t"],))
                unpin(ptb)
                yield
                op(act, I("activation", out=Am[0][:, hs, :], in_=v2(pa), func=AF.Copy), R=(pab,), W=(Bg["Am0"],))
                for j, h in enumerate(H):
                    op(dve, I("tensor_tensor", out=Pm[0][:, h, :], in0=pa[:, j * 128:(j + 1) * 128], in1=ident,
                              op=ALU.add), R=(pab,) + CC, W=(Bg["Pm0"],))
                unpin(pab)
                yield
                for k in range(5):
                    a, b_ = k % 2, (k + 1) % 2
                    AmA, BmA, PmA = Bg["Am%d" % a], Bg["Bm%d" % a], Bg["Pm%d" % a]
                    AmB, BmB, PmB = Bg["Am%d" % b_], Bg["Bm%d" % b_], Bg["Pm%d" % b_]
                    p12, p12b = bank(pin=True)
                    fns = [I("matmul", p12[:, 256 + j * 128:256 + (j + 1) * 128], lhsT=Am[a][:, H[j], :],
                             rhs=Bm[a][:, H[j], :], start=True, stop=True) for j in range(2)]
                    if k < 4:
                        fns = [I("matmul", p12[:, j * 128:(j + 1) * 128], lhsT=Bm[a][:, H[j], :],
                                 rhs=Am[a][:, H[j], :], start=True, stop=True) for j in range(2)] + fns
                    pe_group(pe, fns, R=(AmA, BmA), W=(p12b,))
                    yield
                    if k < 4:
                        op(act, I("activation", out=Am[b_][:, hs, :], in_=v2(p12), func=AF.Copy), R=(p12b,), W=(AmB,))
                    op(dve, I("tensor_copy", out=Bm[b_][:, hs, :],
                              in_=p12[:, 256:512].rearrange("p (h t) -> p h t", h=2)), R=(p12b,), W=(BmB,))
                    unpin(p12b)
                    p3, p3b = bank(pin=True)
                    pe_group(pe, [I("matmul", p3[:, j * 128:(j + 1) * 128], lhsT=Bm[b_][:, H[j], :],
                                    rhs=Pm[a][:, H[j], :], start=True, stop=True) for j in range(2)],
                             R=(BmB, PmA), W=(p3b,))
                    yield
                    op(dve, I("tensor_tensor", out=Pm[b_][:, hs, :], in0=v2(p3), in1=Pm[a][:, hs, :].bitcast(F32),
                              op=ALU.add), R=(p3b, PmA), W=(PmB,))
                    unpin(p3b)
                TTm, TTb = Pm[1], Bg["Pm1"]
                pu, pub = bank(pin=True)
                pe_group(pe, [I("matmul", pu[:, j * 128:(j + 1) * 128], lhsT=TTm[:, H[j], :], rhs=vb_t[:, H[j], :],
                                start=True, stop=True) for j in range(2)] +
                         [I("matmul", pu[:, 256 + j * 128:256 + (j + 1) * 128], lhsT=kbg_t[:, H[j], :],
                            rhs=TTm[:, H[j], :], start=True, stop=True) for j in range(2)],
                         R=(TTb, Bg["vb_t"], Bg["kbg_t"]), W=(pub,))
                yield
                op(act, I("activation", out=u_sb[:, hs, :], in_=v2(pu), func=AF.Copy), R=(pub,), W=(Bg["u_sb"],))
                op(dve, I("tensor_copy", out=wT[:, hs, :], in_=pu[:, 256:512].rearrange("p (h t) -> p h t", h=2)),
                   R=(pub,), W=(Bg["wT"],))
                unpin(pub)
                yield
                po, pob = bank(pin=True)
                for cch in range(2):
                    gch = (ti * 4 + blk) * 2 + cch
                    si, so = gch % 2, (gch + 1) % 2
                    rs = slice(cch * 64, (cch + 1) * 64)
                    pv, pvb = bank(pin=True)
                    pe_group(pe, [I("matmul", pv[:, j * 128:(j + 1) * 128], lhsT=wT[:, H[j], :],
                                    rhs=Sst[si][:, H[j], :], start=True, stop=True) for j in range(2)],
                             R=(Bg["wT"], Bg["S%d" % si]), W=(pvb,))
                    yield
                    op(dve, I("tensor_tensor", out=vnew[rs, hs, :], in0=u_sb[rs, hs, :],
                              in1=pv[rs, 0:256].rearrange("p (h t) -> p h t", h=2), op=ALU.subtract),
                       R=(pvb, Bg["u_sb"]), W=(Bg["vnew"],))
                    unpin(pvb)
                    fns = []
                    for j, h in enumerate(H):
                        fns.append(I("matmul", po[:, j * 128 + cch * 64:j * 128 + cch * 64 + 64],
                                     lhsT=Sst[si][:, h, :], rhs=qgT[:, h, cch * 64:(cch + 1) * 64],
                                     start=True, stop=False))
                        fns.append(I("matmul", po[:, j * 128 + cch * 64:j * 128 + cch * 64 + 64],
                                     lhsT=vnew[:, h, :], rhs=aT[:, h, cch * 64:(cch + 1) * 64],
                                     start=False, stop=True))
                    for j, h in enumerate(H):
                        fns.append(I("matmul", po[:, 256 + j * 128:256 + (j + 1) * 128], lhsT=kd_t[cch][:, h, :],
                                     rhs=vnew[:, h, :], start=True, stop=True))
                    pe_group(pe, fns, R=(Bg["S%d" % si], Bg["qgT"], Bg["vnew"], Bg["aT"], Bg["kd_t"]), W=(pob,))
                    yield
                    for j, h in enumerate(H):
                        op(dve, I("scalar_tensor_tensor", out=Sst[so][:, h, :], in0=Sst[si][:, h, :].bitcast(F32),
                                  scalar=dl[:, h, cch:cch + 1], in1=po[:, 256 + j * 128:256 + (j + 1) * 128],
                                  op0=ALU.mult, op1=ALU.add),
                           R=(pob, Bg["S%d" % si], Bg["dl"]), W=(Bg["S%d" % so],))
                    yield
                op(act, I("activation", out=oTs[:, hs, bs], in_=v2(po), func=AF.Copy), R=(pob,), W=(Bg["oTs"],))
                unpin(pob)

            def attn_stream(blk):
                attn_sc(blk, 0)
                yield
                yield
                attn_pv(blk, 0)
                yield
                attn_sc(blk, 1)
                yield
                yield
                attn_pv(blk, 1)
                yield
                attn_fin(blk)

            def gated(g):
                tmp, tmpb = (rt2, B["rt2"]) if g == 0 else (rstd, B["rstd"])
                pbs = []
                for h in (2 * g, 2 * g + 1):
                    op(act, I("activation", out=uT[:, h, :], in_=oTs[:, h, :], func=AF.Square),
                       R=(BS[g]["oTs"],), W=(uT_b[h],))
                    pb, pbb = bank(pin=True)
                    pbs.append((pb, pbb))
                    pe_group(pe, [I("matmul", pb[:, :], lhsT=onesb[:, :], rhs=uT[:, h, :], start=True, stop=True)],
                             R=(uT_b[h],) + CC, W=(pbb,))
                yield
                for pb, pbb in pbs:
                    op(act, I("activation", out=pb[:, :], in_=pb[:, :], func=AF.Ln, bias=epsc, scale=1.0 / 128.0),
                       R=(pbb,) + CC, W=(pbb,))
                    op(act, I("activation", out=pb[:, :], in_=pb[:, :], func=AF.Exp, scale=-0.5), R=(pbb,), W=(pbb,))
                yield
                for j, h in enumerate((2 * g, 2 * g + 1)):
                    pb, pbb = pbs[j]
                    op(dve, I("scalar_tensor_tensor", out=tmp[:, :], in0=oTs[:, h, :], scalar=cpk[:, C_DNN:C_DNN + 1],
                              in1=pb[:, :], op0=ALU.mult, op1=ALU.mult), R=(BS[g]["oTs"], pbb) + CC, W=(tmpb,))
                    unpin(pbb)
                    op(dve, I("tensor_tensor", out=dnT[:, h, :], in0=tmp[:, :], in1=zs[:, h, :], op=ALU.mult),
                       R=(tmpb, B["zs"]), W=(B["dnT"],))

            def chain(mk, tail=None):
                for blk in range(4):
                    yield from mk(blk)
                if tail is not None:
                    yield from tail

            class St:
                def __init__(self, gen):
                    self.gen = gen
                    self.clock = 0.0

            sts = [St(chain(lambda blk: dn_stream(blk, 0), gated(0))), St(chain(lambda blk: dn_stream(blk, 1), gated(1))),
                   St(chain(attn_stream))]
            base = max(pe.t if hasattr(pe, "t") else 0.0, 0.0)
            for st in sts:
                st.clock = 0.0
            sts[1].clock = float(SKEW) * 0.0
            alive = list(sts)
            while alive:
                st = min(alive, key=lambda x: x.clock)
                SchedState.cur = st
                try:
                    next(st.gen)
                except StopIteration:
                    alive.remove(st)
                SchedState.cur = None

            if PH < 8:
                return
            for s in range(2):
                wt, wtb = wslot(L0 + 8 + s)
                wv = wt[:, :].rearrange("p (k n) -> p k n", k=8)
                for j in range(4):
                    oc = s * 4 + j
                    pb, pbb = bank()
                    fns = []
                    for kc in range(8):
                        src = aoT[:, kc, :] if kc < 4 else dnT[:, kc - 4, :]
                        fns.append(I("matmul",
                            pb[:, :], lhsT=wv[:, kc, j * 128:(j + 1) * 128], rhs=src, start=(kc == 0), stop=(kc == 7)))
                    pe_group(pe, fns, R=(B["aoT"], B["dnT"], wtb), W=(pbb,))
                    op(dve, I("tensor_tensor", out=hT[:, oc, :], in0=pb[:, :], in1=hT[:, oc, :],
                                                                    op=ALU.add), R=(pbb, hT_b[oc]), W=(hT_b[oc],))
                wdone(L0 + 8 + s)

            if PH < 9:
                return
            if ti + 1 < NT:
                for c in range(6):
                    spq.dma(I("dma_start", out=XP[c], in_=xT_d[:, c, t0 + TT:t0 + 2 * TT]), W=XP_b[c])
            rmsnorm(hTc, hT_b, C_G + 8, uT, uT_b)
            for s in range(8):
                wt, wtb = wslot(L0 + 10 + s)
                wv = wt[:, :].rearrange("p (k n) -> p k n", k=8)
                for j in range(4):
                    hc = s * 4 + j
                    pb, pbb = bank()
                    if hc == 0:
                        mm8_split(pb, pbb, wv, j, wtb)
                    else:
                        pe_group(pe, mm8(pb, wv, j), R=tuple(uT_b) + (wtb,), W=(pbb,))
                    op(act, I("activation", out=rt2[:, :], in_=pb[:, :], func=AF.Square),
                       R=(pbb,), W=(B["rt2"],))
                    op(dve, I("scalar_tensor_tensor",
                        out=hid[:, hc, :], in0=pb[:, :], scalar=0.0, in1=rt2[:, :], op0=ALU.is_gt, op1=ALU.mult),
                       R=(pbb, B["rt2"]), W=(B["scrB"],) + tuple(y_b))
                wdone(L0 + 10 + s)
            for oc in range(8):
                wt, wtb = wslot(L0 + 18 + oc)
                wv = wt[:, :].rearrange("p (k n) -> p k n", k=32)
                pb, pbb = bank()
                pe_group(pe, [I("matmul", pb[:, :], lhsT=wv[:, kc, :], rhs=hid[:, kc, :],
                                                                      start=(kc == 0), stop=(kc == 31))
                              for kc in range(32)], R=(B["scrB"], wtb), W=(pbb,))
                op(dve, I("tensor_tensor", out=hT[:, oc, :], in0=pb[:, :], in1=hT[:, oc, :],
                                                                op=ALU.add), R=(pbb, hT_b[oc]), W=(hT_b[oc],))
                wdone(L0 + 18 + oc)

            if PH < 10:
                return
            rmsnorm(hTc, hT_b, C_G + 16, uT, uT_b)
            op(act, I("activation", out=pTb[:, :, :], in_=pTs[:, :, :], func=AF.Copy), R=(B["pTs"],), W=(B["pTb"],))
            if ti + 1 < NT:
                for c in (6, 7):
                    spq.dma(I("dma_start", out=XP[c], in_=xT_d[:, c, t0 + TT:t0 + 2 * TT]), W=XP_b[c])
            wp_t, wp_b = None, None
            for s in range(2):
                wt, wtb = wslot(L0 + 26 + s)
                wv = wt[:, :].rearrange("p (k n) -> p k n", k=8)
                if s == 0:
                    pass
                for j in range(4):
                    oc = s * 4 + j
                    pb, pbb = bank()
                    if oc == 0:
                        mm8_split(pb, pbb, wv, j, wtb)
                    else:
                        pe_group(pe, mm8(pb, wv, j), R=tuple(uT_b) + (wtb,), W=(pbb,))
                    sgt, sgb = (rt1, B["rt1"]) if oc % 2 == 0 else (rt2, B["rt2"])
                    op(act, I("activation", out=sgt[:, :], in_=pb[:, :], func=AF.Sigmoid),
                       R=(pbb,), W=(sgb,))
                    wt2, wt2b = wslot(L0 + 28)
                    wv2 = wt2[:, 0:2048].rearrange("p (k n) -> p k n", k=2)
                    pb2, pb2b = bank()
                    pe_group(pe, [I("matmul",
                        pb2[:, :], lhsT=wv2[:, kc, oc * 128:(oc + 1) * 128], rhs=pTb[:, kc, :],
                        start=(kc == 0), stop=(kc == 1)) for kc in range(2)], R=(B["pTb"], wt2b), W=(pb2b,))
                    op(dve, I("tensor_tensor", out=sgt[:, :], in0=pb2[:, :], in1=sgt[:, :], op=ALU.mult),
                       R=(pb2b, sgb), W=(sgb,))
                    op(dve, I("tensor_tensor", out=hT[:, oc, :], in0=sgt[:, :], in1=hT[:, oc, :],
                                                             op=ALU.add), R=(sgb, hT_b[oc]), W=(hT_b[oc],))
                wdone(L0 + 26 + s) if s == 0 else None
            wdone(L0 + 28)

        def tile_final(ti):
            t0 = ti * TT
            rmsnorm(hTc, hT_b, C_G + 24, outt, None, dst_all_b=B["scrB"])
            out_toks.append(spq.dma(I("dma_start", out=out_d[:, :, t0:t0 + TT], in_=outt),
                                    R=(B["scrB"],), W=()))

        for ti in range(NT):
            tile_body(ti)
            tile_final(ti)
        for tok in out_toks:
            sp.wait(tok)
        with nc.Block() as block:
            @block.sync
            def _(e):
                for f in sp.ops:
                    f(e)

            @block.tensor
            def _(e):
                for f in pe.ops:
                    f(e)

            @block.scalar
            def _(e):
                for f in act.ops:
                    f(e)

            @block.vector
            def _(e):
                for f in dve.ops:
                    f(e)

            @block.gpsimd
            def _(e):
                for f in pool.ops:
                    f(e)
    return nc


def _fm(a, nchunk):
    T = a.shape[0]
    return np.ascontiguousarray(a.reshape(T, nchunk, 128).transpose(2, 1, 0))


def _slot8(G):
    return np.ascontiguousarray(G.reshape(8, 128, 512).transpose(1, 0, 2).reshape(128, 4096))


def host_prepare(inputs, S):
    w_in = np.asarray(inputs["w_in"][0], dtype=np.float32)
    w_o = np.asarray(inputs["w_o"][0], dtype=np.float32)
    w_up = np.asarray(inputs["w_up"][0], dtype=np.float32)
    w_down = np.asarray(inputs["w_down"][0], dtype=np.float32)
    w_g = np.asarray(inputs["w_ple_gate"][0], dtype=np.float32)
    w_p = np.asarray(inputs["w_ple_proj"][0], dtype=np.float32)
    wpack = np.zeros((NSLOT, 128, 4096), np.float32)
    perm = np.arange(64)
    perm = (perm + 32) % 64
    aq = w_in[:, 0:512]
    ak = w_in[:, 512:640]

    def rotcols(G):
        nh = G.shape[1] // 64
        return np.concatenate([G[:, h * 64:(h + 1) * 64][:, perm] for h in range(nh)], axis=1)

    for s in range(2):
        cols = []
        for pr in range(2):
            c = 2 * s + pr
            g = aq[:, c * 128:(c + 1) * 128]
            cols += [g, rotcols(g)]
        wpack[s] = _slot8(np.concatenate(cols, axis=1))
    kA = np.concatenate([ak[:, 0:64], ak[:, 0:64]], axis=1)
    kB = np.concatenate([ak[:, 64:128], ak[:, 64:128]], axis=1)
    wpack[2] = _slot8(np.concatenate([kA, rotcols(kA), kB, rotcols(kB)], axis=1))
    wpack[3] = _slot8(w_in[:, 768:1280])
    wpack[4] = _slot8(w_in[:, 1280:1792])
    wpack[5] = _slot8(w_in[:, 1792:2304])
    wpack[6] = _slot8(w_in[:, 2304:2816])
    g7 = np.zeros((1024, 512), np.float32)
    g7[:, 0:128] = w_in[:, 640:768]
    g7[:, 128:136] = w_in[:, 2816:2824]
    wpack[7] = _slot8(g7)
    for s in range(2):
        wpack[8 + s] = _slot8(w_o[:, s * 512:(s + 1) * 512])
        wpack[26 + s] = _slot8(w_g[:, s * 512:(s + 1) * 512])
    for s in range(8):
        wpack[10 + s] = _slot8(w_up[:, s * 512:(s + 1) * 512])
        wpack[18 + s] = w_down[:, s * 128:(s + 1) * 128].reshape(32, 128, 128).transpose(1, 0, 2).reshape(128, 4096)
    wpack[28, :, 0:2048] = w_p.reshape(2, 128, 1024).transpose(1, 0, 2).reshape(128, 2048)

    cp = np.zeros((128, NCP), np.float32)
    i = np.arange(128)
    ii, jj = i[:, None], i[None, :]
    same = (ii // 64) == (jj // 64)
    cp[:, C_ID:C_ID + 128] = np.eye(128, dtype=np.float32)
    cp[:, C_ONE:C_ONE + 128] = 1.0
    cp[:, C_TRI:C_TRI + 128] = (same & (ii <= jj)).astype(np.float32)
    cp[:, C_BON:C_BON + 128] = same.astype(np.float32)
    cp[:, C_ML:C_ML + 128] = np.where(same & (ii > jj), 0.0, NEG)
    cp[:, C_MU:C_MU + 128] = np.where(same & (jj >= ii), 0.0, NEG)
    cp[:, C_M0] = (i < 64)
    cp[:, C_M1] = (i >= 64)
    cp[:, C_END:C_END + 128] = (jj == (ii // 64) * 64 + 63).astype(np.float32)
    cp[:, C_AP:C_AP + 128] = (jj < ii).astype(np.float32)
    cp[:, C_AC:C_AC + 128] = (jj >= ii).astype(np.float32)
    for n, nm in enumerate(["norm_mix", "norm_mlp", "norm_ple"]):
        cp[:, C_G + 8 * n:C_G + 8 * n + 8] = np.asarray(inputs[nm][0], np.float32).reshape(8, 128).T
    cp[:, C_G + 24:C_G + 32] = np.asarray(inputs["norm_final"], np.float32).reshape(8, 128).T
    cw = np.asarray(inputs["conv_w"][0], np.float32)
    cp[:, C_CW:C_CW + 48] = cw.reshape(4, 12, 128).transpose(2, 1, 0).reshape(128, 48)
    cp[:, C_DNN] = np.asarray(inputs["dn_norm"][0], np.float32)
    cp[:, C_ALOG:C_ALOG + 4] = np.asarray(inputs["a_log"][0], np.float32)[None, :]
    cp[:, C_DTB:C_DTB + 4] = np.asarray(inputs["dt_bias"][0], np.float32)[None, :]
    sk = np.asarray(inputs["sinks"][0], np.float32)
    cp[:, C_SNK:C_SNK + 4] = sk.reshape(4, 2)[:, (np.arange(128) // 64)].T

    half = 32
    inv = 1.0 / (10000.0 ** (np.arange(half, dtype=np.float32) * (2.0 / 64)))
    pos = np.arange(S, dtype=np.float32)
    ang = (pos[None, :] * inv[:, None]).astype(np.float32)
    cos = np.cos(ang).astype(np.float32)
    sin = np.sin(ang).astype(np.float32)
    pidx = np.arange(128) % 64
    fidx = pidx % 32
    sign = np.where(pidx < 32, -1.0, 1.0).astype(np.float32)
    rope = np.zeros((128, 4, S), np.float32)
    rope[:, 0, :] = cos[fidx] * 0.125
    rope[:, 1, :] = sin[fidx] * sign[:, None] * 0.125
    rope[:, 2, :] = cos[fidx]
    rope[:, 3, :] = sin[fidx] * sign[:, None]
    return wpack, cp, rope


_CACHE = {}


def run(inputs, S, ncores, dbg=None):
    x = np.asarray(inputs["x"], np.float32)
    p = np.asarray(inputs["p"], np.float32)[0]
    wpack, cp, rope = host_prepare(inputs, S)
    key = (S, tuple(sorted(dbg.items())) if dbg else None)
    nc = build(S, dbg)
    in_maps = []
    for b in range(ncores):
        in_maps.append({"xT": _fm(x[b, :S], 8), "pT": _fm(p[b, :S], 2), "wpack": wpack, "cpack": cp, "rope": rope})
    res = run_bass_kernel_spmd(nc, in_maps, core_ids=list(range(ncores)))
    outs = []
    for b in range(ncores):
        o = res.results[b]["outT"]
        outs.append(o.transpose(2, 1, 0).reshape(S, D))
    return np.stack(outs, axis=0), res


def kernel(**inputs):
    out, _ = run(inputs, 4096, 8)
    return out.astype(np.float32)
```
